# Optimizing a Trainium2 kernel written in Bass

```python
import math
import jax
import jax.numpy as jnp
from jax import lax
import numpy as np

D_MODEL = 1024
BATCH = 16
SEQ = 256
DEPTH = 4
DEC_BATCH = 4
DEC_SEQ = 2048
PAST_LEN = 256

GRID_W = 64
N_MIXERS = 2
N_SSD_LAYERS = (DEPTH + 1) // 2
N_ATT_LAYERS = DEPTH // 2
EPS = 1e-6

SSD_INNER = 2 * D_MODEL
SSD_HEAD_DIM = 64
SSD_HEADS = SSD_INNER // SSD_HEAD_DIM
SSD_GROUPS = 4
SSD_STATE = 128
SSD_CONV = 3
SSD_CHUNK = 128
SSD_BC = SSD_GROUPS * SSD_STATE
SSD_CONV_CH = SSD_INNER + 2 * SSD_BC
SSD_IN_DIM = SSD_INNER + SSD_CONV_CH + 2 * SSD_HEADS

DA_HEAD_DIM = 64
DA_HEADS = D_MODEL // (2 * DA_HEAD_DIM)
DA_SCALE = DA_HEAD_DIM ** -0.5
Q_BLOCK = 128
ROPE_BASE = 10000.0
ROPE_AXIS_DIM = DA_HEAD_DIM // 2

FFN_HIDDEN = 2816
FFN_CONV = 3

kernel_name = 'hybrid_ssd_diffattn_dit_step'

F32 = jnp.float32


def _rms(x):
    xf = x.astype(F32)
    return xf * lax.rsqrt(jnp.mean(xf * xf, axis=-1, keepdims=True) + EPS)


def rmsnorm(x, g):
    return (_rms(x) * g.astype(F32)).astype(x.dtype)


def modulation(cvec, w, b):
    m = jax.nn.silu(cvec) @ w + b
    return jnp.split(m[:, None, :], 6, axis=-1)


def modulate(x, g, shift, scale):
    return rmsnorm(x, g) * (1 + scale) + shift


def dwconv(x, w, b):
    k, ch = w.shape
    pad = k // 2
    y = lax.conv_general_dilated(x, w[:, None, :], window_strides=(1,), padding=[(pad, pad)],
                                 dimension_numbers=('NWC', 'WIO', 'NWC'), feature_group_count=ch)
    return y + b


def ssd_scan(x, dt, a, bm, cm, h0):
    b, l, nh, p = x.shape
    g, n = bm.shape[2], bm.shape[3]
    e = nh // g
    nc = l // SSD_CHUNK
    xd = (x.astype(F32) * dt[..., None]).reshape(b, nc, SSD_CHUNK, g, e, p)
    a_cs = jnp.cumsum((dt * a).reshape(b, nc, SSD_CHUNK, g, e), axis=2)
    bm = bm.astype(F32).reshape(b, nc, SSD_CHUNK, g, n)
    cm = cm.astype(F32).reshape(b, nc, SSD_CHUNK, g, n)
    tri = jnp.tril(jnp.ones((SSD_CHUNK, SSD_CHUNK), bool))[:, :, None, None]
    seg = a_cs[:, :, :, None] - a_cs[:, :, None, :]
    decay = jnp.exp(jnp.where(tri, seg, -jnp.inf))
    cb = jnp.einsum('bcign,bcjgn->bcijg', cm, bm)
    y_diag = jnp.einsum('bcijge,bcjgep->bcigep', cb[..., None] * decay, xd)
    xw = xd * jnp.exp(a_cs[:, :, -1:] - a_cs)[..., None]
    states = jnp.einsum('bcqgn,bcqgep->bcgepn', bm, xw)
    chunk_decay = jnp.exp(a_cs[:, :, -1])

    def step(hc, inp):
        dec, st = inp
        return dec[..., None, None] * hc + st, hc

    h_last, h_in = lax.scan(step, h0.astype(F32).reshape(b, g, e, p, n),
                            (jnp.moveaxis(chunk_decay, 1, 0), jnp.moveaxis(states, 1, 0)))
    h_in = jnp.moveaxis(h_in, 0, 1)
    y_off = jnp.einsum('bcqgn,bcgepn->bcqgep', cm, h_in) * jnp.exp(a_cs)[..., None]
    y = (y_diag + y_off).reshape(b, l, nh, p)
    return y.astype(x.dtype), h_last.reshape(b, nh, p, n).astype(x.dtype)


def ssd_mixer(h, h0, in_w, conv_w, conv_b, dt_bias, a_log, d_skip, norm_g, out_w):
    b, l, _ = h.shape
    proj = h @ in_w
    z = proj[..., :SSD_INNER]
    xbc = jax.nn.silu(dwconv(proj[..., SSD_INNER:SSD_INNER + SSD_CONV_CH], conv_w, conv_b))
    dt_raw = proj[..., SSD_INNER + SSD_CONV_CH:].reshape(b, l, 2, SSD_HEADS)
    xs = xbc[..., :SSD_INNER].reshape(b, l, SSD_HEADS, SSD_HEAD_DIM)
    bm = xbc[..., SSD_INNER:SSD_INNER + SSD_BC].reshape(b, l, SSD_GROUPS, SSD_STATE)
    cm = xbc[..., SSD_INNER + SSD_BC:].reshape(b, l, SSD_GROUPS, SSD_STATE)
    dt = jax.nn.softplus(dt_raw.astype(F32) + dt_bias.astype(F32))
    a = -jnp.exp(a_log.astype(F32))
    y_f, h_f = ssd_scan(xs, dt[:, :, 0], a[0], bm, cm, h0[:, 0])
    rev = lambda t: t[:, ::-1]
    y_b, h_b = ssd_scan(rev(xs), rev(dt[:, :, 1]), a[1], rev(bm), rev(cm), h0[:, 1])
    y = y_f + rev(y_b) + d_skip[:, None] * xs
    y = y.reshape(b, l, SSD_INNER) * jax.nn.silu(z)
    y = _rms(y.reshape(b, l, SSD_GROUPS, SSD_INNER // SSD_GROUPS)).reshape(b, l, SSD_INNER)
    y = (y * norm_g.astype(F32)).astype(h.dtype)
    return y @ out_w, jnp.stack([h_f, h_b], axis=1)


def diff_lambda(lam_vecs, lam_init):
    lv = lam_vecs.astype(F32)
    return jnp.exp(jnp.sum(lv[0] * lv[1])) - jnp.exp(jnp.sum(lv[2] * lv[3])) + lam_init


def diff_qkv(h, w):
    b, l, _ = h.shape
    q, k, v = jnp.split(h @ w, 3, axis=-1)
    return (q.reshape(b, l, DA_HEADS, 2, DA_HEAD_DIM),
            k.reshape(b, l, DA_HEADS, 2, DA_HEAD_DIM),
            v.reshape(b, l, DA_HEADS, 2 * DA_HEAD_DIM))


def diff_attend(q, k, v, lam):
    b, lq, nh, _, dh = q.shape
    nb = lq // Q_BLOCK
    qb = jnp.moveaxis(q.reshape(b, nb, Q_BLOCK, nh, 2, dh), 1, 0)

    def one_block(qi):
        s = jnp.einsum('bqhcd,bkhcd->bhcqk', qi, k, preferred_element_type=F32) * DA_SCALE
        p = jax.nn.softmax(s, axis=-1)
        w = p[:, :, 0] - lam * p[:, :, 1]
        return jnp.einsum('bhqk,bkhd->bqhd', w.astype(v.dtype), v)

    o = lax.map(one_block, qb)
    return jnp.moveaxis(o, 0, 1).reshape(b, lq, nh, 2 * dh)


def diff_out(o, subln_g, out_w, lam_init):
    b, l = o.shape[:2]
    o = (_rms(o) * subln_g.astype(F32) * (1.0 - lam_init)).astype(o.dtype)
    return o.reshape(b, l, D_MODEL) @ out_w


def axial_rope_tables(rows):
    row = jnp.repeat(jnp.arange(rows, dtype=F32), GRID_W)
    col = jnp.tile(jnp.arange(GRID_W, dtype=F32), rows)
    inv = 1.0 / (ROPE_BASE ** (jnp.arange(0, ROPE_AXIS_DIM, 2, dtype=F32) / ROPE_AXIS_DIM))
    ang_r = row[:, None] * inv
    ang_c = col[:, None] * inv
    shp = lambda t: t[None, :, None, None, :]
    return (shp(jnp.cos(ang_r)), shp(jnp.sin(ang_r)), shp(jnp.cos(ang_c)), shp(jnp.sin(ang_c)))


def _rotate(x, cos, sin):
    x1, x2 = jnp.split(x, 2, axis=-1)
    return jnp.concatenate([x1 * cos - x2 * sin, x1 * sin + x2 * cos], axis=-1)


def rope_2d(x, tabs):
    cr, sr, cc, sc = (t.astype(x.dtype) for t in tabs)
    xr, xc = jnp.split(x, 2, axis=-1)
    return jnp.concatenate([_rotate(xr, cr, sr), _rotate(xc, cc, sc)], axis=-1)


def conv_ffn(h, up_w, conv_w, conv_b, down_w):
    u = dwconv(h @ up_w, conv_w, conv_b)
    a, v = jnp.split(u, 2, axis=-1)
    return (jax.nn.silu(a) * v) @ down_w


def lambda_init_fn(layer):
    return 0.8 - 0.6 * math.exp(-0.3 * layer)


def setup_inputs(seed: int = 0) -> dict:
    key = jax.random.key(seed)
    ks = iter(jax.random.split(key, 64))
    nrm = lambda shape, scale: jax.random.normal(next(ks), shape, F32) * scale
    gain = lambda shape: 1.0 + 0.02 * jax.random.normal(next(ks), shape, F32)
    u_dt = jax.random.uniform(next(ks), (N_SSD_LAYERS, 2, SSD_HEADS), F32)
    dt0 = jnp.exp(u_dt * (math.log(0.1) - math.log(0.001)) + math.log(0.001))
    dt_bias = dt0 + jnp.log(-jnp.expm1(-dt0))
    a_log = jnp.log(jax.random.uniform(next(ks), (N_SSD_LAYERS, 2, SSD_HEADS), F32, 1.0, 16.0))
    return {
        'x_prompt': nrm((BATCH, SEQ, D_MODEL), 1.0),
        'x_sample': nrm((DEC_BATCH, DEC_SEQ, D_MODEL), 1.0),
        'state_ssd': nrm((DEC_BATCH, N_SSD_LAYERS, 2, SSD_HEADS, SSD_HEAD_DIM, SSD_STATE), 1.0),
        'cache_k': nrm((DEC_BATCH, N_ATT_LAYERS, PAST_LEN, DA_HEADS, 2, DA_HEAD_DIM), 1.0),
        'cache_v': nrm((DEC_BATCH, N_ATT_LAYERS, PAST_LEN, DA_HEADS, 2 * DA_HEAD_DIM), 1.0),
        'c': nrm((DEC_BATCH, D_MODEL), 1.0),
        'c_ctx': nrm((D_MODEL,), 1.0),
        'mod_w': nrm((DEPTH, D_MODEL, 6 * D_MODEL), 0.5 * D_MODEL ** -0.5),
        'mod_b': nrm((DEPTH, 6 * D_MODEL), 0.01),
        'norm_mix_g': gain((DEPTH, D_MODEL)),
        'norm_ffn_g': gain((DEPTH, D_MODEL)),
        'ssd_in_w': nrm((N_SSD_LAYERS, D_MODEL, SSD_IN_DIM), D_MODEL ** -0.5),
        'ssd_conv_w': nrm((N_SSD_LAYERS, SSD_CONV, SSD_CONV_CH), SSD_CONV ** -0.5),
        'ssd_conv_b': nrm((N_SSD_LAYERS, SSD_CONV_CH), 0.01),
        'ssd_dt_bias': dt_bias,
        'ssd_a_log': a_log,
        'ssd_d': gain((N_SSD_LAYERS, SSD_HEADS)),
        'ssd_norm_g': gain((N_SSD_LAYERS, SSD_INNER)),
        'ssd_out_w': nrm((N_SSD_LAYERS, SSD_INNER, D_MODEL), SSD_INNER ** -0.5),
        'att_qkv_w': nrm((N_ATT_LAYERS, D_MODEL, 3 * D_MODEL), D_MODEL ** -0.5),
        'att_lambda': nrm((N_ATT_LAYERS, 4, DA_HEAD_DIM), 0.1),
        'att_subln_g': gain((N_ATT_LAYERS, 2 * DA_HEAD_DIM)),
        'att_out_w': nrm((N_ATT_LAYERS, D_MODEL, D_MODEL), D_MODEL ** -0.5),
        'ffn_up_w': nrm((DEPTH, D_MODEL, 2 * FFN_HIDDEN), D_MODEL ** -0.5),
        'ffn_conv_w': nrm((DEPTH, FFN_CONV, 2 * FFN_HIDDEN), FFN_CONV ** -0.5),
        'ffn_conv_b': nrm((DEPTH, 2 * FFN_HIDDEN), 0.01),
        'ffn_down_w': nrm((DEPTH, FFN_HIDDEN, D_MODEL), FFN_HIDDEN ** -0.5),
        'final_norm_g': gain((D_MODEL,)),
    }


def reference(x_prompt, x_sample, state_ssd, cache_k, cache_v, c, c_ctx,
              mod_w, mod_b, norm_mix_g, norm_ffn_g,
              ssd_in_w, ssd_conv_w, ssd_conv_b, ssd_dt_bias, ssd_a_log, ssd_d, ssd_norm_g, ssd_out_w,
              att_qkv_w, att_lambda, att_subln_g, att_out_w,
              ffn_up_w, ffn_conv_w, ffn_conv_b, ffn_down_w, final_norm_g):
    rows = x_sample.shape[1] // GRID_W
    tabs = axial_rope_tables(rows)
    xp, xs = x_prompt, x_sample
    new_ssd, new_k, new_v = [], [], []
    for i in range(DEPTH):
        slot = i // N_MIXERS
        p_sh1, p_sc1, p_g1, p_sh2, p_sc2, p_g2 = modulation(c_ctx[None], mod_w[i], mod_b[i])
        s_sh1, s_sc1, s_g1, s_sh2, s_sc2, s_g2 = modulation(c, mod_w[i], mod_b[i])
        hp = modulate(xp, norm_mix_g[i], p_sh1, p_sc1)
        hs = modulate(xs, norm_mix_g[i], s_sh1, s_sc1)
        if i % N_MIXERS == 0:
            ssd_p = (ssd_in_w[slot], ssd_conv_w[slot], ssd_conv_b[slot], ssd_dt_bias[slot],
                     ssd_a_log[slot], ssd_d[slot], ssd_norm_g[slot], ssd_out_w[slot])
            h_zero = jnp.zeros((xp.shape[0], 2, SSD_HEADS, SSD_HEAD_DIM, SSD_STATE), xp.dtype)
            op, st = ssd_mixer(hp, h_zero, *ssd_p)
            os_, _ = ssd_mixer(hs, state_ssd[:, slot], *ssd_p)
            new_ssd.append(st)
        else:
            lam_init = lambda_init_fn(i)
            lam = diff_lambda(att_lambda[slot], lam_init)
            qp, kp, vp = diff_qkv(hp, att_qkv_w[slot])
            op = diff_out(diff_attend(qp, kp, vp, lam), att_subln_g[slot], att_out_w[slot], lam_init)
            qs, ks_, vs = diff_qkv(hs, att_qkv_w[slot])
            qs = rope_2d(qs, tabs)
            ks_ = rope_2d(ks_, tabs)
            k_all = jnp.concatenate([cache_k[:, slot], ks_], axis=1)
            v_all = jnp.concatenate([cache_v[:, slot], vs], axis=1)
            os_ = diff_out(diff_attend(qs, k_all, v_all, lam), att_subln_g[slot], att_out_w[slot], lam_init)
            new_k.append(kp)
            new_v.append(vp)
        xp = xp + p_g1 * op
        xs = xs + s_g1 * os_
        ffn_p = (ffn_up_w[i], ffn_conv_w[i], ffn_conv_b[i], ffn_down_w[i])
        xp = xp + p_g2 * conv_ffn(modulate(xp, norm_ffn_g[i], p_sh2, p_sc2), *ffn_p)
        xs = xs + s_g2 * conv_ffn(modulate(xs, norm_ffn_g[i], s_sh2, s_sc2), *ffn_p)
    y_prompt = rmsnorm(xp, final_norm_g)
    y_sample = rmsnorm(xs, final_norm_g)
    new_state_ssd = jnp.stack(new_ssd, axis=1)
    new_cache_k = jnp.stack(new_k, axis=1)
    new_cache_v = jnp.stack(new_v, axis=1)
    return (y_prompt, y_sample, new_state_ssd, new_cache_k, new_cache_v)
```

```python
import math
import numpy as np
from contextlib import ExitStack
import concourse.bass as bass
import concourse.mybir as mybir
from concourse.bass_utils import run_bass_kernel_spmd

F32 = mybir.dt.float32
AF = mybir.ActivationFunctionType
ALU = mybir.AluOpType
AX = mybir.AxisListType

DEPTH_RUN = 4
DO_SAMPLE = True
DO_PROMPT = True
NCORES = 8
EPS = 1e-6
D = 1024
LS = 2048
LH = 1024
PAIRS = [[0, 1], [2, 3], [4, 5], [6, 7]]
LP = 256
FH = 2816
NPAGES = 93
STQ = "act"


class Buf:
    __slots__ = ("name", "lw", "rd")

    def __init__(self, name=""):
        self.name = name
        self.lw = None
        self.rd = {}


class Prog:
    ENG = ["pe", "act", "dve", "pool", "sp"]
    NDMA = 16
    DMA_POOLS = {"sp": (0, 16), "act": (0, 16), "pool": (0, 16)}
    SHARED_POOL = True
    SAME_ENGINE_SYNC = True

    def __init__(self, nc):
        self.nc = nc
        self.streams = {e: [] for e in self.ENG}
        self.cnt = {e: 0 for e in self.ENG}
        self.seen = {e: {} for e in self.ENG}
        self.ndma = {q: 0 for q in self.DMA_POOLS}
        self.ncc = 0

    def op(self, eng, fn, reads=(), writes=(), dma=False, cc=False):
        deps = {}

        def add(ev):
            if ev is None:
                return
            k, v = ev
            if v > deps.get(k, 0):
                deps[k] = v

        for b in reads:
            add(b.lw)
        for b in writes:
            add(b.lw)
            for kv in b.rd.items():
                add(kv)
        if dma:
            base, npool = self.DMA_POOLS[eng]
            qk = "sp" if self.SHARED_POOL else eng
            i = self.ndma[qk]
            self.ndma[qk] += 1
            slot = base + i % npool
            val = 16 * (i // npool + 1)
            ev = (("dma", slot), val)
            if val > 16:
                add((("dma", slot), val - 16))
        elif cc:
            self.ncc += 1
            ev = ("cc", self.ncc)
        else:
            self.cnt[eng] += 1
            ev = (eng, self.cnt[eng])
        waits = []
        seen = self.seen[eng]
        for k, v in deps.items():
            if k == eng and (eng == "pe" or not self.SAME_ENGINE_SYNC):
                continue
            if seen.get(k, 0) >= v:
                continue
            seen[k] = v
            waits.append((k, v))
        self.streams[eng].append((fn, waits, ev))
        k, v = ev
        for b in reads:
            if b.rd.get(k, 0) < v:
                b.rd[k] = v
        for b in writes:
            b.lw = ev
            b.rd = {}
        return ev

    def emit(self, final_events=()):
        nc = self.nc
        with ExitStack() as es:
            sems = {}
            for e in self.ENG:
                sems[e] = es.enter_context(nc.semaphore("s_" + e))
            for i in range(self.NDMA):
                sems[("dma", i)] = es.enter_context(nc.semaphore("s_dma%d" % i))
            sems["cc"] = es.enter_context(nc.semaphore("s_cc"))
            block = es.enter_context(nc.Block())
            streams = self.streams

            def run(engname, eng):
                for fn, waits, ev in streams[engname]:
                    for k, v in waits:
                        eng.wait_ge(sems[k], v)
                    inst = fn(eng)
                    k, v = ev
                    inst.then_inc(sems[k], 16 if isinstance(k, tuple) else 1)
                if engname == "sp":
                    fe = {}
                    for k, v in final_events:
                        fe[k] = max(fe.get(k, 0), v)
                    for k, v in fe.items():
                        eng.wait_ge(sems[k], v)

            @block.tensor
            def _(eng):
                run("pe", eng)

            @block.scalar
            def _(eng):
                run("act", eng)

            @block.vector
            def _(eng):
                run("dve", eng)

            @block.gpsimd
            def _(eng):
                run("pool", eng)

            @block.sync
            def _(eng):
                run("sp", eng)


class Tile:
    def __init__(self, arena, off, n):
        self.arena = arena
        self.off = off
        self.n = n

    def ap(self, lo=0, hi=None):
        hi = self.n if hi is None else hi
        return self.arena.t[:, self.off + lo:self.off + hi]

    def v3(self, a, b, lo=0):
        return self.arena.t[:, self.off + lo:self.off + lo + a * b].rearrange("p (a b) -> p a b", a=a, b=b)

    def b(self, lo=0, hi=None):
        hi = self.n if hi is None else hi
        p0 = (self.off + lo) // 512
        p1 = (self.off + hi - 1) // 512
        return self.arena.pages[p0:p1 + 1]


class Arena:
    def __init__(self, nc, es, npages):
        self.t = es.enter_context(nc.sbuf_tensor("arena", [128, npages * 512], F32))
        self.pages = [Buf("pg%d" % i) for i in range(npages)]
        self.top = 0
        self.npages = npages

    def alloc(self, nelem):
        npg = (nelem + 511) // 512
        assert self.top + npg <= self.npages, ("arena overflow", self.top, npg, self.npages)
        t = Tile(self, self.top * 512, nelem)
        self.top += npg
        return t

    def mark(self):
        return self.top

    def release(self, m):
        self.top = m


class Ring:
    def __init__(self, arena, n, nelem):
        self.tiles = [arena.alloc(nelem) for _ in range(n)]
        self.i = 0

    def next(self):
        t = self.tiles[self.i % len(self.tiles)]
        self.i += 1
        return t


def halo_blocks(nseq, L, maxn=510):
    out = []
    if L > maxn:
        nb = -(-L // maxn)
        base = -(-L // nb)
        for s in range(nseq):
            t = 0
            while t < L:
                n = min(base, L - t)
                out.append((s * L + t, n, t > 0, t + n < L))
                t += n
    else:
        for s in range(nseq):
            out.append((s * L, L, False, False))
    return out


def build_program(depth_run, do_sample, do_prompt):
    nc = bass.Bass("TRN2", target_bir_lowering=False)
    es = ExitStack()

    def din(name, shape):
        return nc.dram_tensor(name, list(shape), F32, kind="ExternalInput").ap()

    def dout(name, shape):
        return nc.dram_tensor(name, list(shape), F32, kind="ExternalOutput").ap()

    def dscr(name, shape):
        return nc.dram_tensor(name, list(shape), F32, kind="Internal").ap()

    I = {}
    I["xs_in"] = din("xs_in", [LH, D])
    I["rmask"] = din("rmask", [128, 2])
    I["xp_in"] = din("xp_in", [2 * LP, D])
    I["state"] = din("state", [2, 2, 16, 128, 128])
    I["ck"] = din("ck", [2, 256, D])
    I["cv"] = din("cv", [2, 256, D])
    I["cmod"] = din("cmod", [128, 16])
    I["mod_w"] = din("mod_w", [4, D, 6 * D])
    I["mod_bT"] = din("mod_bT", [128, 4 * 48])
    I["nmg"] = din("nmg", [128, 32])
    I["nfg"] = din("nfg", [128, 32])
    I["fng"] = din("fng", [128, 8])
    I["ssd_in_w"] = din("ssd_in_w", [2, D, 5184])
    I["ssd_cw"] = din("ssd_cw", [128, 2 * 24 * 3])
    I["ssd_cb"] = din("ssd_cb", [128, 2 * 24])
    I["dtb"] = din("dtb", [64, 2])
    I["alog"] = din("alog", [64, 2])
    I["dsk"] = din("dsk", [128, 64])
    I["sng"] = din("sng", [128, 32])
    I["ssd_out_w"] = din("ssd_out_w", [2, 2048, D])
    I["att_qkv_w"] = din("att_qkv_w", [2, D, 3 * D])
    I["lamb"] = din("lamb", [128, 2 * 256])
    I["sub"] = din("sub", [128, 2])
    I["att_out_w"] = din("att_out_w", [2, D, D])
    I["ffn_up_w"] = din("ffn_up_w", [4, D, 2 * FH])
    I["fcw"] = din("fcw", [128, 4 * 44 * 3])
    I["fcb"] = din("fcb", [128, 4 * 44])
    I["ffn_down_w"] = din("ffn_down_w", [4, FH, D])
    I["cmat"] = din("cmat", [128, 7 * 128])
    I["ropeC"] = din("ropeC", [128, LH])
    I["ropeS"] = din("ropeS", [128, LH])
    O = {}
    O["y_s"] = dout("y_s", [LH, D])
    O["y_p"] = dout("y_p", [2 * LP, D])
    O["nstate"] = dout("nstate", [2, 2, 2, 16, 128, 128])
    O["nk"] = dout("nk", [2, 2, LP, D])
    O["nv"] = dout("nv", [2, 2, LP, D])
    S_proj = dscr("s_proj", [5248, LH])
    S_projG = dscr("s_projg", [2 * 5248, LH])
    S_y = dscr("s_y", [LS, 2048])
    S_q = dscr("s_q", [D, LH])
    S_kv = dscr("s_kv", [2 * D, LH])
    S_kvG = dscr("s_kvg", [4 * D, LH])
    S_o = dscr("s_o", [D, LH])
    S_hst = dscr("s_hst", [2048, 128])
    S_hstG = dscr("s_hstg", [4096, 128])
    H_loc = dscr("h_loc", [128, 16])
    H_all = dscr("h_all", [256, 16])
    dbufs = {}

    def db(*key):
        if key not in dbufs:
            dbufs[key] = Buf(str(key))
        return dbufs[key]

    with es:
        P = Prog(nc)
        A = Arena(nc, es, NPAGES)
        psum = [es.enter_context(nc.psum_tensor("ps%d" % i, [128, 512], F32)) for i in range(8)]
        pb = [Buf("psum%d" % i) for i in range(8)]
        out_events = []

        def mm(out, lhsT, rhs, start, stop, rd, wr):
            P.op("pe", lambda e: e.matmul(out, lhsT=lhsT, rhs=rhs, start=start, stop=stop), reads=rd, writes=wr)

        def act(out, in_, func, rd, wr, bias=None, scale=None, eng="act"):
            kw = {}
            if bias is not None:
                kw["bias"] = bias
            if scale is not None:
                kw["scale"] = scale
            P.op("act", lambda e: e.activation(out=out, in_=in_, func=func, **kw), reads=rd, writes=wr)

        def tt(out, in0, in1, op, rd, wr, eng="dve"):
            P.op(eng, lambda e: e.tensor_tensor(out=out, in0=in0, in1=in1, op=op), reads=rd, writes=wr)

        def ts(out, in0, s1, s2, op0, op1, rd, wr, eng="dve"):
            if op1 is None:
                P.op(eng, lambda e: e.tensor_scalar(out=out, in0=in0, scalar1=s1, scalar2=None, op0=op0), reads=rd, writes=wr)
            else:
                P.op(eng, lambda e: e.tensor_scalar(out=out, in0=in0, scalar1=s1, scalar2=s2, op0=op0, op1=op1), reads=rd, writes=wr)

        def stt(out, in0, scalar, in1, op0, op1, rd, wr):
            P.op("dve", lambda e: e.scalar_tensor_tensor(out=out, in0=in0, scalar=scalar, in1=in1, op0=op0, op1=op1), reads=rd, writes=wr)

        def cp(out, in_, rd, wr, eng="dve"):
            P.op(eng, lambda e: e.tensor_copy(out=out, in_=in_), reads=rd, writes=wr)

        def mset(ap, val, wr, eng="pool"):
            P.op(eng, lambda e: e.memset(ap, val), writes=wr)

        def recip(out, in_, rd, wr):
            P.op("dve", lambda e: e.reciprocal(out=out, in_=in_), reads=rd, writes=wr)

        def dma(out, in_, rd, wr, final=False, q="sp"):
            ev = P.op(q, lambda e: e.dma_start(out=out, in_=in_), reads=rd, writes=wr, dma=True)
            if final:
                out_events.append(ev)
            return ev

        def allgather(inp, outp, rd, wr):
            return P.op("pool", lambda e: e.collective_compute("AllGather", ALU.bypass, replica_groups=PAIRS, ins=[inp], outs=[outp]), reads=rd, writes=wr, cc=True)

        CCR = 512

        def allgather_rows(src, dst, nrows, rd_fn, key):
            for i, b0 in enumerate(range(0, nrows, CCR)):
                b1 = min(nrows, b0 + CCR)
                allgather(src[b0:b1, :], dst[2 * b0:2 * b1, :], rd_fn(b0, b1), [db(key, i)])

        def gathered(dst, nrows, key, r_, row0, nr):
            i = row0 // CCR
            b0 = i * CCR
            b1 = min(nrows, b0 + CCR)
            assert row0 + nr <= b1
            base = 2 * b0 + r_ * (b1 - b0) + (row0 - b0)
            return dst[base:base + nr, :], [db(key, i)]

        cm = A.alloc(7 * 128)
        dma(cm.ap(), I["cmat"], [], cm.b())
        cmv = cm.v3(7, 128)
        ident, ones, mT, mL, mTs, mLs, prot = [cmv[:, i, :] for i in range(7)]
        CM = cm.b()
        small = A.alloc(1024)
        SM = small.b()
        sm = small.ap()
        mset(sm[:, 0:1], EPS, SM)
        mset(sm[:, 1:2], 1.0, SM)
        epsc = sm[:, 0:1]
        dma(sm[:, 8:16], I["fng"], [], SM)
        dma(sm[:, 16:18], I["sub"], [], SM)
        dma(sm[0:64, 18:20], I["dtb"], [], SM)
        dma(sm[0:64, 20:22], I["alog"], [], SM)
        act(sm[0:64, 20:22], sm[0:64, 20:22], AF.Exp, SM, SM)
        ts(sm[0:64, 20:22], sm[0:64, 20:22], -1.0, None, ALU.mult, None, SM, SM)
        dma(sm[:, 64:128], I["dsk"], [], SM)
        dma(sm[:, 32:34], I["rmask"], [], SM)
        mA = sm[:, 32:33]
        mB = sm[:, 33:34]
        lam_init = [0.8 - 0.6 * math.exp(-0.3 * 1), 0.8 - 0.6 * math.exp(-0.3 * 3)]
        lw = A.alloc(512)
        dma(lw.ap(), I["lamb"], [], lw.b())
        lwv = lw.v3(2, 256)
        for sl in range(2):
            for j in range(2):
                tt(lwv[:, sl, j * 128:j * 128 + 64], lwv[:, sl, j * 128:j * 128 + 64], lwv[:, sl, j * 128 + 64:j * 128 + 128], ALU.mult, lw.b(), lw.b())
                P.op("dve", lambda e, sl=sl, j=j: e.tensor_reduce(out=sm[:, 28 + 2 * sl + j:29 + 2 * sl + j], in_=lwv[:, sl, j * 128:j * 128 + 64], axis=AX.X, op=ALU.add), reads=lw.b(), writes=SM)
            act(sm[:, 28 + 2 * sl:30 + 2 * sl], sm[:, 28 + 2 * sl:30 + 2 * sl], AF.Exp, SM, SM)
            tt(sm[:, 24 + sl:25 + sl], sm[:, 28 + 2 * sl:29 + 2 * sl], sm[:, 29 + 2 * sl:30 + 2 * sl], ALU.subtract, SM, SM)
            ts(sm[:, 24 + sl:25 + sl], sm[:, 24 + sl:25 + sl], lam_init[sl], -1.0, ALU.add, ALU.mult, SM, SM)
            ts(sm[:, 26 + sl:27 + sl], sm[:, 16 + sl:17 + sl], 1.0 - lam_init[sl], None, ALU.mult, None, SM, SM)
        vec = A.alloc(32 + 32 + 32 + 144 + 48 + 528 + 176)
        VB = vec.b()
        va = vec.ap()
        o_ = 0
        nmg = va[:, 0:32]; nfg = va[:, 32:64]; sng = va[:, 64:96]
        scw = va[:, 96:240]; scb = va[:, 240:288]; fcw = va[:, 288:816]; fcb = va[:, 816:992]
        for ap_, nm in ((nmg, "nmg"), (nfg, "nfg"), (sng, "sng"), (scw, "ssd_cw"), (scb, "ssd_cb"), (fcw, "fcw"), (fcb, "fcb")):
            dma(ap_, I[nm], [], VB)
        modv = A.alloc(4 * 48 * 2)
        MB = modv.b()
        mv = modv.ap().rearrange("p (l f c) -> p l f c", l=4, f=48, c=2)
        mbt = A.alloc(4 * 48)
        dma(mbt.ap(), I["mod_bT"], [], mbt.b())
        mbv = mbt.v3(4, 48)
        cmod = A.alloc(16)
        dma(cmod.ap(), I["cmod"], [], cmod.b())
        act(cmod.ap(), cmod.ap(), AF.Silu, cmod.b(), cmod.b())
        cmv2 = cmod.v3(8, 2)
        xT = A.alloc(8 * LH)
        phase_mark = A.mark()
        RG = {}

        def xTc(k, t0, t1):
            return xT.ap(k * LH + t0, k * LH + t1), xT.b(k * LH + t0, k * LH + t1)

        mM = A.mark()
        RG["wsm"] = Ring(A, 4, 1024)
        for l in range(depth_run):
            for f in range(48):
                w = RG["wsm"].next()
                dma(w.v3(8, 128), I["mod_w"][l, :, f * 128:(f + 1) * 128].rearrange("(k p) n -> p k n", p=128), [], w.b())
                bank = f % 2
                for k in range(8):
                    mm(psum[bank][:, 0:2], w.v3(8, 128)[:, k, :], cmv2[:, k, :], k == 0, k == 7, w.b() + cmod.b(), [pb[bank]])
                ts(mv[:, l, f, :], psum[bank][:, 0:2], mbv[:, l, f:f + 1], None, ALU.add, None, [pb[bank]] + mbt.b(), MB)
        for l in range(depth_run):
            for c in range(2):
                stt(mv[:, l, 8:16, c], mv[:, l, 8:16, c], 1.0, nmg[:, l * 8:(l + 1) * 8], ALU.add, ALU.mult, MB + VB, MB)
                stt(mv[:, l, 32:40, c], mv[:, l, 32:40, c], 1.0, nfg[:, l * 8:(l + 1) * 8], ALU.add, ALU.mult, MB + VB, MB)

        A.release(mM)
        sqr = Ring(A, 2, 512)
        rstd_t = A.alloc(512)
        phase_mark = A.mark()

        def compute_rstd(t0, ncol, bank):
            for k in range(8):
                s = sqr.next()
                xa, xb = xTc(k, t0, t0 + ncol)
                act(s.ap(0, ncol), xa, AF.Square, xb, s.b())
                mm(psum[bank][:, :ncol], ones, s.ap(0, ncol), k == 0, k == 7, s.b() + CM, [pb[bank]])
            act(rstd_t.ap(0, ncol), psum[bank][:, :ncol], AF.Sqrt, [pb[bank]] + SM, rstd_t.b(), bias=epsc, scale=1.0 / D)
            recip(rstd_t.ap(0, ncol), rstd_t.ap(0, ncol), rstd_t.b(), rstd_t.b())

        def make_hs(hs, t0, ncol, l, gidx, sidx, c, bank, col_off=0):
            compute_rstd(t0, ncol, bank)
            W = hs.n // 8
            for k in range(8):
                xa, xb = xTc(k, t0, t0 + ncol)
                o = hs.ap(k * W + col_off, k * W + col_off + ncol)
                ob = hs.b(k * W + col_off, k * W + col_off + ncol)
                stt(o, xa, mv[:, l, gidx + k, c:c + 1], rstd_t.ap(0, ncol), ALU.mult, ALU.mult, xb + MB + rstd_t.b(), ob)
                act(o, o, AF.Identity, ob + MB, ob, bias=mv[:, l, sidx + k, c:c + 1])

        def load_wcols(Wd, col0, ncol):
            w = RG["wsm"].next()
            dma(w.v3(8, 128)[:, :, 0:ncol], Wd[:, col0:col0 + ncol].rearrange("(k p) n -> p k n", p=128), [], w.b())
            return w

        def proj_fm(hs, Wd, col0, ncol, c_lo, c_hi, bank):
            w = load_wcols(Wd, col0, ncol)
            Wn = hs.n // 8
            for k in range(8):
                mm(psum[bank][0:ncol, c_lo:c_hi], w.v3(8, 128)[:, k, 0:ncol], hs.ap(k * Wn + c_lo, k * Wn + c_hi), k == 0, k == 7,
                   w.b() + hs.b(k * Wn + c_lo, k * Wn + c_hi), [pb[bank]])

        def conv3(dst, dstb, u, n, cw3, cbias, ub, npart=128):
            ts(dst, u[:, 0:n], cw3[:, 0:1], cbias, ALU.mult, ALU.add, ub + VB, dstb)
            stt(dst, u[:, 1:n + 1], cw3[:, 1:2], dst, ALU.mult, ALU.add, ub + VB + dstb, dstb)
            stt(dst, u[:, 2:n + 2], cw3[:, 2:3], dst, ALU.mult, ALU.add, ub + VB + dstb, dstb)

        def residual_update(l, gate_idx, c, oc, t0, ncol, bank):
            xa, xb = xTc(oc, t0, t0 + ncol)
            stt(xa, psum[bank][:, 0:ncol], mv[:, l, gate_idx + oc, c:c + 1], xa, ALU.mult, ALU.add, [pb[bank]] + MB + xb, xb)

        def edge_halos(l, G, gidx, sidx, c):
            edge = A.alloc(16)
            if not G["shard"]:
                mset(edge.ap(), 0.0, edge.b(), eng="pool")
                return edge
            T = G["T"]
            h2 = A.alloc(16); h3 = A.alloc(16); el = A.alloc(16); e0 = A.alloc(16); e1 = A.alloc(16)
            make_hs(h2, 0, 2, l, gidx, sidx, c, 0)
            make_hs(h3, T - 2, 2, l, gidx, sidx, c, 0)
            cp(el.ap(0, 8).unsqueeze(2), h2.v3(8, 2)[:, :, 0:1], h2.b(), el.b())
            cp(el.ap(8, 16).unsqueeze(2), h3.v3(8, 2)[:, :, 1:2], h3.b(), el.b())
            dma(H_loc, el.ap(), el.b(), [db("hloc")], q=STQ)
            allgather(H_loc, H_all, [db("hloc")], [db("hall")])
            dma(e0.ap(), H_all[0:128, :], [db("hall")], e0.b())
            dma(e1.ap(), H_all[128:256, :], [db("hall")], e1.b())
            mset(edge.ap(0, 8), 0.0, edge.b(), eng="pool")
            ts(e0.ap(8, 16), e0.ap(8, 16), mB, None, ALU.mult, None, e0.b() + SM, e0.b())
            stt(edge.ap(8, 16), e1.ap(8, 16), mA, e0.ap(8, 16), ALU.mult, ALU.add, e1.b() + SM + e0.b(), edge.b())
            return edge

        def load_x(src, T):
            m = A.mark()
            ring = Ring(A, 2, 1024)
            for q in range(T // 128):
                t = ring.next()
                dma(t.ap(), src[q * 128:(q + 1) * 128, :], [], t.b())
                for k in range(8):
                    bank = k % 2
                    P.op("pe", lambda e, bank=bank, t=t, k=k: e.transpose(psum[bank][:, 0:128], t.ap(k * 128, (k + 1) * 128), ident), reads=t.b() + CM, writes=[pb[bank]])
                    xa, xb = xTc(k, q * 128, (q + 1) * 128)
                    act(xa, psum[bank][:, 0:128], AF.Copy, [pb[bank]], xb)
            A.release(m)

        def final_out(dst, T):
            m = A.mark()
            ring = Ring(A, 2, 1024)
            yT = A.alloc(8 * 512)
            for b0 in range(0, T, 512):
                compute_rstd(b0, 512, 0)
                for k in range(8):
                    xa, xb = xTc(k, b0, b0 + 512)
                    stt(yT.ap(k * 512, (k + 1) * 512), xa, sm[:, 8 + k:9 + k], rstd_t.ap(0, 512), ALU.mult, ALU.mult, xb + SM + rstd_t.b(), yT.b(k * 512, (k + 1) * 512))
                for q in range(4):
                    t = ring.next()
                    for k in range(8):
                        bank = 2 + k % 2
                        P.op("pe", lambda e, bank=bank, k=k, q=q: e.transpose(psum[bank][:, 0:128], yT.ap(k * 512 + q * 128, k * 512 + (q + 1) * 128), ident),
                             reads=yT.b(k * 512 + q * 128, k * 512 + (q + 1) * 128) + CM, writes=[pb[bank]])
                        act(t.ap(k * 128, (k + 1) * 128), psum[bank][:, 0:128], AF.Copy, [pb[bank]], t.b(k * 128, (k + 1) * 128))
                    dma(dst[b0 + q * 128:b0 + (q + 1) * 128, :], t.ap(), t.b(), [], final=True, q=STQ)
            A.release(m)

        def ffn_packed(l, G):
            m = A.mark()
            c = G["c"]
            RG["wsm"] = Ring(A, 4, 1024)
            wbig = Ring(A, 2, 22 * 128)
            hs = A.alloc(8 * 512)
            actT = A.alloc(22 * 512)
            ur = Ring(A, 2, 1024)
            tr_ = Ring(A, 3, 1024)
            for u in ur.tiles:
                mset(u.ap(0, 516), 0.0, u.b(), eng="pool")
            Wup = I["ffn_up_w"][l]
            Wdn = I["ffn_down_w"][l]
            make_hs(hs, 0, 512, l, 32, 24, c, 0)
            for j in range(22):
                res = []
                for half in range(2):
                    ch = half * 22 + j
                    bank = 2 * half + (j % 2)
                    proj_fm(hs, Wup, half * FH + j * 128, 128, 0, 512, bank)
                    u = ur.next()
                    act(u.ap(1, 257), psum[bank][:, 0:256], AF.Copy, [pb[bank]], u.b())
                    act(u.ap(259, 515), psum[bank][:, 256:512], AF.Copy, [pb[bank]], u.b())
                    t = tr_.next()
                    cw3 = fcw[:, (l * 44 + ch) * 3:(l * 44 + ch) * 3 + 3]
                    conv3(t.ap(0, 514), t.b(), u.ap(0, 516), 514, cw3, fcb[:, l * 44 + ch:l * 44 + ch + 1], u.b())
                    res.append(t)
                ta, tv = res
                act(ta.ap(0, 514), ta.ap(0, 514), AF.Silu, ta.b(), ta.b())
                tt(actT.v3(2, 256, j * 512), ta.v3(2, 258)[:, :, 0:256], tv.v3(2, 258)[:, :, 0:256], ALU.mult, ta.b() + tv.b(), actT.b(j * 512, (j + 1) * 512))
            for oc in range(8):
                w = wbig.next()
                dma(w.v3(22, 128), Wdn[:, oc * 128:(oc + 1) * 128].rearrange("(k p) n -> p k n", p=128), [], w.b())
                bank = 4 + oc % 2
                for j in range(22):
                    mm(psum[bank][:, 0:512], w.v3(22, 128)[:, j, :], actT.ap(j * 512, (j + 1) * 512), j == 0, j == 21, w.b() + actT.b(j * 512, (j + 1) * 512), [pb[bank]])
                residual_update(l, 40, c, oc, 0, 512, bank)
            A.release(m)

        def ffn_layer(l, G):
            if G["packed"]:
                return ffn_packed(l, G)
            m = A.mark()
            c = G["c"]
            blocks = halo_blocks(G["nseq"], G["L"], 342)
            NB = max(b[1] for b in blocks)
            Wn = NB + 2
            RG["wsm"] = Ring(A, 4, 1024)
            wbig = Ring(A, 2, 22 * 128)
            hs = A.alloc(8 * Wn)
            actT = A.alloc(22 * NB)
            ur = Ring(A, 2, 512)
            tr_ = Ring(A, 3, 512)
            stash = A.alloc(8)
            edge = edge_halos(l, G, 32, 24, c)
            Wup = I["ffn_up_w"][l]
            Wdn = I["ffn_down_w"][l]
            for (t0, n, hl, hr) in blocks:
                c_hi = n + 2 if hr else n + 1
                make_hs(hs, t0, c_hi - 1, l, 32, 24, c, 0, col_off=1)
                if hl:
                    for k in range(8):
                        cp(hs.ap(k * Wn, k * Wn + 1), stash.ap(k, k + 1), stash.b(), hs.b(k * Wn, k * Wn + 1), eng="pool")
                else:
                    cp(hs.v3(8, Wn)[:, :, 0:1], edge.ap(0, 8).unsqueeze(2), edge.b(), hs.b(), eng="pool")
                if not hr:
                    cp(hs.v3(8, Wn)[:, :, n + 1:n + 2], edge.ap(8, 16).unsqueeze(2), edge.b(), hs.b(), eng="pool")
                c_lo, c_hi = 0, n + 2
                for k in range(8):
                    cp(stash.ap(k, k + 1), hs.ap(k * Wn + n, k * Wn + n + 1), hs.b(k * Wn + n, k * Wn + n + 1), stash.b(), eng="pool")
                for j in range(22):
                    res = []
                    for half in range(2):
                        ch = half * 22 + j
                        bank = 2 * half + (j % 2)
                        proj_fm(hs, Wup, half * FH + j * 128, 128, c_lo, c_hi, bank)
                        u = ur.next()
                        act(u.ap(c_lo, c_hi), psum[bank][:, c_lo:c_hi], AF.Copy, [pb[bank]], u.b())
                        t = tr_.next()
                        cw3 = fcw[:, (l * 44 + ch) * 3:(l * 44 + ch) * 3 + 3]
                        conv3(t.ap(0, n), t.b(), u.ap(), n, cw3, fcb[:, l * 44 + ch:l * 44 + ch + 1], u.b())
                        res.append(t)
                    ta, tv = res
                    act(ta.ap(0, n), ta.ap(0, n), AF.Silu, ta.b(), ta.b())
                    tt(actT.ap(j * NB, j * NB + n), ta.ap(0, n), tv.ap(0, n), ALU.mult, ta.b() + tv.b(), actT.b(j * NB, j * NB + n))
                for oc in range(8):
                    w = wbig.next()
                    dma(w.v3(22, 128), Wdn[:, oc * 128:(oc + 1) * 128].rearrange("(k p) n -> p k n", p=128), [], w.b())
                    bank = 4 + oc % 2
                    for j in range(22):
                        mm(psum[bank][:, 0:n], w.v3(22, 128)[:, j, :], actT.ap(j * NB, j * NB + n), j == 0, j == 21, w.b() + actT.b(j * NB, j * NB + n), [pb[bank]])
                    residual_update(l, 40, c, oc, t0, n, bank)
            A.release(m)

        def ssd_layer(l, G):
            slot = l // 2
            c = G["c"]
            T, L, nseq = G["T"], G["L"], G["nseq"]
            Win = I["ssd_in_w"][slot]
            Lf = L
            shard = G["shard"]
            NT = Lf // 128
            m0 = A.mark()
            ssq = A.alloc(nseq * NT * 4 + 8)
            m = A.mark()
            blocks = halo_blocks(nseq, L)
            if G["packed"]:
                blocks = []
                RG["wsm"] = Ring(A, 4, 1024)
                hs = A.alloc(8 * 512)
                ur = Ring(A, 2, 1024)
                tr_ = Ring(A, 3, 1024)
                for u in ur.tiles:
                    mset(u.ap(0, 516), 0.0, u.b(), eng="pool")
                make_hs(hs, 0, 512, l, 8, 0, c, 0)
                for j in range(41):
                    bank = 2 + j % 2
                    ncol = 128 if j < 40 else 64
                    proj_fm(hs, Win, j * 128, ncol, 0, 512, bank)
                    t = tr_.next()
                    if j < 16:
                        act(t.ap(0, 512), psum[bank][:, 0:512], AF.Silu, [pb[bank]], t.b())
                        dma(S_proj[j * 128:(j + 1) * 128, 0:512], t.ap(0, 512), t.b(), [db("proj", j, 0)], q=STQ)
                    elif j < 40:
                        u = ur.next()
                        act(u.ap(1, 257), psum[bank][:, 0:256], AF.Copy, [pb[bank]], u.b())
                        act(u.ap(259, 515), psum[bank][:, 256:512], AF.Copy, [pb[bank]], u.b())
                        ch = slot * 24 + (j - 16)
                        conv3(t.ap(0, 514), t.b(), u.ap(0, 516), 514, scw[:, ch * 3:ch * 3 + 3], scb[:, ch:ch + 1], u.b())
                        act(t.ap(0, 514), t.ap(0, 514), AF.Silu, t.b(), t.b())
                        dma(S_proj[j * 128:(j + 1) * 128, 0:256], t.ap(0, 256), t.b(), [db("proj", j, 0)], q=STQ)
                        dma(S_proj[j * 128:(j + 1) * 128, 256:512], t.ap(258, 514), t.b(), [db("proj", j, 0)], q=STQ)
                    else:
                        tp = t.arena.t[0:64, t.off:t.off + 512]
                        act(tp, psum[bank][0:64, 0:512], AF.Exp, [pb[bank]] + SM, t.b(), bias=sm[0:64, 18 + slot:19 + slot])
                        act(tp, tp, AF.Ln, t.b() + SM, t.b(), bias=sm[0:64, 1:2])
                        t2 = tr_.next()
                        tp2 = t2.arena.t[0:64, t2.off:t2.off + 512]
                        ts(tp2, tp, sm[0:64, 20 + slot:21 + slot], None, ALU.mult, None, t.b() + SM, t2.b())
                        dma(S_proj[5184:5248, 0:512], tp2, t2.b(), [db("proj", 41, 0)], q=STQ)
                        dma(S_proj[5120:5184, 0:512], tp, t.b(), [db("proj", 40, 0)], q=STQ)
            else:
                NB = max(b[1] for b in blocks)
                Wn = NB + 2
                RG["wsm"] = Ring(A, 4, 1024)
                hs = A.alloc(8 * Wn)
                ur = Ring(A, 3, 512)
                tr_ = Ring(A, 3, 512)
                edge = edge_halos(l, G, 8, 0, c)
            for bi, (t0, n, hl, hr) in enumerate(blocks):
                c_lo = 0 if hl else 1
                c_hi = n + 2 if hr else n + 1
                lo_tok = t0 - 1 + c_lo
                make_hs(hs, lo_tok, c_hi - c_lo, l, 8, 0, c, 0, col_off=c_lo)
                if not hl:
                    cp(hs.v3(8, Wn)[:, :, 0:1], edge.ap(0, 8).unsqueeze(2), edge.b(), hs.b(), eng="pool")
                if not hr:
                    cp(hs.v3(8, Wn)[:, :, n + 1:n + 2], edge.ap(8, 16).unsqueeze(2), edge.b(), hs.b(), eng="pool")
                c_lo, c_hi = 0, n + 2
                for j in range(41):
                    bank = 2 + j % 2
                    ncol = 128 if j < 40 else 64
                    proj_fm(hs, Win, j * 128, ncol, c_lo, c_hi, bank)
                    t = tr_.next()
                    if j < 16:
                        act(t.ap(0, n), psum[bank][:, 1:n + 1], AF.Silu, [pb[bank]], t.b())
                    elif j < 40:
                        u = ur.next()
                        act(u.ap(c_lo, c_hi), psum[bank][:, c_lo:c_hi], AF.Copy, [pb[bank]], u.b())
                        ch = slot * 24 + (j - 16)
                        conv3(t.ap(0, n), t.b(), u.ap(), n, scw[:, ch * 3:ch * 3 + 3], scb[:, ch:ch + 1], u.b())
                        act(t.ap(0, n), t.ap(0, n), AF.Silu, t.b(), t.b())
                    else:
                        tp = t.arena.t[0:64, t.off:t.off + n]
                        act(tp, psum[bank][0:64, 1:n + 1], AF.Exp, [pb[bank]] + SM, t.b(), bias=sm[0:64, 18 + slot:19 + slot])
                        act(tp, tp, AF.Ln, t.b() + SM, t.b(), bias=sm[0:64, 1:2])
                        t2 = tr_.next()
                        tp2 = t2.arena.t[0:64, t2.off:t2.off + n]
                        ts(tp2, tp, sm[0:64, 20 + slot:21 + slot], None, ALU.mult, None, t.b() + SM, t2.b())
                        dma(S_proj[5184:5248, t0:t0 + n], tp2, t2.b(), [db("proj", 41, bi)], q=STQ)
                    if j < 40:
                        dma(S_proj[j * 128:(j + 1) * 128, t0:t0 + n], t.ap(0, n), t.b(), [db("proj", j, bi)], q=STQ)
                    else:
                        dma(S_proj[5120:5184, t0:t0 + n], t.arena.t[0:64, t.off:t.off + n], t.b(), [db("proj", 40, bi)], q=STQ)
            A.release(m)
            nblk = max(1, len(blocks))

            def pj(j):
                return [db("proj", j, bi) for bi in range(nblk)]

            def psrc(row0, nrows, tb_, tok0, ntok, j):
                return S_proj[row0:row0 + nrows, tb_ + tok0:tb_ + tok0 + ntok], pj(j)

            m = A.mark()
            BT = A.alloc(Lf); CT = A.alloc(Lf)
            Btok = A.alloc(Lf)
            xtok = A.alloc(Lf)
            ytok = A.alloc(Lf)
            dtok = A.alloc(NT * 64)
            datok = A.alloc(NT * 64)
            hT = [A.alloc(128), A.alloc(128)]
            wk = Ring(A, 4, 512)
            sm2 = Ring(A, 6, 128)
            ld = Ring(A, 3, 512)
            PC = min(512, Lf)
            mset(ssq.ap(), 0.0, ssq.b(), eng="pool")
            for d in range(2):
                if d == 1 and shard:
                    allgather(S_hst, S_hstG, [db("hst", hp_) for hp_ in range(16)], [db("hstG")])
                for s in range(nseq):
                    tb = s * Lf
                    if d == 0 or nseq > 1:
                        for srow, dst_ in ((5120, dtok), (5184, datok)):
                            for p0 in range(0, Lf, PC):
                                lt = ld.next()
                                sa_, sb_ = psrc(srow, 64, tb, p0, PC, 40)
                                dma(lt.arena.t[0:64, lt.off:lt.off + PC], sa_, sb_ + pj(41), lt.b())
                                for qq in range(PC // 128):
                                    q = p0 // 128 + qq
                                    P.op("pe", lambda e, lt=lt, qq=qq: e.transpose(psum[0][:, 0:64], lt.arena.t[0:64, lt.off + qq * 128:lt.off + (qq + 1) * 128], ident[0:64, 0:64]),
                                         reads=lt.b() + CM, writes=[pb[0]])
                                    cp(dst_.ap(q * 64, (q + 1) * 64), psum[0][:, 0:64], [pb[0]], dst_.b())
                    for hp in range(16):
                        g = hp // 4
                        if hp % 4 == 0:
                            sa_, sb_ = psrc(4096 + g * 128, 128, tb, 0, Lf, 32 + g)
                            dma(BT.ap(), sa_, sb_, BT.b())
                            sa_, sb_ = psrc(4608 + g * 128, 128, tb, 0, Lf, 36 + g)
                            dma(CT.ap(), sa_, sb_, CT.b())
                            for q in range(NT):
                                P.op("pe", lambda e, q=q: e.transpose(psum[1][:, 0:128], BT.ap(q * 128, (q + 1) * 128), ident), reads=BT.b(q * 128, (q + 1) * 128) + CM, writes=[pb[1]])
                                cp(Btok.ap(q * 128, (q + 1) * 128), psum[1][:, 0:128], [pb[1]], Btok.b(q * 128, (q + 1) * 128))
                        for p0 in range(0, Lf, PC):
                            lt = ld.next()
                            sa_, sb_ = psrc(2048 + hp * 128, 128, tb, p0, PC, 16 + hp)
                            dma(lt.ap(0, PC), sa_, sb_, lt.b())
                            for qq in range(PC // 128):
                                q = p0 // 128 + qq
                                P.op("pe", lambda e, lt=lt, qq=qq: e.transpose(psum[1][:, 0:128], lt.ap(qq * 128, (qq + 1) * 128), ident), reads=lt.b() + CM, writes=[pb[1]])
                                cp(xtok.ap(q * 128, (q + 1) * 128), psum[1][:, 0:128], [pb[1]], xtok.b(q * 128, (q + 1) * 128))
                                if d == 0:
                                    tt(ytok.v3(2, 64, q * 128), xtok.v3(2, 64, q * 128), sm[:, 64 + slot * 32 + 2 * hp:64 + slot * 32 + 2 * hp + 2].unsqueeze(2).broadcast_to([128, 2, 64]),
                                       ALU.mult, xtok.b(q * 128, (q + 1) * 128) + SM, ytok.b(q * 128, (q + 1) * 128))
                        if d == 1:
                            for q in range(NT):
                                dma(ytok.ap(q * 128, (q + 1) * 128), S_y[tb + q * 128:tb + (q + 1) * 128, hp * 128:(hp + 1) * 128], [db("y", s, q, hp)], ytok.b(q * 128, (q + 1) * 128))
                        h = hT[d]
                        HB = h.b()
                        hc0 = d * 32 + 2 * hp
                        if G["sample"]:
                            st_in = wk.next()
                            if d == 0:
                                dma(st_in.ap(0, 128), I["state"][slot, d, hp], [], st_in.b())
                            else:
                                st2 = wk.next()
                                dma(st_in.ap(0, 128), S_hstG[hp * 128:(hp + 1) * 128, :], [db("hstG")], st_in.b())
                                dma(st2.ap(0, 128), S_hstG[2048 + hp * 128:2048 + (hp + 1) * 128, :], [db("hstG")], st2.b())
                                ts(st_in.ap(0, 128), st_in.ap(0, 128), mB, None, ALU.mult, None, st_in.b() + SM, st_in.b())
                                stt(h.ap(), st2.ap(0, 128), mA, st_in.ap(0, 128), ALU.mult, ALU.add, st2.b() + SM + st_in.b(), HB)
                            if d == 0:
                                P.op("pe", lambda e, st_in=st_in: e.transpose(psum[1][:, 0:128], st_in.ap(0, 128), ident), reads=st_in.b() + CM, writes=[pb[1]])
                                cp(h.ap(), psum[1][:, 0:128], [pb[1]], HB)
                        else:
                            mset(h.ap(), 0.0, HB, eng="pool")
                        order = range(NT) if d == 0 else range(NT - 1, -1, -1)
                        mR = mT if d == 0 else mL
                        mLh = mLs if d == 0 else mTs
                        for q in order:
                            da = datok.v3(NT, 64)[:, q, hc0:hc0 + 2]
                            dtq = dtok.v3(NT, 64)[:, q, hc0:hc0 + 2]
                            R = wk.next()
                            tt(R.v3(2, 128), mR.unsqueeze(1).broadcast_to([128, 2, 128]), da.unsqueeze(2).broadcast_to([128, 2, 128]), ALU.mult, CM + datok.b(), R.b())
                            mm(psum[2][:, 0:256], mLh, R.ap(0, 256), True, True, CM + R.b(), [pb[2]])
                            mm(psum[3][:, 0:2], mR, da, True, True, CM + datok.b(), [pb[3]])
                            mm(psum[3][:, 8:10], ones, da, True, True, CM + datok.b(), [pb[3]])
                            E = wk.next()
                            act(E.ap(0, 256), psum[2][:, 0:256], AF.Exp, [pb[2]], E.b())
                            ea = sm2.next()
                            act(ea.ap(0, 2), psum[3][:, 0:2], AF.Exp, [pb[3]], ea.b())
                            act(ea.ap(8, 10), psum[3][:, 8:10], AF.Exp, [pb[3]], ea.b())
                            mm(psum[4][:, 0:128], BT.ap(q * 128, (q + 1) * 128), CT.ap(q * 128, (q + 1) * 128), True, True, BT.b(q * 128, (q + 1) * 128) + CT.b(q * 128, (q + 1) * 128), [pb[4]])
                            cb = sm2.next()
                            tt(cb.ap(), psum[4][:, 0:128], mR, ALU.mult, [pb[4]] + CM, cb.b())
                            xd = wk.next()
                            tt(xd.v3(2, 64), xtok.v3(2, 64, q * 128), dtq.unsqueeze(2).broadcast_to([128, 2, 64]), ALU.mult, xtok.b(q * 128, (q + 1) * 128) + dtok.b(), xd.b())
                            lastc = 127 if d == 0 else 0
                            tt(xd.v3(2, 64, 128), xd.v3(2, 64), E.v3(2, 128)[:, :, lastc:lastc + 1].broadcast_to([128, 2, 64]), ALU.mult, xd.b() + E.b(), xd.b())
                            tt(E.v3(2, 128), E.v3(2, 128), cb.ap().unsqueeze(1).broadcast_to([128, 2, 128]), ALU.mult, E.b() + cb.b(), E.b())
                            for hh in range(2):
                                mm(psum[5][:, hh * 64:(hh + 1) * 64], E.v3(2, 128)[:, hh, :], xd.ap(hh * 64, (hh + 1) * 64), True, True, E.b() + xd.b(), [pb[5]])
                            mm(psum[6][:, 0:128], CT.ap(q * 128, (q + 1) * 128), h.ap(), True, True, CT.b(q * 128, (q + 1) * 128) + HB, [pb[6]])
                            mm(psum[7][:, 0:128], Btok.ap(q * 128, (q + 1) * 128), xd.ap(128, 256), True, True, Btok.b(q * 128, (q + 1) * 128) + xd.b(), [pb[7]])
                            yb = ytok.b(q * 128, (q + 1) * 128)
                            tmp = sm2.next()
                            tt(tmp.v3(2, 64), psum[6][:, 0:128].rearrange("p (a b) -> p a b", a=2, b=64), ea.ap(0, 2).unsqueeze(2).broadcast_to([128, 2, 64]), ALU.mult, [pb[6]] + ea.b(), tmp.b())
                            tt(ytok.ap(q * 128, (q + 1) * 128), ytok.ap(q * 128, (q + 1) * 128), tmp.ap(), ALU.add, yb + tmp.b(), yb)
                            tt(ytok.ap(q * 128, (q + 1) * 128), ytok.ap(q * 128, (q + 1) * 128), psum[5][:, 0:128], ALU.add, yb + [pb[5]], yb)
                            tt(h.v3(2, 64), h.v3(2, 64), ea.ap(8, 10).unsqueeze(2).broadcast_to([128, 2, 64]), ALU.mult, HB + ea.b(), HB)
                            tt(h.ap(), h.ap(), psum[7][:, 0:128], ALU.add, HB + [pb[7]], HB)
                        if not G["sample"]:
                            P.op("pe", lambda e, h=h: e.transpose(psum[1][:, 0:128], h.ap(), ident), reads=HB + CM, writes=[pb[1]])
                            so = wk.next()
                            cp(so.ap(0, 128), psum[1][:, 0:128], [pb[1]], so.b())
                            dma(O["nstate"][s, slot, d, hp], so.ap(0, 128), so.b(), [], final=True, q=STQ)
                        elif d == 0:
                            dma(S_hst[hp * 128:(hp + 1) * 128, :], h.ap(), HB, [db("hst", hp)], q=STQ)
                        if d == 1:
                            for p0 in range(0, Lf, PC):
                                lt = ld.next()
                                sa_, sb_ = psrc(hp * 128, 128, tb, p0, PC, hp)
                                dma(lt.ap(0, PC), sa_, sb_, lt.b())
                                for qq in range(PC // 128):
                                    q = p0 // 128 + qq
                                    P.op("pe", lambda e, lt=lt, qq=qq: e.transpose(psum[1][:, 0:128], lt.ap(qq * 128, (qq + 1) * 128), ident), reads=lt.b() + CM, writes=[pb[1]])
                                    yb = ytok.b(q * 128, (q + 1) * 128)
                                    tt(ytok.ap(q * 128, (q + 1) * 128), ytok.ap(q * 128, (q + 1) * 128), psum[1][:, 0:128], ALU.mult, yb + [pb[1]], yb)
                                    sq_ = sm2.next()
                                    sc_ = nseq * NT * 4
                                    P.op("act", lambda e, sq_=sq_, q=q, sc_=sc_: e.activation(out=sq_.ap(), in_=ytok.ap(q * 128, (q + 1) * 128), func=AF.Square, accum_out=ssq.ap(sc_, sc_ + 1)),
                                         reads=yb, writes=sq_.b() + ssq.b())
                                    si = (s * NT + q) * 4 + g
                                    tt(ssq.ap(si, si + 1), ssq.ap(si, si + 1), ssq.ap(sc_, sc_ + 1), ALU.add, ssq.b(), ssq.b())
                        for q in range(NT):
                            dma(S_y[tb + q * 128:tb + (q + 1) * 128, hp * 128:(hp + 1) * 128], ytok.ap(q * 128, (q + 1) * 128), ytok.b(q * 128, (q + 1) * 128), [db("y", s, q, hp)], q=STQ)
            A.release(m)
            m3 = A.mark()
            NTo = NT
            wbig = Ring(A, 2, 16 * 128)
            yr = Ring(A, 4, 2048)
            yTt = A.alloc(16 * 512)
            rs = A.alloc(nseq * NT * 4)
            act(rs.ap(), ssq.ap(0, nseq * NT * 4), AF.Sqrt, ssq.b() + SM, rs.b(), bias=epsc, scale=1.0 / 512)
            recip(rs.ap(), rs.ap(), rs.b(), rs.b())
            Wo = I["ssd_out_w"][slot]
            NTall = nseq * NT
            TBk = min(4, NTall)
            NW = TBk * 128
            for s in range(1):
                tb = 0
                for q0 in range(0, NTall, TBk):
                    for qq in range(TBk):
                        q = q0 + qq
                        yt = yr.next()
                        dma(yt.ap(), S_y[q * 128:(q + 1) * 128, :], [db("y", q // NT, q % NT, hp_) for hp_ in range(16)], yt.b())
                        for g in range(4):
                            si = q * 4 + g
                            ts(yt.ap(g * 512, (g + 1) * 512), yt.ap(g * 512, (g + 1) * 512), rs.ap(si, si + 1), None, ALU.mult, None, yt.b(g * 512, (g + 1) * 512) + rs.b(), yt.b(g * 512, (g + 1) * 512))
                        for kc in range(16):
                            bank = kc % 2
                            P.op("pe", lambda e, yt=yt, kc=kc, bank=bank: e.transpose(psum[bank][:, 0:128], yt.ap(kc * 128, (kc + 1) * 128), ident), reads=yt.b(kc * 128, (kc + 1) * 128) + CM, writes=[pb[bank]])
                            dst_lo = kc * NW + qq * 128
                            if kc % 2 == 0:
                                ts(yTt.ap(dst_lo, dst_lo + 128), psum[bank][:, 0:128], sng[:, slot * 16 + kc:slot * 16 + kc + 1], None, ALU.mult, None, [pb[bank]] + VB, yTt.b(dst_lo, dst_lo + 128))
                            else:
                                act(yTt.ap(dst_lo, dst_lo + 128), psum[bank][:, 0:128], AF.Copy, [pb[bank]] + VB, yTt.b(dst_lo, dst_lo + 128), scale=sng[:, slot * 16 + kc:slot * 16 + kc + 1])
                    for oc in range(8):
                        w = wbig.next()
                        dma(w.v3(16, 128), Wo[:, oc * 128:(oc + 1) * 128].rearrange("(k p) n -> p k n", p=128), [], w.b())
                        bank = 2 + oc % 2
                        for kc in range(16):
                            mm(psum[bank][:, 0:NW], w.v3(16, 128)[:, kc, :], yTt.ap(kc * NW, (kc + 1) * NW), kc == 0, kc == 15, w.b() + yTt.b(kc * NW, (kc + 1) * NW), [pb[bank]])
                        residual_update(l, 16, c, oc, tb + q0 * 128, NW, bank)
            A.release(m3)
            A.release(m0)

        def attn_layer(l, G):
            slot = l // 2
            c = G["c"]
            T, L, nseq = G["T"], G["L"], G["nseq"]
            sample = G["sample"]
            Wq = I["att_qkv_w"][slot]
            m = A.mark()
            RG["wsm"] = Ring(A, 4, 1024)
            wbig = Ring(A, 2, 8 * 256)
            hs = A.alloc(8 * 512)
            tr_ = Ring(A, 3, 512)
            rc = A.alloc(512); rsn = A.alloc(512)
            for bi, b0 in enumerate(range(0, T, 512)):
                make_hs(hs, b0, 512, l, 8, 0, c, 0)
                if sample:
                    dma(rc.ap(), I["ropeC"][:, b0:b0 + 512], [], rc.b())
                    dma(rsn.ap(), I["ropeS"][:, b0:b0 + 512], [], rsn.b())
                for j in range(16):
                    bank = 2 + j % 2
                    proj_fm(hs, Wq, j * 128, 128, 0, 512, bank)
                    t = tr_.next()
                    act(t.ap(), psum[bank][:, :], AF.Copy, [pb[bank]], t.b())
                    if sample:
                        mm(psum[4 + j % 2][:, :], prot, t.ap(), True, True, CM + t.b(), [pb[4 + j % 2]])
                        t2 = tr_.next()
                        tt(t2.ap(), psum[4 + j % 2][:, :], rsn.ap(), ALU.mult, [pb[4 + j % 2]] + rsn.b(), t2.b())
                        tt(t.ap(), t.ap(), rc.ap(), ALU.mult, t.b() + rc.b(), t.b())
                        tt(t.ap(), t.ap(), t2.ap(), ALU.add, t.b() + t2.b(), t.b())
                    if j < 8:
                        dma(S_q[j * 128:(j + 1) * 128, b0:b0 + 512], t.ap(), t.b(), [db("qk", j, bi)], q=STQ)
                    else:
                        dma(S_kv[(j - 8) * 128:(j - 7) * 128, b0:b0 + 512], t.ap(), t.b(), [db("qk", j, bi)], q=STQ)
                for part in (range(4, 12) if not sample else range(8, 12)):
                    w = wbig.next()
                    dma(w.v3(8, 256), Wq[:, part * 256:(part + 1) * 256].rearrange("(k p) n -> p k n", p=128), [], w.b())
                    for q in range(4):
                        bank = 6 + q % 2
                        for k in range(8):
                            mm(psum[bank][:, 0:256], hs.ap(k * 512 + q * 128, k * 512 + (q + 1) * 128), w.v3(8, 256)[:, k, :], k == 0, k == 7, hs.b(k * 512 + q * 128, k * 512 + (q + 1) * 128) + w.b(), [pb[bank]])
                        t = tr_.next()
                        act(t.ap(0, 256), psum[bank][:, 0:256], AF.Copy, [pb[bank]], t.b())
                        tok = b0 + q * 128
                        if sample:
                            dma(S_kv[D + tok:D + tok + 128, (part - 8) * 256:(part - 7) * 256], t.ap(0, 256), t.b(), [db("v", tok // 128)], q=STQ)
                        else:
                            s_, tq = tok // LP, tok % LP
                            if part < 8:
                                dma(O["nk"][s_, slot, tq:tq + 128, (part - 4) * 256:(part - 3) * 256], t.ap(0, 256), t.b(), [], final=True, q=STQ)
                            else:
                                dma(O["nv"][s_, slot, tq:tq + 128, (part - 8) * 256:(part - 7) * 256], t.ap(0, 256), t.b(), [db("v", tok // 128)], final=True, q=STQ)
            A.release(m)
            nblk = T // 512
            if G["shard"]:
                allgather_rows(S_kv, S_kvG, 2 * D, lambda b0, b1: ([db("qk", 8 + j, bi) for j in range(b0 // 128, b1 // 128) for bi in range(nblk)] if b0 < D
                                                                     else [db("v", q) for q in range((b0 - D) // 128, (b1 - D) // 128)]), "kvG")
            m = A.mark()
            NKT = (G["Lf"] + (256 if sample else 0)) // 128
            LK = NKT * 128
            QB = min(512, L)
            qT = A.alloc(L); kT = A.alloc(LK)
            vv = A.alloc(NKT * 128)
            pTr = Ring(A, 4, 512)
            sacc = [Ring(A, 2, 512), Ring(A, 2, 512)]
            rr = Ring(A, 4, 512)
            ot = Ring(A, 2, 512)
            ck_t = Ring(A, 2, 128)
            neglam = sm[:, 24 + slot:25 + slot]
            subs = sm[:, 26 + slot:27 + slot]
            vva = vv.v3(NKT, 128)
            qbi = 0
            for s in range(nseq):
                tb = s * L
                for hd in range(8):
                    kb = 0
                    dma(qT.ap(), S_q[hd * 128:(hd + 1) * 128, tb:tb + L], [db("qk", hd, bi) for bi in range(nblk)], qT.b())
                    if sample:
                        kb = 256
                        for q in range(2):
                            ct = ck_t.next()
                            dma(ct.ap(0, 128), I["ck"][slot, q * 128:(q + 1) * 128, hd * 128:(hd + 1) * 128], [], ct.b())
                            P.op("pe", lambda e, ct=ct: e.transpose(psum[1][:, 0:128], ct.ap(0, 128), ident), reads=ct.b() + CM, writes=[pb[1]])
                            cp(kT.ap(q * 128, (q + 1) * 128), psum[1][:, 0:128], [pb[1]], kT.b(q * 128, (q + 1) * 128))
                            dma(vva[:, q, :], I["cv"][slot, q * 128:(q + 1) * 128, hd * 128:(hd + 1) * 128], [], vv.b(q * 128, (q + 1) * 128))
                    if sample:
                        for r_ in range(2):
                            k0_ = kb + r_ * LH
                            ga, gb = gathered(S_kvG, 2 * D, "kvG", r_, hd * 128, 128)
                            dma(kT.ap(k0_, k0_ + LH), ga, gb, kT.b(k0_, k0_ + LH))
                            for q in range(LH // 128):
                                kq = k0_ // 128 + q
                                ga, gb = gathered(S_kvG, 2 * D, "kvG", r_, D + q * 128, 128)
                                dma(vva[:, kq, :], ga[:, hd * 128:(hd + 1) * 128], gb, vv.b(kq * 128, (kq + 1) * 128))
                    else:
                        dma(kT.ap(kb, kb + L), S_kv[hd * 128:(hd + 1) * 128, tb:tb + L], [db("qk", 8 + hd, bi) for bi in range(nblk)], kT.b(kb, kb + L))
                        for q in range(L // 128):
                            tok = tb + q * 128
                            src = O["nv"][tok // LP, slot, tok % LP:tok % LP + 128, hd * 128:(hd + 1) * 128]
                            kq = kb // 128 + q
                            dma(vva[:, kq, :], src, [db("v", tok // 128)], vv.b(kq * 128, (kq + 1) * 128))
                    for qb0 in range(0, L, QB):
                        par = qbi % 2
                        qbi += 1
                        sa = [sacc[0].next(), sacc[1].next()]
                        accb = [4 + 2 * par, 5 + 2 * par]

                        def score(kt, mp):
                            sb = 2 * (kt % 2) + mp
                            mm(psum[sb][:, 0:QB], kT.arena.t[mp * 64:(mp + 1) * 64, kT.off + kt * 128:kT.off + (kt + 1) * 128],
                               qT.arena.t[mp * 64:(mp + 1) * 64, qT.off + qb0:qT.off + qb0 + QB],
                               True, True, kT.b(kt * 128, (kt + 1) * 128) + qT.b(qb0, qb0 + QB), [pb[sb]])

                        score(0, 0); score(0, 1)
                        for kt in range(NKT):
                            pts = []
                            for mp in range(2):
                                sb = 2 * (kt % 2) + mp
                                pt = pTr.next()
                                act(pt.ap(0, QB), psum[sb][:, 0:QB], AF.Exp, [pb[sb]], pt.b(), scale=0.125)
                                pts.append(pt)
                            if kt + 1 < NKT:
                                score(kt + 1, 0); score(kt + 1, 1)
                            for mp in range(2):
                                pt = pts[mp]
                                mm(psum[accb[mp]][:, 0:QB], vva[:, kt, :], pt.ap(0, QB), kt == 0, kt == NKT - 1, vv.b(kt * 128, (kt + 1) * 128) + pt.b(), [pb[accb[mp]]])
                                if kt == 0:
                                    cp(sa[mp].ap(0, QB), pt.ap(0, QB), pt.b(), sa[mp].b(), eng="pool")
                                else:
                                    tt(sa[mp].ap(0, QB), sa[mp].ap(0, QB), pt.ap(0, QB), ALU.add, sa[mp].b() + pt.b(), sa[mp].b(), eng="pool")
                        rc_ = []
                        for mp in range(2):
                            mm(psum[mp][:, 0:QB], ones, sa[mp].ap(0, QB), True, True, CM + sa[mp].b(), [pb[mp]])
                            r = rr.next()
                            recip(r.ap(0, QB), psum[mp][:, 0:QB], [pb[mp]], r.b())
                            rc_.append(r)
                        ts(rc_[1].ap(0, QB), rc_[1].ap(0, QB), neglam, None, ALU.mult, None, rc_[1].b() + SM, rc_[1].b())
                        o = ot.next()
                        tt(o.ap(0, QB), psum[accb[0]][:, 0:QB], rc_[0].ap(0, QB), ALU.mult, [pb[accb[0]]] + rc_[0].b(), o.b())
                        tt(rc_[1].ap(0, QB), psum[accb[1]][:, 0:QB], rc_[1].ap(0, QB), ALU.mult, [pb[accb[1]]] + rc_[1].b(), rc_[1].b())
                        tt(o.ap(0, QB), o.ap(0, QB), rc_[1].ap(0, QB), ALU.add, o.b() + rc_[1].b(), o.b())
                        sq_ = rr.next()
                        act(sq_.ap(0, QB), o.ap(0, QB), AF.Square, o.b(), sq_.b())
                        mm(psum[0][:, 0:QB], ones, sq_.ap(0, QB), True, True, CM + sq_.b(), [pb[0]])
                        act(sq_.ap(0, QB), psum[0][:, 0:QB], AF.Sqrt, [pb[0]] + SM, sq_.b(), bias=epsc, scale=1.0 / 128)
                        recip(sq_.ap(0, QB), sq_.ap(0, QB), sq_.b(), sq_.b())
                        stt(o.ap(0, QB), o.ap(0, QB), subs, sq_.ap(0, QB), ALU.mult, ALU.mult, o.b() + SM + sq_.b(), o.b())
                        for b5 in range(qb0, qb0 + QB, 128):
                            pass
                        dma(S_o[hd * 128:(hd + 1) * 128, tb + qb0:tb + qb0 + QB], o.ap(0, QB), o.b(), [db("o", hd, (tb + qb0) // 512, (tb + qb0) % 512)], q=STQ)
            A.release(m)
            m = A.mark()
            RG["wsm"] = Ring(A, 4, 1024)
            oTb = A.alloc(8 * 512)
            Wo = I["att_out_w"][slot]
            for bi, b0 in enumerate(range(0, T, 512)):
                for hd in range(8):
                    dma(oTb.ap(hd * 512, (hd + 1) * 512), S_o[hd * 128:(hd + 1) * 128, b0:b0 + 512], [db("o", hd, bi, off_) for off_ in range(0, 512, min(512, L))], oTb.b(hd * 512, (hd + 1) * 512))
                for oc in range(8):
                    w = load_wcols(Wo, oc * 128, 128)
                    bank = 2 + oc % 2
                    for k in range(8):
                        mm(psum[bank][:, :], w.v3(8, 128)[:, k, :], oTb.ap(k * 512, (k + 1) * 512), k == 0, k == 7, w.b() + oTb.b(k * 512, (k + 1) * 512), [pb[bank]])
                    residual_update(l, 16, c, oc, b0, 512, bank)
            A.release(m)

        groups = []
        if do_sample:
            groups.append(dict(sample=True, shard=True, packed=False, c=0, T=LH, L=LH, Lf=LS, nseq=1, src=I["xs_in"], dst=O["y_s"]))
        if do_prompt:
            groups.append(dict(sample=False, shard=False, packed=True, c=1, T=2 * LP, L=LP, Lf=LP, nseq=2, src=I["xp_in"], dst=O["y_p"]))
        for G in groups:
            load_x(G["src"], G["T"])
            for l in range(depth_run):
                if l % 2 == 0:
                    ssd_layer(l, G)
                else:
                    attn_layer(l, G)
                ffn_layer(l, G)
            final_out(G["dst"], G["T"])
        P.emit(out_events)
    return nc


def _fm(v, nchunk):
    return np.ascontiguousarray(np.asarray(v, np.float32).reshape(nchunk, 128).T)


def _consts():
    a = np.arange(128)
    ident = np.eye(128, dtype=np.float32)
    ones = np.ones((128, 128), np.float32)
    mT = (a[:, None] <= a[None, :]).astype(np.float32)
    mL = (a[:, None] >= a[None, :]).astype(np.float32)
    mTs = (a[:, None] < a[None, :]).astype(np.float32)
    mLs = (a[:, None] > a[None, :]).astype(np.float32)
    i = a % 32
    partner = np.where(i < 16, a + 16, a - 16)
    prot = np.zeros((128, 128), np.float32)
    prot[partner, a] = 1.0
    cmat = np.stack([ident, ones, mT, mL, mTs, mLs, prot], axis=1).reshape(128, 7 * 128)
    t = np.arange(LS)
    row = (t // 64).astype(np.float32)
    col = (t % 64).astype(np.float32)
    inv = (1.0 / (np.float32(10000.0) ** (np.arange(0, 32, 2, dtype=np.float32) / np.float32(32)))).astype(np.float32)
    dd = a % 64
    axis_col = dd >= 32
    f = i % 16
    pos = np.where(axis_col[:, None], col[None, :], row[None, :]).astype(np.float32)
    ang = (pos * inv[f][:, None]).astype(np.float32)
    C = np.cos(ang).astype(np.float32)
    S = np.sin(ang).astype(np.float32)
    S = np.where((i < 16)[:, None], -S, S).astype(np.float32)
    return np.ascontiguousarray(cmat), np.ascontiguousarray(C), np.ascontiguousarray(S)


_CACHE = {}


def kernel(**inp):
    f = lambda k: np.asarray(inp[k], np.float32)
    key = (DEPTH_RUN, DO_SAMPLE, DO_PROMPT)
    if key not in _CACHE:
        _CACHE[key] = build_program(DEPTH_RUN, DO_SAMPLE, DO_PROMPT)
    nc = _CACHE[key]
    cmat, rC, rS = _consts()
    shared = {
        "mod_w": f("mod_w"),
        "mod_bT": np.ascontiguousarray(np.concatenate([_fm(f("mod_b")[l], 48) for l in range(4)], axis=1)),
        "nmg": np.ascontiguousarray(np.concatenate([_fm(f("norm_mix_g")[l], 8) for l in range(4)], axis=1)),
        "nfg": np.ascontiguousarray(np.concatenate([_fm(f("norm_ffn_g")[l], 8) for l in range(4)], axis=1)),
        "fng": _fm(f("final_norm_g"), 8),
        "ssd_in_w": f("ssd_in_w"),
        "ssd_cw": np.ascontiguousarray(f("ssd_conv_w").reshape(2, 3, 24, 128).transpose(3, 0, 2, 1).reshape(128, 2 * 24 * 3)),
        "ssd_cb": np.ascontiguousarray(f("ssd_conv_b").reshape(2, 24, 128).transpose(2, 0, 1).reshape(128, 48)),
        "dtb": np.ascontiguousarray(f("ssd_dt_bias").reshape(2, 64).T),
        "alog": np.ascontiguousarray(f("ssd_a_log").reshape(2, 64).T),
        "dsk": np.ascontiguousarray(np.broadcast_to(f("ssd_d").reshape(1, 64), (128, 64))),
        "sng": np.ascontiguousarray(np.concatenate([_fm(f("ssd_norm_g")[s], 16) for s in range(2)], axis=1)),
        "ssd_out_w": f("ssd_out_w"),
        "att_qkv_w": f("att_qkv_w"),
        "lamb": np.ascontiguousarray(np.broadcast_to(f("att_lambda").reshape(1, 512), (128, 512))),
        "sub": np.ascontiguousarray(f("att_subln_g").T),
        "att_out_w": f("att_out_w"),
        "ffn_up_w": f("ffn_up_w"),
        "fcw": np.ascontiguousarray(f("ffn_conv_w").reshape(4, 3, 44, 128).transpose(3, 0, 2, 1).reshape(128, 4 * 44 * 3)),
        "fcb": np.ascontiguousarray(f("ffn_conv_b").reshape(4, 44, 128).transpose(2, 0, 1).reshape(128, 4 * 44)),
        "ffn_down_w": f("ffn_down_w"),
        "cmat": cmat,
    }
    def dirswap(w):
        w2 = w.copy()
        w2[..., 5120:5152] = w[..., 5152:5184]
        w2[..., 5152:5184] = w[..., 5120:5152]
        return w2
    shared_odd = dict(shared)
    shared_odd["ssd_in_w"] = dirswap(f("ssd_in_w"))
    shared_odd["dtb"] = np.ascontiguousarray(f("ssd_dt_bias")[:, ::-1].reshape(2, 64).T)
    shared_odd["alog"] = np.ascontiguousarray(f("ssd_a_log")[:, ::-1].reshape(2, 64).T)
    shared_odd["ssd_cw"] = np.ascontiguousarray(f("ssd_conv_w")[:, ::-1].reshape(2, 3, 24, 128).transpose(3, 0, 2, 1).reshape(128, 2 * 24 * 3))
    shared_odd["fcw"] = np.ascontiguousarray(f("ffn_conv_w")[:, ::-1].reshape(4, 3, 44, 128).transpose(3, 0, 2, 1).reshape(128, 4 * 44 * 3))
    xs, xp = f("x_sample"), f("x_prompt")
    st, ck, cv, cc, cctx = f("state_ssd"), f("cache_k"), f("cache_v"), f("c"), f("c_ctx")
    in_maps = []
    for core in range(NCORES):
        b, r = core // 2, core % 2
        mp = dict(shared_odd if r else shared)
        fl = (lambda a, ax: np.flip(a, axis=ax)) if r else (lambda a, ax: a)
        mp["xs_in"] = np.ascontiguousarray(fl(xs[b, r * LH:(r + 1) * LH], 0))
        mp["ropeC"] = np.ascontiguousarray(fl(rC[:, r * LH:(r + 1) * LH], 1))
        mp["ropeS"] = np.ascontiguousarray(fl(rS[:, r * LH:(r + 1) * LH], 1))
        mp["rmask"] = np.ascontiguousarray(np.broadcast_to(np.array([[1.0 - r, float(r)]], np.float32), (128, 2)))
        mp["xp_in"] = np.ascontiguousarray(fl(xp[2 * core:2 * core + 2], 1).reshape(2 * LP, D))
        mp["state"] = np.ascontiguousarray(fl(st[b], 1).reshape(2, 2, 16, 128, 128))
        mp["ck"] = np.ascontiguousarray(ck[b].reshape(2, 256, D))
        mp["cv"] = np.ascontiguousarray(cv[b].reshape(2, 256, D))
        cm2 = np.stack([_fm(cc[b], 8), _fm(cctx, 8)], axis=2).reshape(128, 16)
        mp["cmod"] = np.ascontiguousarray(cm2)
        in_maps.append(mp)
    res = run_bass_kernel_spmd(nc, in_maps, core_ids=list(range(NCORES)))
    R = res.results
    def flo(c, a, ax):
        return np.flip(a, axis=ax) if c % 2 else a
    y_prompt = np.concatenate([flo(c, R[c]["y_p"].reshape(2, LP, D), 1) for c in range(NCORES)], axis=0)
    y_sample = np.stack([np.concatenate([R[2 * b]["y_s"], np.flip(R[2 * b + 1]["y_s"], axis=0)], axis=0) for b in range(4)], axis=0)
    nstate = np.concatenate([flo(c, R[c]["nstate"].reshape(2, 2, 2, 32, 64, 128), 2) for c in range(NCORES)], axis=0)
    nk = np.concatenate([flo(c, R[c]["nk"].reshape(2, 2, LP, 8, 2, 64), 2) for c in range(NCORES)], axis=0)
    nv = np.concatenate([flo(c, R[c]["nv"].reshape(2, 2, LP, 8, 128), 2) for c in range(NCORES)], axis=0)
    return (y_prompt.astype(np.float32), y_sample.astype(np.float32), nstate.astype(np.float32),
            nk.astype(np.float32), nv.astype(np.float32))
```

```python
import math
import numpy as np
from contextlib import ExitStack
import concourse.bass as bass
import concourse.mybir as mybir
from concourse.bass_utils import run_bass_kernel_spmd

F32 = mybir.dt.float32
AF = mybir.ActivationFunctionType
ALU = mybir.AluOpType
AX = mybir.AxisListType

DEPTH_RUN = 4
DO_SAMPLE = True
DO_PROMPT = True
NCORES = 8
EPS = 1e-6
D = 1024
LS = 2048
LH = 1024
PAIRS = [[0, 1], [2, 3], [4, 5], [6, 7]]
LP = 256
FH = 2816
NPAGES = 93
STQ = "act"


class Buf:
    __slots__ = ("name", "lw", "rd")

    def __init__(self, name=""):
        self.name = name
        self.lw = None
        self.rd = {}


class Prog:
    ENG = ["pe", "act", "dve", "pool", "sp"]
    NDMA = 16
    DMA_POOLS = {"sp": (0, 16), "act": (0, 16), "pool": (0, 16)}
    SHARED_POOL = True
    SAME_ENGINE_SYNC = True

    def __init__(self, nc):
        self.nc = nc
        self.streams = {e: [] for e in self.ENG}
        self.cnt = {e: 0 for e in self.ENG}
        self.seen = {e: {} for e in self.ENG}
        self.ndma = {q: 0 for q in self.DMA_POOLS}
        self.ncc = 0

    def op(self, eng, fn, reads=(), writes=(), dma=False, cc=False):
        deps = {}

        def add(ev):
            if ev is None:
                return
            k, v = ev
            if v > deps.get(k, 0):
                deps[k] = v

        for b in reads:
            add(b.lw)
        for b in writes:
            add(b.lw)
            for kv in b.rd.items():
                add(kv)
        if dma:
            base, npool = self.DMA_POOLS[eng]
            qk = "sp" if self.SHARED_POOL else eng
            i = self.ndma[qk]
            self.ndma[qk] += 1
            slot = base + i % npool
            val = 16 * (i // npool + 1)
            ev = (("dma", slot), val)
            if val > 16:
                add((("dma", slot), val - 16))
        elif cc:
            self.ncc += 1
            ev = ("cc", self.ncc)
        else:
            self.cnt[eng] += 1
            ev = (eng, self.cnt[eng])
        waits = []
        seen = self.seen[eng]
        for k, v in deps.items():
            if k == eng and (eng == "pe" or not self.SAME_ENGINE_SYNC):
                continue
            if seen.get(k, 0) >= v:
                continue
            seen[k] = v
            waits.append((k, v))
        self.streams[eng].append((fn, waits, ev))
        k, v = ev
        for b in reads:
            if b.rd.get(k, 0) < v:
                b.rd[k] = v
        for b in writes:
            b.lw = ev
            b.rd = {}
        return ev

    def emit(self, final_events=()):
        nc = self.nc
        with ExitStack() as es:
            sems = {}
            for e in self.ENG:
                sems[e] = es.enter_context(nc.semaphore("s_" + e))
            for i in range(self.NDMA):
                sems[("dma", i)] = es.enter_context(nc.semaphore("s_dma%d" % i))
            sems["cc"] = es.enter_context(nc.semaphore("s_cc"))
            block = es.enter_context(nc.Block())
            streams = self.streams

            def run(engname, eng):
                for fn, waits, ev in streams[engname]:
                    for k, v in waits:
                        eng.wait_ge(sems[k], v)
                    inst = fn(eng)
                    k, v = ev
                    inst.then_inc(sems[k], 16 if isinstance(k, tuple) else 1)
                if engname == "sp":
                    fe = {}
                    for k, v in final_events:
                        fe[k] = max(fe.get(k, 0), v)
                    for k, v in fe.items():
                        eng.wait_ge(sems[k], v)

            @block.tensor
            def _(eng):
                run("pe", eng)

            @block.scalar
            def _(eng):
                run("act", eng)

            @block.vector
            def _(eng):
                run("dve", eng)

            @block.gpsimd
            def _(eng):
                run("pool", eng)

            @block.sync
            def _(eng):
                run("sp", eng)


class Tile:
    def __init__(self, arena, off, n):
        self.arena = arena
        self.off = off
        self.n = n

    def ap(self, lo=0, hi=None):
        hi = self.n if hi is None else hi
        return self.arena.t[:, self.off + lo:self.off + hi]

    def v3(self, a, b, lo=0):
        return self.arena.t[:, self.off + lo:self.off + lo + a * b].rearrange("p (a b) -> p a b", a=a, b=b)

    def b(self, lo=0, hi=None):
        hi = self.n if hi is None else hi
        p0 = (self.off + lo) // 512
        p1 = (self.off + hi - 1) // 512
        return self.arena.pages[p0:p1 + 1]


class Arena:
    def __init__(self, nc, es, npages):
        self.t = es.enter_context(nc.sbuf_tensor("arena", [128, npages * 512], F32))
        self.pages = [Buf("pg%d" % i) for i in range(npages)]
        self.top = 0
        self.npages = npages

    def alloc(self, nelem):
        npg = (nelem + 511) // 512
        assert self.top + npg <= self.npages, ("arena overflow", self.top, npg, self.npages)
        t = Tile(self, self.top * 512, nelem)
        self.top += npg
        return t

    def mark(self):
        return self.top

    def release(self, m):
        self.top = m


class Ring:
    def __init__(self, arena, n, nelem):
        self.tiles = [arena.alloc(nelem) for _ in range(n)]
        self.i = 0

    def next(self):
        t = self.tiles[self.i % len(self.tiles)]
        self.i += 1
        return t


def halo_blocks(nseq, L, maxn=510):
    out = []
    if L > maxn:
        nb = -(-L // maxn)
        base = -(-L // nb)
        for s in range(nseq):
            t = 0
            while t < L:
                n = min(base, L - t)
                out.append((s * L + t, n, t > 0, t + n < L))
                t += n
    else:
        for s in range(nseq):
            out.append((s * L, L, False, False))
    return out


def build_program(depth_run, do_sample, do_prompt):
    nc = bass.Bass("TRN2", target_bir_lowering=False)
    es = ExitStack()

    def din(name, shape):
        return nc.dram_tensor(name, list(shape), F32, kind="ExternalInput").ap()

    def dout(name, shape):
        return nc.dram_tensor(name, list(shape), F32, kind="ExternalOutput").ap()

    def dscr(name, shape):
        return nc.dram_tensor(name, list(shape), F32, kind="Internal").ap()

    I = {}
    I["xs_in"] = din("xs_in", [LH, D])
    I["rmask"] = din("rmask", [128, 2])
    I["xp_in"] = din("xp_in", [2 * LP, D])
    I["state"] = din("state", [2, 2, 16, 128, 128])
    I["ck"] = din("ck", [2, 256, D])
    I["cv"] = din("cv", [2, 256, D])
    I["cmod"] = din("cmod", [128, 16])
    I["mod_w"] = din("mod_w", [4, D, 6 * D])
    I["mod_bT"] = din("mod_bT", [128, 4 * 48])
    I["nmg"] = din("nmg", [128, 32])
    I["nfg"] = din("nfg", [128, 32])
    I["fng"] = din("fng", [128, 8])
    I["ssd_in_w"] = din("ssd_in_w", [2, D, 5184])
    I["ssd_cw"] = din("ssd_cw", [128, 2 * 24 * 3])
    I["ssd_cb"] = din("ssd_cb", [128, 2 * 24])
    I["dtb"] = din("dtb", [64, 2])
    I["alog"] = din("alog", [64, 2])
    I["dsk"] = din("dsk", [128, 64])
    I["sng"] = din("sng", [128, 32])
    I["ssd_out_w"] = din("ssd_out_w", [2, 2048, D])
    I["att_qkv_w"] = din("att_qkv_w", [2, D, 3 * D])
    I["lamb"] = din("lamb", [128, 2 * 256])
    I["sub"] = din("sub", [128, 2])
    I["att_out_w"] = din("att_out_w", [2, D, D])
    I["ffn_up_w"] = din("ffn_up_w", [4, D, 2 * FH])
    I["fcw"] = din("fcw", [128, 4 * 44 * 3])
    I["fcb"] = din("fcb", [128, 4 * 44])
    I["ffn_down_w"] = din("ffn_down_w", [4, FH, D])
    I["cmat"] = din("cmat", [128, 7 * 128])
    I["ropeC"] = din("ropeC", [128, LH])
    I["ropeS"] = din("ropeS", [128, LH])
    O = {}
    O["y_s"] = dout("y_s", [LH, D])
    O["y_p"] = dout("y_p", [2 * LP, D])
    O["nstate"] = dout("nstate", [2, 2, 2, 16, 128, 128])
    O["nk"] = dout("nk", [2, 2, LP, D])
    O["nv"] = dout("nv", [2, 2, LP, D])
    S_proj = dscr("s_proj", [5248, LH])
    S_projG = dscr("s_projg", [2 * 5248, LH])
    S_y = dscr("s_y", [LS, 2048])
    S_q = dscr("s_q", [D, LH])
    S_kv = dscr("s_kv", [2 * D, LH])
    S_kvG = dscr("s_kvg", [4 * D, LH])
    S_o = dscr("s_o", [D, LH])
    S_hst = dscr("s_hst", [2048, 128])
    S_hstG = dscr("s_hstg", [4096, 128])
    H_loc = dscr("h_loc", [128, 16])
    H_all = dscr("h_all", [256, 16])
    dbufs = {}

    def db(*key):
        if key not in dbufs:
            dbufs[key] = Buf(str(key))
        return dbufs[key]

    with es:
        P = Prog(nc)
        A = Arena(nc, es, NPAGES)
        psum = [es.enter_context(nc.psum_tensor("ps%d" % i, [128, 512], F32)) for i in range(8)]
        pb = [Buf("psum%d" % i) for i in range(8)]
        out_events = []

        def mm(out, lhsT, rhs, start, stop, rd, wr):
            P.op("pe", lambda e: e.matmul(out, lhsT=lhsT, rhs=rhs, start=start, stop=stop), reads=rd, writes=wr)

        def act(out, in_, func, rd, wr, bias=None, scale=None, eng="act"):
            kw = {}
            if bias is not None:
                kw["bias"] = bias
            if scale is not None:
                kw["scale"] = scale
            P.op("act", lambda e: e.activation(out=out, in_=in_, func=func, **kw), reads=rd, writes=wr)

        def tt(out, in0, in1, op, rd, wr, eng="dve"):
            P.op(eng, lambda e: e.tensor_tensor(out=out, in0=in0, in1=in1, op=op), reads=rd, writes=wr)

        def ts(out, in0, s1, s2, op0, op1, rd, wr, eng="dve"):
            if op1 is None:
                P.op(eng, lambda e: e.tensor_scalar(out=out, in0=in0, scalar1=s1, scalar2=None, op0=op0), reads=rd, writes=wr)
            else:
                P.op(eng, lambda e: e.tensor_scalar(out=out, in0=in0, scalar1=s1, scalar2=s2, op0=op0, op1=op1), reads=rd, writes=wr)

        def stt(out, in0, scalar, in1, op0, op1, rd, wr):
            P.op("dve", lambda e: e.scalar_tensor_tensor(out=out, in0=in0, scalar=scalar, in1=in1, op0=op0, op1=op1), reads=rd, writes=wr)

        def cp(out, in_, rd, wr, eng="dve"):
            P.op(eng, lambda e: e.tensor_copy(out=out, in_=in_), reads=rd, writes=wr)

        def mset(ap, val, wr, eng="pool"):
            P.op(eng, lambda e: e.memset(ap, val), writes=wr)

        def recip(out, in_, rd, wr):
            P.op("dve", lambda e: e.reciprocal(out=out, in_=in_), reads=rd, writes=wr)

        def dma(out, in_, rd, wr, final=False, q="sp"):
            ev = P.op(q, lambda e: e.dma_start(out=out, in_=in_), reads=rd, writes=wr, dma=True)
            if final:
                out_events.append(ev)
            return ev

        def allgather(inp, outp, rd, wr):
            return P.op("pool", lambda e: e.collective_compute("AllGather", ALU.bypass, replica_groups=PAIRS, ins=[inp], outs=[outp]), reads=rd, writes=wr, cc=True)

        CCR = 512

        def allgather_rows(src, dst, nrows, rd_fn, key):
            for i, b0 in enumerate(range(0, nrows, CCR)):
                b1 = min(nrows, b0 + CCR)
                allgather(src[b0:b1, :], dst[2 * b0:2 * b1, :], rd_fn(b0, b1), [db(key, i)])

        def gathered(dst, nrows, key, r_, row0, nr):
            i = row0 // CCR
            b0 = i * CCR
            b1 = min(nrows, b0 + CCR)
            assert row0 + nr <= b1
            base = 2 * b0 + r_ * (b1 - b0) + (row0 - b0)
            return dst[base:base + nr, :], [db(key, i)]

        cm = A.alloc(7 * 128)
        dma(cm.ap(), I["cmat"], [], cm.b())
        cmv = cm.v3(7, 128)
        ident, ones, mT, mL, mTs, mLs, prot = [cmv[:, i, :] for i in range(7)]
        CM = cm.b()
        small = A.alloc(1024)
        SM = small.b()
        sm = small.ap()
        mset(sm[:, 0:1], EPS, SM)
        mset(sm[:, 1:2], 1.0, SM)
        epsc = sm[:, 0:1]
        dma(sm[:, 8:16], I["fng"], [], SM)
        dma(sm[:, 16:18], I["sub"], [], SM)
        dma(sm[0:64, 18:20], I["dtb"], [], SM)
        dma(sm[0:64, 20:22], I["alog"], [], SM)
        act(sm[0:64, 20:22], sm[0:64, 20:22], AF.Exp, SM, SM)
        ts(sm[0:64, 20:22], sm[0:64, 20:22], -1.0, None, ALU.mult, None, SM, SM)
        dma(sm[:, 64:128], I["dsk"], [], SM)
        dma(sm[:, 32:34], I["rmask"], [], SM)
        mA = sm[:, 32:33]
        mB = sm[:, 33:34]
        lam_init = [0.8 - 0.6 * math.exp(-0.3 * 1), 0.8 - 0.6 * math.exp(-0.3 * 3)]
        lw = A.alloc(512)
        dma(lw.ap(), I["lamb"], [], lw.b())
        lwv = lw.v3(2, 256)
        for sl in range(2):
            for j in range(2):
                tt(lwv[:, sl, j * 128:j * 128 + 64], lwv[:, sl, j * 128:j * 128 + 64], lwv[:, sl, j * 128 + 64:j * 128 + 128], ALU.mult, lw.b(), lw.b())
                P.op("dve", lambda e, sl=sl, j=j: e.tensor_reduce(out=sm[:, 28 + 2 * sl + j:29 + 2 * sl + j], in_=lwv[:, sl, j * 128:j * 128 + 64], axis=AX.X, op=ALU.add), reads=lw.b(), writes=SM)
            act(sm[:, 28 + 2 * sl:30 + 2 * sl], sm[:, 28 + 2 * sl:30 + 2 * sl], AF.Exp, SM, SM)
            tt(sm[:, 24 + sl:25 + sl], sm[:, 28 + 2 * sl:29 + 2 * sl], sm[:, 29 + 2 * sl:30 + 2 * sl], ALU.subtract, SM, SM)
            ts(sm[:, 24 + sl:25 + sl], sm[:, 24 + sl:25 + sl], lam_init[sl], -1.0, ALU.add, ALU.mult, SM, SM)
            ts(sm[:, 26 + sl:27 + sl], sm[:, 16 + sl:17 + sl], 1.0 - lam_init[sl], None, ALU.mult, None, SM, SM)
        vec = A.alloc(32 + 32 + 32 + 144 + 48 + 528 + 176)
        VB = vec.b()
        va = vec.ap()
        o_ = 0
        nmg = va[:, 0:32]; nfg = va[:, 32:64]; sng = va[:, 64:96]
        scw = va[:, 96:240]; scb = va[:, 240:288]; fcw = va[:, 288:816]; fcb = va[:, 816:992]
        for ap_, nm in ((nmg, "nmg"), (nfg, "nfg"), (sng, "sng"), (scw, "ssd_cw"), (scb, "ssd_cb"), (fcw, "fcw"), (fcb, "fcb")):
            dma(ap_, I[nm], [], VB)
        modv = A.alloc(4 * 48 * 2)
        MB = modv.b()
        mv = modv.ap().rearrange("p (l f c) -> p l f c", l=4, f=48, c=2)
        mbt = A.alloc(4 * 48)
        dma(mbt.ap(), I["mod_bT"], [], mbt.b())
        mbv = mbt.v3(4, 48)
        cmod = A.alloc(16)
        dma(cmod.ap(), I["cmod"], [], cmod.b())
        act(cmod.ap(), cmod.ap(), AF.Silu, cmod.b(), cmod.b())
        cmv2 = cmod.v3(8, 2)
        xT = A.alloc(8 * LH)
        phase_mark = A.mark()
        RG = {}

        def xTc(k, t0, t1):
            return xT.ap(k * LH + t0, k * LH + t1), xT.b(k * LH + t0, k * LH + t1)

        mM = A.mark()
        RG["wsm"] = Ring(A, 4, 1024)
        for l in range(depth_run):
            for f in range(48):
                w = RG["wsm"].next()
                dma(w.v3(8, 128), I["mod_w"][l, :, f * 128:(f + 1) * 128].rearrange("(k p) n -> p k n", p=128), [], w.b())
                bank = f % 2
                for k in range(8):
                    mm(psum[bank][:, 0:2], w.v3(8, 128)[:, k, :], cmv2[:, k, :], k == 0, k == 7, w.b() + cmod.b(), [pb[bank]])
                ts(mv[:, l, f, :], psum[bank][:, 0:2], mbv[:, l, f:f + 1], None, ALU.add, None, [pb[bank]] + mbt.b(), MB)
        for l in range(depth_run):
            for c in range(2):
                stt(mv[:, l, 8:16, c], mv[:, l, 8:16, c], 1.0, nmg[:, l * 8:(l + 1) * 8], ALU.add, ALU.mult, MB + VB, MB)
                stt(mv[:, l, 32:40, c], mv[:, l, 32:40, c], 1.0, nfg[:, l * 8:(l + 1) * 8], ALU.add, ALU.mult, MB + VB, MB)

        A.release(mM)
        sqr = Ring(A, 2, 512)
        rstd_t = A.alloc(512)
        phase_mark = A.mark()

        def compute_rstd(t0, ncol, bank):
            for k in range(8):
                s = sqr.next()
                xa, xb = xTc(k, t0, t0 + ncol)
                act(s.ap(0, ncol), xa, AF.Square, xb, s.b())
                mm(psum[bank][:, :ncol], ones, s.ap(0, ncol), k == 0, k == 7, s.b() + CM, [pb[bank]])
            act(rstd_t.ap(0, ncol), psum[bank][:, :ncol], AF.Sqrt, [pb[bank]] + SM, rstd_t.b(), bias=epsc, scale=1.0 / D)
            recip(rstd_t.ap(0, ncol), rstd_t.ap(0, ncol), rstd_t.b(), rstd_t.b())

        def make_hs(hs, t0, ncol, l, gidx, sidx, c, bank, col_off=0):
            compute_rstd(t0, ncol, bank)
            W = hs.n // 8
            for k in range(8):
                xa, xb = xTc(k, t0, t0 + ncol)
                o = hs.ap(k * W + col_off, k * W + col_off + ncol)
                ob = hs.b(k * W + col_off, k * W + col_off + ncol)
                stt(o, xa, mv[:, l, gidx + k, c:c + 1], rstd_t.ap(0, ncol), ALU.mult, ALU.mult, xb + MB + rstd_t.b(), ob)
                act(o, o, AF.Identity, ob + MB, ob, bias=mv[:, l, sidx + k, c:c + 1])

        def load_wcols(Wd, col0, ncol):
            w = RG["wsm"].next()
            dma(w.v3(8, 128)[:, :, 0:ncol], Wd[:, col0:col0 + ncol].rearrange("(k p) n -> p k n", p=128), [], w.b())
            return w

        def proj_fm(hs, Wd, col0, ncol, c_lo, c_hi, bank):
            w = load_wcols(Wd, col0, ncol)
            Wn = hs.n // 8
            for k in range(8):
                mm(psum[bank][0:ncol, c_lo:c_hi], w.v3(8, 128)[:, k, 0:ncol], hs.ap(k * Wn + c_lo, k * Wn + c_hi), k == 0, k == 7,
                   w.b() + hs.b(k * Wn + c_lo, k * Wn + c_hi), [pb[bank]])

        def conv3(dst, dstb, u, n, cw3, cbias, ub, npart=128):
            ts(dst, u[:, 0:n], cw3[:, 0:1], cbias, ALU.mult, ALU.add, ub + VB, dstb)
            stt(dst, u[:, 1:n + 1], cw3[:, 1:2], dst, ALU.mult, ALU.add, ub + VB + dstb, dstb)
            stt(dst, u[:, 2:n + 2], cw3[:, 2:3], dst, ALU.mult, ALU.add, ub + VB + dstb, dstb)

        def residual_update(l, gate_idx, c, oc, t0, ncol, bank):
            xa, xb = xTc(oc, t0, t0 + ncol)
            stt(xa, psum[bank][:, 0:ncol], mv[:, l, gate_idx + oc, c:c + 1], xa, ALU.mult, ALU.add, [pb[bank]] + MB + xb, xb)

        def edge_halos(l, G, gidx, sidx, c):
            edge = A.alloc(16)
            if not G["shard"]:
                mset(edge.ap(), 0.0, edge.b(), eng="pool")
                return edge
            T = G["T"]
            h2 = A.alloc(16); h3 = A.alloc(16); el = A.alloc(16); e0 = A.alloc(16); e1 = A.alloc(16)
            make_hs(h2, 0, 2, l, gidx, sidx, c, 0)
            make_hs(h3, T - 2, 2, l, gidx, sidx, c, 0)
            cp(el.ap(0, 8).unsqueeze(2), h2.v3(8, 2)[:, :, 0:1], h2.b(), el.b())
            cp(el.ap(8, 16).unsqueeze(2), h3.v3(8, 2)[:, :, 1:2], h3.b(), el.b())
            dma(H_loc, el.ap(), el.b(), [db("hloc")], q=STQ)
            allgather(H_loc, H_all, [db("hloc")], [db("hall")])
            dma(e0.ap(), H_all[0:128, :], [db("hall")], e0.b())
            dma(e1.ap(), H_all[128:256, :], [db("hall")], e1.b())
            mset(edge.ap(0, 8), 0.0, edge.b(), eng="pool")
            ts(e0.ap(8, 16), e0.ap(8, 16), mB, None, ALU.mult, None, e0.b() + SM, e0.b())
            stt(edge.ap(8, 16), e1.ap(8, 16), mA, e0.ap(8, 16), ALU.mult, ALU.add, e1.b() + SM + e0.b(), edge.b())
            return edge

        def load_x(src, T):
            m = A.mark()
            ring = Ring(A, 2, 1024)
            for q in range(T // 128):
                t = ring.next()
                dma(t.ap(), src[q * 128:(q + 1) * 128, :], [], t.b())
                for k in range(8):
                    bank = k % 2
                    P.op("pe", lambda e, bank=bank, t=t, k=k: e.transpose(psum[bank][:, 0:128], t.ap(k * 128, (k + 1) * 128), ident), reads=t.b() + CM, writes=[pb[bank]])
                    xa, xb = xTc(k, q * 128, (q + 1) * 128)
                    act(xa, psum[bank][:, 0:128], AF.Copy, [pb[bank]], xb)
            A.release(m)

        def final_out(dst, T):
            m = A.mark()
            ring = Ring(A, 2, 1024)
            yT = A.alloc(8 * 512)
            for b0 in range(0, T, 512):
                compute_rstd(b0, 512, 0)
                for k in range(8):
                    xa, xb = xTc(k, b0, b0 + 512)
                    stt(yT.ap(k * 512, (k + 1) * 512), xa, sm[:, 8 + k:9 + k], rstd_t.ap(0, 512), ALU.mult, ALU.mult, xb + SM + rstd_t.b(), yT.b(k * 512, (k + 1) * 512))
                for q in range(4):
                    t = ring.next()
                    for k in range(8):
                        bank = 2 + k % 2
                        P.op("pe", lambda e, bank=bank, k=k, q=q: e.transpose(psum[bank][:, 0:128], yT.ap(k * 512 + q * 128, k * 512 + (q + 1) * 128), ident),
                             reads=yT.b(k * 512 + q * 128, k * 512 + (q + 1) * 128) + CM, writes=[pb[bank]])
                        act(t.ap(k * 128, (k + 1) * 128), psum[bank][:, 0:128], AF.Copy, [pb[bank]], t.b(k * 128, (k + 1) * 128))
                    dma(dst[b0 + q * 128:b0 + (q + 1) * 128, :], t.ap(), t.b(), [], final=True, q=STQ)
            A.release(m)

        def ffn_packed(l, G):
            m = A.mark()
            c = G["c"]
            RG["wsm"] = Ring(A, 4, 1024)
            wbig = Ring(A, 2, 22 * 128)
            hs = A.alloc(8 * 512)
            actT = A.alloc(22 * 512)
            ur = Ring(A, 2, 1024)
            tr_ = Ring(A, 3, 1024)
            for u in ur.tiles:
                mset(u.ap(0, 516), 0.0, u.b(), eng="pool")
            Wup = I["ffn_up_w"][l]
            Wdn = I["ffn_down_w"][l]
            make_hs(hs, 0, 512, l, 32, 24, c, 0)
            for j in range(22):
                res = []
                for half in range(2):
                    ch = half * 22 + j
                    bank = 2 * half + (j % 2)
                    proj_fm(hs, Wup, half * FH + j * 128, 128, 0, 512, bank)
                    u = ur.next()
                    act(u.ap(1, 257), psum[bank][:, 0:256], AF.Copy, [pb[bank]], u.b())
                    act(u.ap(259, 515), psum[bank][:, 256:512], AF.Copy, [pb[bank]], u.b())
                    t = tr_.next()
                    cw3 = fcw[:, (l * 44 + ch) * 3:(l * 44 + ch) * 3 + 3]
                    conv3(t.ap(0, 514), t.b(), u.ap(0, 516), 514, cw3, fcb[:, l * 44 + ch:l * 44 + ch + 1], u.b())
                    res.append(t)
                ta, tv = res
                act(ta.ap(0, 514), ta.ap(0, 514), AF.Silu, ta.b(), ta.b())
                tt(actT.v3(2, 256, j * 512), ta.v3(2, 258)[:, :, 0:256], tv.v3(2, 258)[:, :, 0:256], ALU.mult, ta.b() + tv.b(), actT.b(j * 512, (j + 1) * 512))
            for oc in range(8):
                w = wbig.next()
                dma(w.v3(22, 128), Wdn[:, oc * 128:(oc + 1) * 128].rearrange("(k p) n -> p k n", p=128), [], w.b())
                bank = 4 + oc % 2
                for j in range(22):
                    mm(psum[bank][:, 0:512], w.v3(22, 128)[:, j, :], actT.ap(j * 512, (j + 1) * 512), j == 0, j == 21, w.b() + actT.b(j * 512, (j + 1) * 512), [pb[bank]])
                residual_update(l, 40, c, oc, 0, 512, bank)
            A.release(m)

        def ffn_layer(l, G):
            if G["packed"]:
                return ffn_packed(l, G)
            m = A.mark()
            c = G["c"]
            blocks = halo_blocks(G["nseq"], G["L"], 342)
            NB = max(b[1] for b in blocks)
            Wn = NB + 2
            RG["wsm"] = Ring(A, 4, 1024)
            wbig = Ring(A, 2, 22 * 128)
            hs = A.alloc(8 * Wn)
            actT = A.alloc(22 * NB)
            ur = Ring(A, 2, 512)
            tr_ = Ring(A, 3, 512)
            stash = A.alloc(8)
            edge = edge_halos(l, G, 32, 24, c)
            Wup = I["ffn_up_w"][l]
            Wdn = I["ffn_down_w"][l]
            for (t0, n, hl, hr) in blocks:
                c_hi = n + 2 if hr else n + 1
                make_hs(hs, t0, c_hi - 1, l, 32, 24, c, 0, col_off=1)
                if hl:
                    for k in range(8):
                        cp(hs.ap(k * Wn, k * Wn + 1), stash.ap(k, k + 1), stash.b(), hs.b(k * Wn, k * Wn + 1), eng="pool")
                else:
                    cp(hs.v3(8, Wn)[:, :, 0:1], edge.ap(0, 8).unsqueeze(2), edge.b(), hs.b(), eng="pool")
                if not hr:
                    cp(hs.v3(8, Wn)[:, :, n + 1:n + 2], edge.ap(8, 16).unsqueeze(2), edge.b(), hs.b(), eng="pool")
                c_lo, c_hi = 0, n + 2
                for k in range(8):
                    cp(stash.ap(k, k + 1), hs.ap(k * Wn + n, k * Wn + n + 1), hs.b(k * Wn + n, k * Wn + n + 1), stash.b(), eng="pool")
                for j in range(22):
                    res = []
                    for half in range(2):
                        ch = half * 22 + j
                        bank = 2 * half + (j % 2)
                        proj_fm(hs, Wup, half * FH + j * 128, 128, c_lo, c_hi, bank)
                        u = ur.next()
                        act(u.ap(c_lo, c_hi), psum[bank][:, c_lo:c_hi], AF.Copy, [pb[bank]], u.b())
                        t = tr_.next()
                        cw3 = fcw[:, (l * 44 + ch) * 3:(l * 44 + ch) * 3 + 3]
                        conv3(t.ap(0, n), t.b(), u.ap(), n, cw3, fcb[:, l * 44 + ch:l * 44 + ch + 1], u.b())
                        res.append(t)
                    ta, tv = res
                    act(ta.ap(0, n), ta.ap(0, n), AF.Silu, ta.b(), ta.b())
                    tt(actT.ap(j * NB, j * NB + n), ta.ap(0, n), tv.ap(0, n), ALU.mult, ta.b() + tv.b(), actT.b(j * NB, j * NB + n))
                for oc in range(8):
                    w = wbig.next()
                    dma(w.v3(22, 128), Wdn[:, oc * 128:(oc + 1) * 128].rearrange("(k p) n -> p k n", p=128), [], w.b())
                    bank = 4 + oc % 2
                    for j in range(22):
                        mm(psum[bank][:, 0:n], w.v3(22, 128)[:, j, :], actT.ap(j * NB, j * NB + n), j == 0, j == 21, w.b() + actT.b(j * NB, j * NB + n), [pb[bank]])
                    residual_update(l, 40, c, oc, t0, n, bank)
            A.release(m)

        def ssd_layer(l, G):
            slot = l // 2
            c = G["c"]
            T, L, nseq = G["T"], G["L"], G["nseq"]
            Win = I["ssd_in_w"][slot]
            Lf = L
            shard = G["shard"]
            NT = Lf // 128
            m0 = A.mark()
            ssq = A.alloc(nseq * NT * 4 + 8)
            m = A.mark()
            blocks = halo_blocks(nseq, L)
            if G["packed"]:
                blocks = []
                RG["wsm"] = Ring(A, 4, 1024)
                hs = A.alloc(8 * 512)
                ur = Ring(A, 2, 1024)
                tr_ = Ring(A, 3, 1024)
                for u in ur.tiles:
                    mset(u.ap(0, 516), 0.0, u.b(), eng="pool")
                make_hs(hs, 0, 512, l, 8, 0, c, 0)
                for j in range(41):
                    bank = 2 + j % 2
                    ncol = 128 if j < 40 else 64
                    proj_fm(hs, Win, j * 128, ncol, 0, 512, bank)
                    t = tr_.next()
                    if j < 16:
                        act(t.ap(0, 512), psum[bank][:, 0:512], AF.Silu, [pb[bank]], t.b())
                        dma(S_proj[j * 128:(j + 1) * 128, 0:512], t.ap(0, 512), t.b(), [db("proj", j, 0)], q=STQ)
                    elif j < 40:
                        u = ur.next()
                        act(u.ap(1, 257), psum[bank][:, 0:256], AF.Copy, [pb[bank]], u.b())
                        act(u.ap(259, 515), psum[bank][:, 256:512], AF.Copy, [pb[bank]], u.b())
                        ch = slot * 24 + (j - 16)
                        conv3(t.ap(0, 514), t.b(), u.ap(0, 516), 514, scw[:, ch * 3:ch * 3 + 3], scb[:, ch:ch + 1], u.b())
                        act(t.ap(0, 514), t.ap(0, 514), AF.Silu, t.b(), t.b())
                        dma(S_proj[j * 128:(j + 1) * 128, 0:256], t.ap(0, 256), t.b(), [db("proj", j, 0)], q=STQ)
                        dma(S_proj[j * 128:(j + 1) * 128, 256:512], t.ap(258, 514), t.b(), [db("proj", j, 0)], q=STQ)
                    else:
                        tp = t.arena.t[0:64, t.off:t.off + 512]
                        act(tp, psum[bank][0:64, 0:512], AF.Exp, [pb[bank]] + SM, t.b(), bias=sm[0:64, 18 + slot:19 + slot])
                        act(tp, tp, AF.Ln, t.b() + SM, t.b(), bias=sm[0:64, 1:2])
                        t2 = tr_.next()
                        tp2 = t2.arena.t[0:64, t2.off:t2.off + 512]
                        ts(tp2, tp, sm[0:64, 20 + slot:21 + slot], None, ALU.mult, None, t.b() + SM, t2.b())
                        dma(S_proj[5184:5248, 0:512], tp2, t2.b(), [db("proj", 41, 0)], q=STQ)
                        dma(S_proj[5120:5184, 0:512], tp, t.b(), [db("proj", 40, 0)], q=STQ)
            else:
                NB = max(b[1] for b in blocks)
                Wn = NB + 2
                RG["wsm"] = Ring(A, 4, 1024)
                hs = A.alloc(8 * Wn)
                ur = Ring(A, 3, 512)
                tr_ = Ring(A, 3, 512)
                edge = edge_halos(l, G, 8, 0, c)
            for bi, (t0, n, hl, hr) in enumerate(blocks):
                c_lo = 0 if hl else 1
                c_hi = n + 2 if hr else n + 1
                lo_tok = t0 - 1 + c_lo
                make_hs(hs, lo_tok, c_hi - c_lo, l, 8, 0, c, 0, col_off=c_lo)
                if not hl:
                    cp(hs.v3(8, Wn)[:, :, 0:1], edge.ap(0, 8).unsqueeze(2), edge.b(), hs.b(), eng="pool")
                if not hr:
                    cp(hs.v3(8, Wn)[:, :, n + 1:n + 2], edge.ap(8, 16).unsqueeze(2), edge.b(), hs.b(), eng="pool")
                c_lo, c_hi = 0, n + 2
                for j in range(41):
                    bank = 2 + j % 2
                    ncol = 128 if j < 40 else 64
                    proj_fm(hs, Win, j * 128, ncol, c_lo, c_hi, bank)
                    t = tr_.next()
                    if j < 16:
                        act(t.ap(0, n), psum[bank][:, 1:n + 1], AF.Silu, [pb[bank]], t.b())
                    elif j < 40:
                        u = ur.next()
                        act(u.ap(c_lo, c_hi), psum[bank][:, c_lo:c_hi], AF.Copy, [pb[bank]], u.b())
                        ch = slot * 24 + (j - 16)
                        conv3(t.ap(0, n), t.b(), u.ap(), n, scw[:, ch * 3:ch * 3 + 3], scb[:, ch:ch + 1], u.b())
                        act(t.ap(0, n), t.ap(0, n), AF.Silu, t.b(), t.b())
                    else:
                        tp = t.arena.t[0:64, t.off:t.off + n]
                        act(tp, psum[bank][0:64, 1:n + 1], AF.Exp, [pb[bank]] + SM, t.b(), bias=sm[0:64, 18 + slot:19 + slot])
                        act(tp, tp, AF.Ln, t.b() + SM, t.b(), bias=sm[0:64, 1:2])
                        t2 = tr_.next()
                        tp2 = t2.arena.t[0:64, t2.off:t2.off + n]
                        ts(tp2, tp, sm[0:64, 20 + slot:21 + slot], None, ALU.mult, None, t.b() + SM, t2.b())
                        dma(S_proj[5184:5248, t0:t0 + n], tp2, t2.b(), [db("proj", 41, bi)], q=STQ)
                    if j < 40:
                        dma(S_proj[j * 128:(j + 1) * 128, t0:t0 + n], t.ap(0, n), t.b(), [db("proj", j, bi)], q=STQ)
                    else:
                        dma(S_proj[5120:5184, t0:t0 + n], t.arena.t[0:64, t.off:t.off + n], t.b(), [db("proj", 40, bi)], q=STQ)
            A.release(m)
            nblk = max(1, len(blocks))

            def pj(j):
                return [db("proj", j, bi) for bi in range(nblk)]

            def psrc(row0, nrows, tb_, tok0, ntok, j):
                return S_proj[row0:row0 + nrows, tb_ + tok0:tb_ + tok0 + ntok], pj(j)

            m = A.mark()
            BT = A.alloc(Lf); CT = A.alloc(Lf)
            Btok = A.alloc(Lf)
            dtok = A.alloc(NT * 64)
            datok = A.alloc(NT * 64)
            LN = [dict(xtok=A.alloc(Lf), ytok=A.alloc(Lf), h=A.alloc(128), X=2 + 3 * i, Y=3 + 3 * i, Z=4 + 3 * i) for i in range(2)]
            wk = Ring(A, 8, 512)
            sm2 = Ring(A, 10, 128)
            ld = Ring(A, 3, 512)
            PC = min(512, Lf)
            mset(ssq.ap(), 0.0, ssq.b(), eng="pool")

            def prep(ln, d, s, tb, hp):
                xtok, ytok, h = ln["xtok"], ln["ytok"], ln["h"]
                HB = h.b()
                for p0 in range(0, Lf, PC):
                    lt = ld.next()
                    sa_, sb_ = psrc(2048 + hp * 128, 128, tb, p0, PC, 16 + hp)
                    dma(lt.ap(0, PC), sa_, sb_, lt.b())
                    for qq in range(PC // 128):
                        q = p0 // 128 + qq
                        P.op("pe", lambda e, lt=lt, qq=qq: e.transpose(psum[1][:, 0:128], lt.ap(qq * 128, (qq + 1) * 128), ident), reads=lt.b() + CM, writes=[pb[1]])
                        cp(xtok.ap(q * 128, (q + 1) * 128), psum[1][:, 0:128], [pb[1]], xtok.b(q * 128, (q + 1) * 128))
                        if d == 0:
                            tt(ytok.v3(2, 64, q * 128), xtok.v3(2, 64, q * 128), sm[:, 64 + slot * 32 + 2 * hp:64 + slot * 32 + 2 * hp + 2].unsqueeze(2).broadcast_to([128, 2, 64]),
                               ALU.mult, xtok.b(q * 128, (q + 1) * 128) + SM, ytok.b(q * 128, (q + 1) * 128), eng="pool")
                if d == 1:
                    for q in range(NT):
                        dma(ytok.ap(q * 128, (q + 1) * 128), S_y[tb + q * 128:tb + (q + 1) * 128, hp * 128:(hp + 1) * 128], [db("y", s, q, hp)], ytok.b(q * 128, (q + 1) * 128))
                if G["sample"]:
                    st_in = wk.next()
                    if d == 0:
                        dma(st_in.ap(0, 128), I["state"][slot, d, hp], [], st_in.b())
                        P.op("pe", lambda e, st_in=st_in: e.transpose(psum[1][:, 0:128], st_in.ap(0, 128), ident), reads=st_in.b() + CM, writes=[pb[1]])
                        cp(h.ap(), psum[1][:, 0:128], [pb[1]], HB)
                    else:
                        st2 = wk.next()
                        dma(st_in.ap(0, 128), S_hstG[hp * 128:(hp + 1) * 128, :], [db("hstG")], st_in.b())
                        dma(st2.ap(0, 128), S_hstG[2048 + hp * 128:2048 + (hp + 1) * 128, :], [db("hstG")], st2.b())
                        ts(st_in.ap(0, 128), st_in.ap(0, 128), mB, None, ALU.mult, None, st_in.b() + SM, st_in.b())
                        stt(h.ap(), st2.ap(0, 128), mA, st_in.ap(0, 128), ALU.mult, ALU.add, st2.b() + SM + st_in.b(), HB)
                else:
                    mset(h.ap(), 0.0, HB, eng="pool")

            def step(ln, d, hp, q):
                xtok, ytok, h = ln["xtok"], ln["ytok"], ln["h"]
                X, Y, Z = ln["X"], ln["Y"], ln["Z"]
                HB = h.b()
                hc0 = d * 32 + 2 * hp
                mR = mT if d == 0 else mL
                mLh = mLs if d == 0 else mTs
                da = datok.v3(NT, 64)[:, q, hc0:hc0 + 2]
                dtq = dtok.v3(NT, 64)[:, q, hc0:hc0 + 2]
                R = wk.next()
                tt(R.v3(2, 128), mR.unsqueeze(1).broadcast_to([128, 2, 128]), da.unsqueeze(2).broadcast_to([128, 2, 128]), ALU.mult, CM + datok.b(), R.b(), eng="pool")
                mm(psum[X][:, 0:256], mLh, R.ap(0, 256), True, True, CM + R.b(), [pb[X]])
                mm(psum[X][:, 256:258], mR, da, True, True, CM + datok.b(), [pb[X]])
                mm(psum[X][:, 264:266], ones, da, True, True, CM + datok.b(), [pb[X]])
                mm(psum[Y][:, 0:128], BT.ap(q * 128, (q + 1) * 128), CT.ap(q * 128, (q + 1) * 128), True, True, BT.b(q * 128, (q + 1) * 128) + CT.b(q * 128, (q + 1) * 128), [pb[Y]])
                E = wk.next()
                act(E.ap(0, 256), psum[X][:, 0:256], AF.Exp, [pb[X]], E.b())
                ea = sm2.next()
                act(ea.ap(0, 10), psum[X][:, 256:266], AF.Exp, [pb[X]], ea.b())
                cb = sm2.next()
                tt(cb.ap(), psum[Y][:, 0:128], mR, ALU.mult, [pb[Y]] + CM, cb.b())
                xd = wk.next()
                tt(xd.v3(2, 64), xtok.v3(2, 64, q * 128), dtq.unsqueeze(2).broadcast_to([128, 2, 64]), ALU.mult, xtok.b(q * 128, (q + 1) * 128) + dtok.b(), xd.b(), eng="pool")
                lastc = 127 if d == 0 else 0
                tt(xd.v3(2, 64, 128), xd.v3(2, 64), E.v3(2, 128)[:, :, lastc:lastc + 1].broadcast_to([128, 2, 64]), ALU.mult, xd.b() + E.b(), xd.b(), eng="pool")
                tt(E.v3(2, 128), E.v3(2, 128), cb.ap().unsqueeze(1).broadcast_to([128, 2, 128]), ALU.mult, E.b() + cb.b(), E.b())
                for hh in range(2):
                    mm(psum[Z][:, hh * 64:(hh + 1) * 64], E.v3(2, 128)[:, hh, :], xd.ap(hh * 64, (hh + 1) * 64), True, True, E.b() + xd.b(), [pb[Z]])
                mm(psum[Z][:, 128:256], CT.ap(q * 128, (q + 1) * 128), h.ap(), True, True, CT.b(q * 128, (q + 1) * 128) + HB, [pb[Z]])
                mm(psum[Z][:, 256:384], Btok.ap(q * 128, (q + 1) * 128), xd.ap(128, 256), True, True, Btok.b(q * 128, (q + 1) * 128) + xd.b(), [pb[Z]])
                yb = ytok.b(q * 128, (q + 1) * 128)
                tmp = sm2.next()
                tt(tmp.v3(2, 64), psum[Z][:, 128:256].rearrange("p (a b) -> p a b", a=2, b=64), ea.ap(0, 2).unsqueeze(2).broadcast_to([128, 2, 64]), ALU.mult, [pb[Z]] + ea.b(), tmp.b())
                tt(tmp.ap(), tmp.ap(), psum[Z][:, 0:128], ALU.add, tmp.b() + [pb[Z]], tmp.b())
                tt(h.v3(2, 64), h.v3(2, 64), ea.ap(8, 10).unsqueeze(2).broadcast_to([128, 2, 64]), ALU.mult, HB + ea.b(), HB)
                tt(h.ap(), h.ap(), psum[Z][:, 256:384], ALU.add, HB + [pb[Z]], HB)
                tt(ytok.ap(q * 128, (q + 1) * 128), ytok.ap(q * 128, (q + 1) * 128), tmp.ap(), ALU.add, yb + tmp.b(), yb, eng="pool")

            def fin(ln, d, s, tb, hp):
                xtok, ytok, h = ln["xtok"], ln["ytok"], ln["h"]
                HB = h.b()
                g = hp // 4
                if not G["sample"]:
                    P.op("pe", lambda e, h=h: e.transpose(psum[1][:, 0:128], h.ap(), ident), reads=HB + CM, writes=[pb[1]])
                    so = wk.next()
                    cp(so.ap(0, 128), psum[1][:, 0:128], [pb[1]], so.b())
                    dma(O["nstate"][s, slot, d, hp], so.ap(0, 128), so.b(), [], final=True, q=STQ)
                elif d == 0:
                    dma(S_hst[hp * 128:(hp + 1) * 128, :], h.ap(), HB, [db("hst", hp)], q=STQ)
                if d == 1:
                    for p0 in range(0, Lf, PC):
                        lt = ld.next()
                        sa_, sb_ = psrc(hp * 128, 128, tb, p0, PC, hp)
                        dma(lt.ap(0, PC), sa_, sb_, lt.b())
                        for qq in range(PC // 128):
                            q = p0 // 128 + qq
                            P.op("pe", lambda e, lt=lt, qq=qq: e.transpose(psum[0][:, 0:128], lt.ap(qq * 128, (qq + 1) * 128), ident), reads=lt.b() + CM, writes=[pb[0]])
                            yb = ytok.b(q * 128, (q + 1) * 128)
                            tt(ytok.ap(q * 128, (q + 1) * 128), ytok.ap(q * 128, (q + 1) * 128), psum[0][:, 0:128], ALU.mult, yb + [pb[0]], yb)
                            sq_ = sm2.next()
                            sc_ = nseq * NT * 4 + (hp % 2)
                            P.op("act", lambda e, sq_=sq_, q=q, sc_=sc_, ytok=ytok: e.activation(out=sq_.ap(), in_=ytok.ap(q * 128, (q + 1) * 128), func=AF.Square, accum_out=ssq.ap(sc_, sc_ + 1)),
                                 reads=yb, writes=sq_.b() + ssq.b())
                            si = (s * NT + q) * 4 + g
                            tt(ssq.ap(si, si + 1), ssq.ap(si, si + 1), ssq.ap(sc_, sc_ + 1), ALU.add, ssq.b(), ssq.b())
                for q in range(NT):
                    dma(S_y[tb + q * 128:tb + (q + 1) * 128, hp * 128:(hp + 1) * 128], ytok.ap(q * 128, (q + 1) * 128), ytok.b(q * 128, (q + 1) * 128), [db("y", s, q, hp)], q=STQ)

            for d in range(2):
                if d == 1 and shard:
                    allgather(S_hst, S_hstG, [db("hst", hp_) for hp_ in range(16)], [db("hstG")])
                for s in range(nseq):
                    tb = s * Lf
                    if d == 0 or nseq > 1:
                        for srow, dst_ in ((5120, dtok), (5184, datok)):
                            for p0 in range(0, Lf, PC):
                                lt = ld.next()
                                sa_, sb_ = psrc(srow, 64, tb, p0, PC, 40)
                                dma(lt.arena.t[0:64, lt.off:lt.off + PC], sa_, sb_ + pj(41), lt.b())
                                for qq in range(PC // 128):
                                    q = p0 // 128 + qq
                                    P.op("pe", lambda e, lt=lt, qq=qq: e.transpose(psum[0][:, 0:64], lt.arena.t[0:64, lt.off + qq * 128:lt.off + (qq + 1) * 128], ident[0:64, 0:64]),
                                         reads=lt.b() + CM, writes=[pb[0]])
                                    cp(dst_.ap(q * 64, (q + 1) * 64), psum[0][:, 0:64], [pb[0]], dst_.b())
                    order = list(range(NT)) if d == 0 else list(range(NT - 1, -1, -1))
                    for hp0 in range(0, 16, 2):
                        g = hp0 // 4
                        if hp0 % 4 == 0:
                            sa_, sb_ = psrc(4096 + g * 128, 128, tb, 0, Lf, 32 + g)
                            dma(BT.ap(), sa_, sb_, BT.b())
                            sa_, sb_ = psrc(4608 + g * 128, 128, tb, 0, Lf, 36 + g)
                            dma(CT.ap(), sa_, sb_, CT.b())
                            for q in range(NT):
                                P.op("pe", lambda e, q=q: e.transpose(psum[1][:, 0:128], BT.ap(q * 128, (q + 1) * 128), ident), reads=BT.b(q * 128, (q + 1) * 128) + CM, writes=[pb[1]])
                                cp(Btok.ap(q * 128, (q + 1) * 128), psum[1][:, 0:128], [pb[1]], Btok.b(q * 128, (q + 1) * 128))
                        for i in range(2):
                            prep(LN[i], d, s, tb, hp0 + i)
                        for q in order:
                            for i in range(2):
                                step(LN[i], d, hp0 + i, q)
                        for i in range(2):
                            fin(LN[i], d, s, tb, hp0 + i)
            A.release(m)
            m3 = A.mark()
            NTo = NT
            wbig = Ring(A, 2, 16 * 128)
            yr = Ring(A, 4, 2048)
            yTt = A.alloc(16 * 512)
            rs = A.alloc(nseq * NT * 4)
            act(rs.ap(), ssq.ap(0, nseq * NT * 4), AF.Sqrt, ssq.b() + SM, rs.b(), bias=epsc, scale=1.0 / 512)
            recip(rs.ap(), rs.ap(), rs.b(), rs.b())
            Wo = I["ssd_out_w"][slot]
            NTall = nseq * NT
            TBk = min(4, NTall)
            NW = TBk * 128
            for s in range(1):
                tb = 0
                for q0 in range(0, NTall, TBk):
                    for qq in range(TBk):
                        q = q0 + qq
                        yt = yr.next()
                        dma(yt.ap(), S_y[q * 128:(q + 1) * 128, :], [db("y", q // NT, q % NT, hp_) for hp_ in range(16)], yt.b())
                        for g in range(4):
                            si = q * 4 + g
                            ts(yt.ap(g * 512, (g + 1) * 512), yt.ap(g * 512, (g + 1) * 512), rs.ap(si, si + 1), None, ALU.mult, None, yt.b(g * 512, (g + 1) * 512) + rs.b(), yt.b(g * 512, (g + 1) * 512))
                        for kc in range(16):
                            bank = kc % 2
                            P.op("pe", lambda e, yt=yt, kc=kc, bank=bank: e.transpose(psum[bank][:, 0:128], yt.ap(kc * 128, (kc + 1) * 128), ident), reads=yt.b(kc * 128, (kc + 1) * 128) + CM, writes=[pb[bank]])
                            dst_lo = kc * NW + qq * 128
                            if kc % 2 == 0:
                                ts(yTt.ap(dst_lo, dst_lo + 128), psum[bank][:, 0:128], sng[:, slot * 16 + kc:slot * 16 + kc + 1], None, ALU.mult, None, [pb[bank]] + VB, yTt.b(dst_lo, dst_lo + 128))
                            else:
                                act(yTt.ap(dst_lo, dst_lo + 128), psum[bank][:, 0:128], AF.Copy, [pb[bank]] + VB, yTt.b(dst_lo, dst_lo + 128), scale=sng[:, slot * 16 + kc:slot * 16 + kc + 1])
                    for oc in range(8):
                        w = wbig.next()
                        dma(w.v3(16, 128), Wo[:, oc * 128:(oc + 1) * 128].rearrange("(k p) n -> p k n", p=128), [], w.b())
                        bank = 2 + oc % 2
                        for kc in range(16):
                            mm(psum[bank][:, 0:NW], w.v3(16, 128)[:, kc, :], yTt.ap(kc * NW, (kc + 1) * NW), kc == 0, kc == 15, w.b() + yTt.b(kc * NW, (kc + 1) * NW), [pb[bank]])
                        residual_update(l, 16, c, oc, tb + q0 * 128, NW, bank)
            A.release(m3)
            A.release(m0)

        def attn_layer(l, G):
            slot = l // 2
            c = G["c"]
            T, L, nseq = G["T"], G["L"], G["nseq"]
            sample = G["sample"]
            Wq = I["att_qkv_w"][slot]
            m = A.mark()
            RG["wsm"] = Ring(A, 4, 1024)
            wbig = Ring(A, 2, 8 * 256)
            hs = A.alloc(8 * 512)
            tr_ = Ring(A, 3, 512)
            rc = A.alloc(512); rsn = A.alloc(512)
            for bi, b0 in enumerate(range(0, T, 512)):
                make_hs(hs, b0, 512, l, 8, 0, c, 0)
                if sample:
                    dma(rc.ap(), I["ropeC"][:, b0:b0 + 512], [], rc.b())
                    dma(rsn.ap(), I["ropeS"][:, b0:b0 + 512], [], rsn.b())
                for j in range(16):
                    bank = 2 + j % 2
                    proj_fm(hs, Wq, j * 128, 128, 0, 512, bank)
                    t = tr_.next()
                    act(t.ap(), psum[bank][:, :], AF.Copy, [pb[bank]], t.b())
                    if sample:
                        mm(psum[4 + j % 2][:, :], prot, t.ap(), True, True, CM + t.b(), [pb[4 + j % 2]])
                        t2 = tr_.next()
                        tt(t2.ap(), psum[4 + j % 2][:, :], rsn.ap(), ALU.mult, [pb[4 + j % 2]] + rsn.b(), t2.b())
                        tt(t.ap(), t.ap(), rc.ap(), ALU.mult, t.b() + rc.b(), t.b())
                        tt(t.ap(), t.ap(), t2.ap(), ALU.add, t.b() + t2.b(), t.b())
                    if j < 8:
                        dma(S_q[j * 128:(j + 1) * 128, b0:b0 + 512], t.ap(), t.b(), [db("qk", j, bi)], q=STQ)
                    else:
                        dma(S_kv[(j - 8) * 128:(j - 7) * 128, b0:b0 + 512], t.ap(), t.b(), [db("qk", j, bi)], q=STQ)
                for part in (range(4, 12) if not sample else range(8, 12)):
                    w = wbig.next()
                    dma(w.v3(8, 256), Wq[:, part * 256:(part + 1) * 256].rearrange("(k p) n -> p k n", p=128), [], w.b())
                    for q in range(4):
                        bank = 6 + q % 2
                        for k in range(8):
                            mm(psum[bank][:, 0:256], hs.ap(k * 512 + q * 128, k * 512 + (q + 1) * 128), w.v3(8, 256)[:, k, :], k == 0, k == 7, hs.b(k * 512 + q * 128, k * 512 + (q + 1) * 128) + w.b(), [pb[bank]])
                        t = tr_.next()
                        act(t.ap(0, 256), psum[bank][:, 0:256], AF.Copy, [pb[bank]], t.b())
                        tok = b0 + q * 128
                        if sample:
                            dma(S_kv[D + tok:D + tok + 128, (part - 8) * 256:(part - 7) * 256], t.ap(0, 256), t.b(), [db("v", tok // 128)], q=STQ)
                        else:
                            s_, tq = tok // LP, tok % LP
                            if part < 8:
                                dma(O["nk"][s_, slot, tq:tq + 128, (part - 4) * 256:(part - 3) * 256], t.ap(0, 256), t.b(), [], final=True, q=STQ)
                            else:
                                dma(O["nv"][s_, slot, tq:tq + 128, (part - 8) * 256:(part - 7) * 256], t.ap(0, 256), t.b(), [db("v", tok // 128)], final=True, q=STQ)
            A.release(m)
            nblk = T // 512
            if G["shard"]:
                allgather_rows(S_kv, S_kvG, 2 * D, lambda b0, b1: ([db("qk", 8 + j, bi) for j in range(b0 // 128, b1 // 128) for bi in range(nblk)] if b0 < D
                                                                     else [db("v", q) for q in range((b0 - D) // 128, (b1 - D) // 128)]), "kvG")
            m = A.mark()
            NKT = (G["Lf"] + (256 if sample else 0)) // 128
            LK = NKT * 128
            QB = min(512, L)
            qT = A.alloc(L); kT = A.alloc(LK)
            vv = A.alloc(NKT * 128)
            pTr = Ring(A, 4, 512)
            sacc = [Ring(A, 2, 512), Ring(A, 2, 512)]
            rr = Ring(A, 4, 512)
            ot = Ring(A, 2, 512)
            ck_t = Ring(A, 2, 128)
            neglam = sm[:, 24 + slot:25 + slot]
            subs = sm[:, 26 + slot:27 + slot]
            vva = vv.v3(NKT, 128)
            qbi = 0
            for s in range(nseq):
                tb = s * L
                for hd in range(8):
                    kb = 0
                    dma(qT.ap(), S_q[hd * 128:(hd + 1) * 128, tb:tb + L], [db("qk", hd, bi) for bi in range(nblk)], qT.b())
                    if sample:
                        kb = 256
                        for q in range(2):
                            ct = ck_t.next()
                            dma(ct.ap(0, 128), I["ck"][slot, q * 128:(q + 1) * 128, hd * 128:(hd + 1) * 128], [], ct.b())
                            P.op("pe", lambda e, ct=ct: e.transpose(psum[1][:, 0:128], ct.ap(0, 128), ident), reads=ct.b() + CM, writes=[pb[1]])
                            cp(kT.ap(q * 128, (q + 1) * 128), psum[1][:, 0:128], [pb[1]], kT.b(q * 128, (q + 1) * 128))
                            dma(vva[:, q, :], I["cv"][slot, q * 128:(q + 1) * 128, hd * 128:(hd + 1) * 128], [], vv.b(q * 128, (q + 1) * 128))
                    if sample:
                        for r_ in range(2):
                            k0_ = kb + r_ * LH
                            ga, gb = gathered(S_kvG, 2 * D, "kvG", r_, hd * 128, 128)
                            dma(kT.ap(k0_, k0_ + LH), ga, gb, kT.b(k0_, k0_ + LH))
                            for q in range(LH // 128):
                                kq = k0_ // 128 + q
                                ga, gb = gathered(S_kvG, 2 * D, "kvG", r_, D + q * 128, 128)
                                dma(vva[:, kq, :], ga[:, hd * 128:(hd + 1) * 128], gb, vv.b(kq * 128, (kq + 1) * 128))
                    else:
                        dma(kT.ap(kb, kb + L), S_kv[hd * 128:(hd + 1) * 128, tb:tb + L], [db("qk", 8 + hd, bi) for bi in range(nblk)], kT.b(kb, kb + L))
                        for q in range(L // 128):
                            tok = tb + q * 128
                            src = O["nv"][tok // LP, slot, tok % LP:tok % LP + 128, hd * 128:(hd + 1) * 128]
                            kq = kb // 128 + q
                            dma(vva[:, kq, :], src, [db("v", tok // 128)], vv.b(kq * 128, (kq + 1) * 128))
                    for qb0 in range(0, L, QB):
                        par = qbi % 2
                        qbi += 1
                        sa = [sacc[0].next(), sacc[1].next()]
                        accb = [4 + 2 * par, 5 + 2 * par]

                        def score(kt, mp):
                            sb = 2 * (kt % 2) + mp
                            mm(psum[sb][:, 0:QB], kT.arena.t[mp * 64:(mp + 1) * 64, kT.off + kt * 128:kT.off + (kt + 1) * 128],
                               qT.arena.t[mp * 64:(mp + 1) * 64, qT.off + qb0:qT.off + qb0 + QB],
                               True, True, kT.b(kt * 128, (kt + 1) * 128) + qT.b(qb0, qb0 + QB), [pb[sb]])

                        score(0, 0); score(0, 1)
                        for kt in range(NKT):
                            pts = []
                            for mp in range(2):
                                sb = 2 * (kt % 2) + mp
                                pt = pTr.next()
                                act(pt.ap(0, QB), psum[sb][:, 0:QB], AF.Exp, [pb[sb]], pt.b(), scale=0.125)
                                pts.append(pt)
                            if kt + 1 < NKT:
                                score(kt + 1, 0); score(kt + 1, 1)
                            for mp in range(2):
                                pt = pts[mp]
                                mm(psum[accb[mp]][:, 0:QB], vva[:, kt, :], pt.ap(0, QB), kt == 0, kt == NKT - 1, vv.b(kt * 128, (kt + 1) * 128) + pt.b(), [pb[accb[mp]]])
                                if kt == 0:
                                    cp(sa[mp].ap(0, QB), pt.ap(0, QB), pt.b(), sa[mp].b(), eng="pool")
                                else:
                                    tt(sa[mp].ap(0, QB), sa[mp].ap(0, QB), pt.ap(0, QB), ALU.add, sa[mp].b() + pt.b(), sa[mp].b(), eng="pool")
                        rc_ = []
                        for mp in range(2):
                            mm(psum[mp][:, 0:QB], ones, sa[mp].ap(0, QB), True, True, CM + sa[mp].b(), [pb[mp]])
                            r = rr.next()
                            recip(r.ap(0, QB), psum[mp][:, 0:QB], [pb[mp]], r.b())
                            rc_.append(r)
                        ts(rc_[1].ap(0, QB), rc_[1].ap(0, QB), neglam, None, ALU.mult, None, rc_[1].b() + SM, rc_[1].b())
                        o = ot.next()
                        tt(o.ap(0, QB), psum[accb[0]][:, 0:QB], rc_[0].ap(0, QB), ALU.mult, [pb[accb[0]]] + rc_[0].b(), o.b())
                        tt(rc_[1].ap(0, QB), psum[accb[1]][:, 0:QB], rc_[1].ap(0, QB), ALU.mult, [pb[accb[1]]] + rc_[1].b(), rc_[1].b())
                        tt(o.ap(0, QB), o.ap(0, QB), rc_[1].ap(0, QB), ALU.add, o.b() + rc_[1].b(), o.b())
                        sq_ = rr.next()
                        act(sq_.ap(0, QB), o.ap(0, QB), AF.Square, o.b(), sq_.b())
                        mm(psum[0][:, 0:QB], ones, sq_.ap(0, QB), True, True, CM + sq_.b(), [pb[0]])
                        act(sq_.ap(0, QB), psum[0][:, 0:QB], AF.Sqrt, [pb[0]] + SM, sq_.b(), bias=epsc, scale=1.0 / 128)
                        recip(sq_.ap(0, QB), sq_.ap(0, QB), sq_.b(), sq_.b())
                        stt(o.ap(0, QB), o.ap(0, QB), subs, sq_.ap(0, QB), ALU.mult, ALU.mult, o.b() + SM + sq_.b(), o.b())
                        for b5 in range(qb0, qb0 + QB, 128):
                            pass
                        dma(S_o[hd * 128:(hd + 1) * 128, tb + qb0:tb + qb0 + QB], o.ap(0, QB), o.b(), [db("o", hd, (tb + qb0) // 512, (tb + qb0) % 512)], q=STQ)
            A.release(m)
            m = A.mark()
            RG["wsm"] = Ring(A, 4, 1024)
            oTb = A.alloc(8 * 512)
            Wo = I["att_out_w"][slot]
            for bi, b0 in enumerate(range(0, T, 512)):
                for hd in range(8):
                    dma(oTb.ap(hd * 512, (hd + 1) * 512), S_o[hd * 128:(hd + 1) * 128, b0:b0 + 512], [db("o", hd, bi, off_) for off_ in range(0, 512, min(512, L))], oTb.b(hd * 512, (hd + 1) * 512))
                for oc in range(8):
                    w = load_wcols(Wo, oc * 128, 128)
                    bank = 2 + oc % 2
                    for k in range(8):
                        mm(psum[bank][:, :], w.v3(8, 128)[:, k, :], oTb.ap(k * 512, (k + 1) * 512), k == 0, k == 7, w.b() + oTb.b(k * 512, (k + 1) * 512), [pb[bank]])
                    residual_update(l, 16, c, oc, b0, 512, bank)
            A.release(m)

        groups = []
        if do_sample:
            groups.append(dict(sample=True, shard=True, packed=False, c=0, T=LH, L=LH, Lf=LS, nseq=1, src=I["xs_in"], dst=O["y_s"]))
        if do_prompt:
            groups.append(dict(sample=False, shard=False, packed=True, c=1, T=2 * LP, L=LP, Lf=LP, nseq=2, src=I["xp_in"], dst=O["y_p"]))
        for G in groups:
            load_x(G["src"], G["T"])
            for l in range(depth_run):
                if l % 2 == 0:
                    ssd_layer(l, G)
                else:
                    attn_layer(l, G)
                ffn_layer(l, G)
            final_out(G["dst"], G["T"])
        P.emit(out_events)
    return nc


def _fm(v, nchunk):
    return np.ascontiguousarray(np.asarray(v, np.float32).reshape(nchunk, 128).T)


def _consts():
    a = np.arange(128)
    ident = np.eye(128, dtype=np.float32)
    ones = np.ones((128, 128), np.float32)
    mT = (a[:, None] <= a[None, :]).astype(np.float32)
    mL = (a[:, None] >= a[None, :]).astype(np.float32)
    mTs = (a[:, None] < a[None, :]).astype(np.float32)
    mLs = (a[:, None] > a[None, :]).astype(np.float32)
    i = a % 32
    partner = np.where(i < 16, a + 16, a - 16)
    prot = np.zeros((128, 128), np.float32)
    prot[partner, a] = 1.0
    cmat = np.stack([ident, ones, mT, mL, mTs, mLs, prot], axis=1).reshape(128, 7 * 128)
    t = np.arange(LS)
    row = (t // 64).astype(np.float32)
    col = (t % 64).astype(np.float32)
    inv = (1.0 / (np.float32(10000.0) ** (np.arange(0, 32, 2, dtype=np.float32) / np.float32(32)))).astype(np.float32)
    dd = a % 64
    axis_col = dd >= 32
    f = i % 16
    pos = np.where(axis_col[:, None], col[None, :], row[None, :]).astype(np.float32)
    ang = (pos * inv[f][:, None]).astype(np.float32)
    C = np.cos(ang).astype(np.float32)
    S = np.sin(ang).astype(np.float32)
    S = np.where((i < 16)[:, None], -S, S).astype(np.float32)
    return np.ascontiguousarray(cmat), np.ascontiguousarray(C), np.ascontiguousarray(S)


_CACHE = {}


def kernel(**inp):
    f = lambda k: np.asarray(inp[k], np.float32)
    key = (DEPTH_RUN, DO_SAMPLE, DO_PROMPT)
    if key not in _CACHE:
        _CACHE[key] = build_program(DEPTH_RUN, DO_SAMPLE, DO_PROMPT)
    nc = _CACHE[key]
    cmat, rC, rS = _consts()
    shared = {
        "mod_w": f("mod_w"),
        "mod_bT": np.ascontiguousarray(np.concatenate([_fm(f("mod_b")[l], 48) for l in range(4)], axis=1)),
        "nmg": np.ascontiguousarray(np.concatenate([_fm(f("norm_mix_g")[l], 8) for l in range(4)], axis=1)),
        "nfg": np.ascontiguousarray(np.concatenate([_fm(f("norm_ffn_g")[l], 8) for l in range(4)], axis=1)),
        "fng": _fm(f("final_norm_g"), 8),
        "ssd_in_w": f("ssd_in_w"),
        "ssd_cw": np.ascontiguousarray(f("ssd_conv_w").reshape(2, 3, 24, 128).transpose(3, 0, 2, 1).reshape(128, 2 * 24 * 3)),
        "ssd_cb": np.ascontiguousarray(f("ssd_conv_b").reshape(2, 24, 128).transpose(2, 0, 1).reshape(128, 48)),
        "dtb": np.ascontiguousarray(f("ssd_dt_bias").reshape(2, 64).T),
        "alog": np.ascontiguousarray(f("ssd_a_log").reshape(2, 64).T),
        "dsk": np.ascontiguousarray(np.broadcast_to(f("ssd_d").reshape(1, 64), (128, 64))),
        "sng": np.ascontiguousarray(np.concatenate([_fm(f("ssd_norm_g")[s], 16) for s in range(2)], axis=1)),
        "ssd_out_w": f("ssd_out_w"),
        "att_qkv_w": f("att_qkv_w"),
        "lamb": np.ascontiguousarray(np.broadcast_to(f("att_lambda").reshape(1, 512), (128, 512))),
        "sub": np.ascontiguousarray(f("att_subln_g").T),
        "att_out_w": f("att_out_w"),
        "ffn_up_w": f("ffn_up_w"),
        "fcw": np.ascontiguousarray(f("ffn_conv_w").reshape(4, 3, 44, 128).transpose(3, 0, 2, 1).reshape(128, 4 * 44 * 3)),
        "fcb": np.ascontiguousarray(f("ffn_conv_b").reshape(4, 44, 128).transpose(2, 0, 1).reshape(128, 4 * 44)),
        "ffn_down_w": f("ffn_down_w"),
        "cmat": cmat,
    }
    def dirswap(w):
        w2 = w.copy()
        w2[..., 5120:5152] = w[..., 5152:5184]
        w2[..., 5152:5184] = w[..., 5120:5152]
        return w2
    shared_odd = dict(shared)
    shared_odd["ssd_in_w"] = dirswap(f("ssd_in_w"))
    shared_odd["dtb"] = np.ascontiguousarray(f("ssd_dt_bias")[:, ::-1].reshape(2, 64).T)
    shared_odd["alog"] = np.ascontiguousarray(f("ssd_a_log")[:, ::-1].reshape(2, 64).T)
    shared_odd["ssd_cw"] = np.ascontiguousarray(f("ssd_conv_w")[:, ::-1].reshape(2, 3, 24, 128).transpose(3, 0, 2, 1).reshape(128, 2 * 24 * 3))
    shared_odd["fcw"] = np.ascontiguousarray(f("ffn_conv_w")[:, ::-1].reshape(4, 3, 44, 128).transpose(3, 0, 2, 1).reshape(128, 4 * 44 * 3))
    xs, xp = f("x_sample"), f("x_prompt")
    st, ck, cv, cc, cctx = f("state_ssd"), f("cache_k"), f("cache_v"), f("c"), f("c_ctx")
    in_maps = []
    for core in range(NCORES):
        b, r = core // 2, core % 2
        mp = dict(shared_odd if r else shared)
        fl = (lambda a, ax: np.flip(a, axis=ax)) if r else (lambda a, ax: a)
        mp["xs_in"] = np.ascontiguousarray(fl(xs[b, r * LH:(r + 1) * LH], 0))
        mp["ropeC"] = np.ascontiguousarray(fl(rC[:, r * LH:(r + 1) * LH], 1))
        mp["ropeS"] = np.ascontiguousarray(fl(rS[:, r * LH:(r + 1) * LH], 1))
        mp["rmask"] = np.ascontiguousarray(np.broadcast_to(np.array([[1.0 - r, float(r)]], np.float32), (128, 2)))
        mp["xp_in"] = np.ascontiguousarray(fl(xp[2 * core:2 * core + 2], 1).reshape(2 * LP, D))
        mp["state"] = np.ascontiguousarray(fl(st[b], 1).reshape(2, 2, 16, 128, 128))
        mp["ck"] = np.ascontiguousarray(ck[b].reshape(2, 256, D))
        mp["cv"] = np.ascontiguousarray(cv[b].reshape(2, 256, D))
        cm2 = np.stack([_fm(cc[b], 8), _fm(cctx, 8)], axis=2).reshape(128, 16)
        mp["cmod"] = np.ascontiguousarray(cm2)
        in_maps.append(mp)
    res = run_bass_kernel_spmd(nc, in_maps, core_ids=list(range(NCORES)))
    R = res.results
    def flo(c, a, ax):
        return np.flip(a, axis=ax) if c % 2 else a
    y_prompt = np.concatenate([flo(c, R[c]["y_p"].reshape(2, LP, D), 1) for c in range(NCORES)], axis=0)
    y_sample = np.stack([np.concatenate([R[2 * b]["y_s"], np.flip(R[2 * b + 1]["y_s"], axis=0)], axis=0) for b in range(4)], axis=0)
    nstate = np.concatenate([flo(c, R[c]["nstate"].reshape(2, 2, 2, 32, 64, 128), 2) for c in range(NCORES)], axis=0)
    nk = np.concatenate([flo(c, R[c]["nk"].reshape(2, 2, LP, 8, 2, 64), 2) for c in range(NCORES)], axis=0)
    nv = np.concatenate([flo(c, R[c]["nv"].reshape(2, 2, LP, 8, 128), 2) for c in range(NCORES)], axis=0)
    return (y_prompt.astype(np.float32), y_sample.astype(np.float32), nstate.astype(np.float32),
            nk.astype(np.float32), nv.astype(np.float32))
```

```python
import math
import numpy as np
from contextlib import ExitStack
import concourse.bass as bass
import concourse.mybir as mybir
from concourse.bass_utils import run_bass_kernel_spmd

F32 = mybir.dt.float32
AF = mybir.ActivationFunctionType
ALU = mybir.AluOpType
AX = mybir.AxisListType

DEPTH_RUN = 4
DO_SAMPLE = True
DO_PROMPT = True
NCORES = 8
EPS = 1e-6
D = 1024
LS = 2048
LH = 1024
PAIRS = [[0, 1], [2, 3], [4, 5], [6, 7]]
LP = 256
FH = 2816
NPAGES = 93
STQ = "act"


class Buf:
    __slots__ = ("name", "lw", "rd")

    def __init__(self, name=""):
        self.name = name
        self.lw = None
        self.rd = {}


class Prog:
    ENG = ["pe", "act", "dve", "pool", "sp"]
    NDMA = 16
    DMA_POOLS = {"sp": (0, 16), "act": (0, 16), "pool": (0, 16)}
    SHARED_POOL = True
    SAME_ENGINE_SYNC = True

    def __init__(self, nc):
        self.nc = nc
        self.streams = {e: [] for e in self.ENG}
        self.cnt = {e: 0 for e in self.ENG}
        self.seen = {e: {} for e in self.ENG}
        self.ndma = {q: 0 for q in self.DMA_POOLS}
        self.ncc = 0

    def op(self, eng, fn, reads=(), writes=(), dma=False, cc=False):
        deps = {}

        def add(ev):
            if ev is None:
                return
            k, v = ev
            if v > deps.get(k, 0):
                deps[k] = v

        for b in reads:
            add(b.lw)
        for b in writes:
            add(b.lw)
            for kv in b.rd.items():
                add(kv)
        if dma:
            base, npool = self.DMA_POOLS[eng]
            qk = "sp" if self.SHARED_POOL else eng
            i = self.ndma[qk]
            self.ndma[qk] += 1
            slot = base + i % npool
            val = 16 * (i // npool + 1)
            ev = (("dma", slot), val)
            if val > 16:
                add((("dma", slot), val - 16))
        elif cc:
            self.ncc += 1
            ev = ("cc", self.ncc)
        else:
            self.cnt[eng] += 1
            ev = (eng, self.cnt[eng])
        waits = []
        seen = self.seen[eng]
        for k, v in deps.items():
            if k == eng and (eng == "pe" or not self.SAME_ENGINE_SYNC):
                continue
            if seen.get(k, 0) >= v:
                continue
            seen[k] = v
            waits.append((k, v))
        self.streams[eng].append((fn, waits, ev))
        k, v = ev
        for b in reads:
            if b.rd.get(k, 0) < v:
                b.rd[k] = v
        for b in writes:
            b.lw = ev
            b.rd = {}
        return ev

    def emit(self, final_events=()):
        nc = self.nc
        with ExitStack() as es:
            sems = {}
            for e in self.ENG:
                sems[e] = es.enter_context(nc.semaphore("s_" + e))
            for i in range(self.NDMA):
                sems[("dma", i)] = es.enter_context(nc.semaphore("s_dma%d" % i))
            sems["cc"] = es.enter_context(nc.semaphore("s_cc"))
            block = es.enter_context(nc.Block())
            streams = self.streams

            def run(engname, eng):
                for fn, waits, ev in streams[engname]:
                    for k, v in waits:
                        eng.wait_ge(sems[k], v)
                    inst = fn(eng)
                    k, v = ev
                    inst.then_inc(sems[k], 16 if isinstance(k, tuple) else 1)
                if engname == "sp":
                    fe = {}
                    for k, v in final_events:
                        fe[k] = max(fe.get(k, 0), v)
                    for k, v in fe.items():
                        eng.wait_ge(sems[k], v)

            @block.tensor
            def _(eng):
                run("pe", eng)

            @block.scalar
            def _(eng):
                run("act", eng)

            @block.vector
            def _(eng):
                run("dve", eng)

            @block.gpsimd
            def _(eng):
                run("pool", eng)

            @block.sync
            def _(eng):
                run("sp", eng)


class Tile:
    def __init__(self, arena, off, n):
        self.arena = arena
        self.off = off
        self.n = n

    def ap(self, lo=0, hi=None):
        hi = self.n if hi is None else hi
        return self.arena.t[:, self.off + lo:self.off + hi]

    def v3(self, a, b, lo=0):
        return self.arena.t[:, self.off + lo:self.off + lo + a * b].rearrange("p (a b) -> p a b", a=a, b=b)

    def b(self, lo=0, hi=None):
        hi = self.n if hi is None else hi
        p0 = (self.off + lo) // 512
        p1 = (self.off + hi - 1) // 512
        return self.arena.pages[p0:p1 + 1]


class Arena:
    def __init__(self, nc, es, npages):
        self.t = es.enter_context(nc.sbuf_tensor("arena", [128, npages * 512], F32))
        self.pages = [Buf("pg%d" % i) for i in range(npages)]
        self.top = 0
        self.npages = npages

    def alloc(self, nelem):
        npg = (nelem + 511) // 512
        assert self.top + npg <= self.npages, ("arena overflow", self.top, npg, self.npages)
        t = Tile(self, self.top * 512, nelem)
        self.top += npg
        return t

    def mark(self):
        return self.top

    def release(self, m):
        self.top = m


class Ring:
    def __init__(self, arena, n, nelem):
        self.tiles = [arena.alloc(nelem) for _ in range(n)]
        self.i = 0

    def next(self):
        t = self.tiles[self.i % len(self.tiles)]
        self.i += 1
        return t


def halo_blocks(nseq, L, maxn=510):
    out = []
    if L > maxn:
        nb = -(-L // maxn)
        base = -(-L // nb)
        for s in range(nseq):
            t = 0
            while t < L:
                n = min(base, L - t)
                out.append((s * L + t, n, t > 0, t + n < L))
                t += n
    else:
        for s in range(nseq):
            out.append((s * L, L, False, False))
    return out


def build_program(depth_run, do_sample, do_prompt):
    nc = bass.Bass("TRN2", target_bir_lowering=False)
    es = ExitStack()

    def din(name, shape):
        return nc.dram_tensor(name, list(shape), F32, kind="ExternalInput").ap()

    def dout(name, shape):
        return nc.dram_tensor(name, list(shape), F32, kind="ExternalOutput").ap()

    def dscr(name, shape):
        return nc.dram_tensor(name, list(shape), F32, kind="Internal").ap()

    I = {}
    I["xs_in"] = din("xs_in", [LH, D])
    I["rmask"] = din("rmask", [128, 2])
    I["xp_in"] = din("xp_in", [2 * LP, D])
    I["state"] = din("state", [2, 2, 16, 128, 128])
    I["ck"] = din("ck", [2, 256, D])
    I["cv"] = din("cv", [2, 256, D])
    I["cmod"] = din("cmod", [128, 16])
    I["mod_w"] = din("mod_w", [4, D, 6 * D])
    I["mod_bT"] = din("mod_bT", [128, 4 * 48])
    I["nmg"] = din("nmg", [128, 32])
    I["nfg"] = din("nfg", [128, 32])
    I["fng"] = din("fng", [128, 8])
    I["ssd_in_w"] = din("ssd_in_w", [2, D, 5184])
    I["ssd_cw"] = din("ssd_cw", [128, 2 * 24 * 3])
    I["ssd_cb"] = din("ssd_cb", [128, 2 * 24])
    I["dtb"] = din("dtb", [64, 2])
    I["alog"] = din("alog", [64, 2])
    I["dsk"] = din("dsk", [128, 64])
    I["sng"] = din("sng", [128, 32])
    I["ssd_out_w"] = din("ssd_out_w", [2, 2048, D])
    I["att_qkv_w"] = din("att_qkv_w", [2, D, 3 * D])
    I["lamb"] = din("lamb", [128, 2 * 256])
    I["sub"] = din("sub", [128, 2])
    I["att_out_w"] = din("att_out_w", [2, D, D])
    I["ffn_up_w"] = din("ffn_up_w", [4, D, 2 * FH])
    I["fcw"] = din("fcw", [128, 4 * 44 * 3])
    I["fcb"] = din("fcb", [128, 4 * 44])
    I["ffn_down_w"] = din("ffn_down_w", [4, FH, D])
    I["cmat"] = din("cmat", [128, 7 * 128])
    I["ropeC"] = din("ropeC", [128, LH])
    I["ropeS"] = din("ropeS", [128, LH])
    O = {}
    O["y_s"] = dout("y_s", [LH, D])
    O["y_p"] = dout("y_p", [2 * LP, D])
    O["nstate"] = dout("nstate", [2, 2, 2, 16, 128, 128])
    O["nk"] = dout("nk", [2, 2, LP, D])
    O["nv"] = dout("nv", [2, 2, LP, D])
    S_proj = dscr("s_proj", [5248, LH])
    S_projG = dscr("s_projg", [2 * 5248, LH])
    S_y = dscr("s_y", [LS, 2048])
    S_q = dscr("s_q", [D, LH])
    S_kv = dscr("s_kv", [2 * D, LH])
    S_kvG = dscr("s_kvg", [4 * D, LH])
    S_o = dscr("s_o", [D, LH])
    S_hst = dscr("s_hst", [2048, 128])
    S_hstG = dscr("s_hstg", [4096, 128])
    H_loc = dscr("h_loc", [128, 16])
    H_all = dscr("h_all", [256, 16])
    dbufs = {}

    def db(*key):
        if key not in dbufs:
            dbufs[key] = Buf(str(key))
        return dbufs[key]

    with es:
        P = Prog(nc)
        A = Arena(nc, es, NPAGES)
        psum = [es.enter_context(nc.psum_tensor("ps%d" % i, [128, 512], F32)) for i in range(8)]
        pb = [Buf("psum%d" % i) for i in range(8)]
        out_events = []

        def mm(out, lhsT, rhs, start, stop, rd, wr):
            P.op("pe", lambda e: e.matmul(out, lhsT=lhsT, rhs=rhs, start=start, stop=stop), reads=rd, writes=wr)

        def act(out, in_, func, rd, wr, bias=None, scale=None, eng="act"):
            kw = {}
            if bias is not None:
                kw["bias"] = bias
            if scale is not None:
                kw["scale"] = scale
            P.op("act", lambda e: e.activation(out=out, in_=in_, func=func, **kw), reads=rd, writes=wr)

        def tt(out, in0, in1, op, rd, wr, eng="dve"):
            P.op(eng, lambda e: e.tensor_tensor(out=out, in0=in0, in1=in1, op=op), reads=rd, writes=wr)

        def ts(out, in0, s1, s2, op0, op1, rd, wr, eng="dve"):
            if op1 is None:
                P.op(eng, lambda e: e.tensor_scalar(out=out, in0=in0, scalar1=s1, scalar2=None, op0=op0), reads=rd, writes=wr)
            else:
                P.op(eng, lambda e: e.tensor_scalar(out=out, in0=in0, scalar1=s1, scalar2=s2, op0=op0, op1=op1), reads=rd, writes=wr)

        def stt(out, in0, scalar, in1, op0, op1, rd, wr):
            P.op("dve", lambda e: e.scalar_tensor_tensor(out=out, in0=in0, scalar=scalar, in1=in1, op0=op0, op1=op1), reads=rd, writes=wr)

        def cp(out, in_, rd, wr, eng="dve"):
            P.op(eng, lambda e: e.tensor_copy(out=out, in_=in_), reads=rd, writes=wr)

        def mset(ap, val, wr, eng="pool"):
            P.op(eng, lambda e: e.memset(ap, val), writes=wr)

        def recip(out, in_, rd, wr):
            P.op("dve", lambda e: e.reciprocal(out=out, in_=in_), reads=rd, writes=wr)

        def dma(out, in_, rd, wr, final=False, q="sp"):
            ev = P.op(q, lambda e: e.dma_start(out=out, in_=in_), reads=rd, writes=wr, dma=True)
            if final:
                out_events.append(ev)
            return ev

        def allgather(inp, outp, rd, wr):
            return P.op("pool", lambda e: e.collective_compute("AllGather", ALU.bypass, replica_groups=PAIRS, ins=[inp], outs=[outp]), reads=rd, writes=wr, cc=True)

        CCR = 512

        def allgather_rows(src, dst, nrows, rd_fn, key):
            for i, b0 in enumerate(range(0, nrows, CCR)):
                b1 = min(nrows, b0 + CCR)
                allgather(src[b0:b1, :], dst[2 * b0:2 * b1, :], rd_fn(b0, b1), [db(key, i)])

        def gathered(dst, nrows, key, r_, row0, nr):
            i = row0 // CCR
            b0 = i * CCR
            b1 = min(nrows, b0 + CCR)
            assert row0 + nr <= b1
            base = 2 * b0 + r_ * (b1 - b0) + (row0 - b0)
            return dst[base:base + nr, :], [db(key, i)]

        cm = A.alloc(7 * 128)
        dma(cm.ap(), I["cmat"], [], cm.b())
        cmv = cm.v3(7, 128)
        ident, ones, mT, mL, mTs, mLs, prot = [cmv[:, i, :] for i in range(7)]
        CM = cm.b()
        small = A.alloc(1024)
        SM = small.b()
        sm = small.ap()
        mset(sm[:, 0:1], EPS, SM)
        mset(sm[:, 1:2], 1.0, SM)
        epsc = sm[:, 0:1]
        dma(sm[:, 8:16], I["fng"], [], SM)
        dma(sm[:, 16:18], I["sub"], [], SM)
        dma(sm[0:64, 18:20], I["dtb"], [], SM)
        dma(sm[0:64, 20:22], I["alog"], [], SM)
        act(sm[0:64, 20:22], sm[0:64, 20:22], AF.Exp, SM, SM)
        ts(sm[0:64, 20:22], sm[0:64, 20:22], -1.0, None, ALU.mult, None, SM, SM)
        dma(sm[:, 64:128], I["dsk"], [], SM)
        dma(sm[:, 32:34], I["rmask"], [], SM)
        mA = sm[:, 32:33]
        mB = sm[:, 33:34]
        lam_init = [0.8 - 0.6 * math.exp(-0.3 * 1), 0.8 - 0.6 * math.exp(-0.3 * 3)]
        lw = A.alloc(512)
        dma(lw.ap(), I["lamb"], [], lw.b())
        lwv = lw.v3(2, 256)
        for sl in range(2):
            for j in range(2):
                tt(lwv[:, sl, j * 128:j * 128 + 64], lwv[:, sl, j * 128:j * 128 + 64], lwv[:, sl, j * 128 + 64:j * 128 + 128], ALU.mult, lw.b(), lw.b())
                P.op("dve", lambda e, sl=sl, j=j: e.tensor_reduce(out=sm[:, 28 + 2 * sl + j:29 + 2 * sl + j], in_=lwv[:, sl, j * 128:j * 128 + 64], axis=AX.X, op=ALU.add), reads=lw.b(), writes=SM)
            act(sm[:, 28 + 2 * sl:30 + 2 * sl], sm[:, 28 + 2 * sl:30 + 2 * sl], AF.Exp, SM, SM)
            tt(sm[:, 24 + sl:25 + sl], sm[:, 28 + 2 * sl:29 + 2 * sl], sm[:, 29 + 2 * sl:30 + 2 * sl], ALU.subtract, SM, SM)
            ts(sm[:, 24 + sl:25 + sl], sm[:, 24 + sl:25 + sl], lam_init[sl], -1.0, ALU.add, ALU.mult, SM, SM)
            ts(sm[:, 26 + sl:27 + sl], sm[:, 16 + sl:17 + sl], 1.0 - lam_init[sl], None, ALU.mult, None, SM, SM)
        vec = A.alloc(32 + 32 + 32 + 144 + 48 + 528 + 176)
        VB = vec.b()
        va = vec.ap()
        o_ = 0
        nmg = va[:, 0:32]; nfg = va[:, 32:64]; sng = va[:, 64:96]
        scw = va[:, 96:240]; scb = va[:, 240:288]; fcw = va[:, 288:816]; fcb = va[:, 816:992]
        for ap_, nm in ((nmg, "nmg"), (nfg, "nfg"), (sng, "sng"), (scw, "ssd_cw"), (scb, "ssd_cb"), (fcw, "fcw"), (fcb, "fcb")):
            dma(ap_, I[nm], [], VB)
        modv = A.alloc(4 * 48 * 2)
        MB = modv.b()
        mv = modv.ap().rearrange("p (l f c) -> p l f c", l=4, f=48, c=2)
        mbt = A.alloc(4 * 48)
        dma(mbt.ap(), I["mod_bT"], [], mbt.b())
        mbv = mbt.v3(4, 48)
        cmod = A.alloc(16)
        dma(cmod.ap(), I["cmod"], [], cmod.b())
        act(cmod.ap(), cmod.ap(), AF.Silu, cmod.b(), cmod.b())
        cmv2 = cmod.v3(8, 2)
        xT = A.alloc(8 * LH)
        phase_mark = A.mark()
        RG = {}

        def xTc(k, t0, t1):
            return xT.ap(k * LH + t0, k * LH + t1), xT.b(k * LH + t0, k * LH + t1)

        mM = A.mark()
        RG["wsm"] = Ring(A, 4, 1024)
        for l in range(depth_run):
            for f in range(48):
                w = RG["wsm"].next()
                dma(w.v3(8, 128), I["mod_w"][l, :, f * 128:(f + 1) * 128].rearrange("(k p) n -> p k n", p=128), [], w.b())
                bank = f % 2
                for k in range(8):
                    mm(psum[bank][:, 0:2], w.v3(8, 128)[:, k, :], cmv2[:, k, :], k == 0, k == 7, w.b() + cmod.b(), [pb[bank]])
                ts(mv[:, l, f, :], psum[bank][:, 0:2], mbv[:, l, f:f + 1], None, ALU.add, None, [pb[bank]] + mbt.b(), MB)
        for l in range(depth_run):
            for c in range(2):
                stt(mv[:, l, 8:16, c], mv[:, l, 8:16, c], 1.0, nmg[:, l * 8:(l + 1) * 8], ALU.add, ALU.mult, MB + VB, MB)
                stt(mv[:, l, 32:40, c], mv[:, l, 32:40, c], 1.0, nfg[:, l * 8:(l + 1) * 8], ALU.add, ALU.mult, MB + VB, MB)

        A.release(mM)
        sqr = Ring(A, 2, 512)
        rstd_t = A.alloc(512)
        phase_mark = A.mark()

        def compute_rstd(t0, ncol, bank):
            for k in range(8):
                s = sqr.next()
                xa, xb = xTc(k, t0, t0 + ncol)
                act(s.ap(0, ncol), xa, AF.Square, xb, s.b())
                mm(psum[bank][:, :ncol], ones, s.ap(0, ncol), k == 0, k == 7, s.b() + CM, [pb[bank]])
            act(rstd_t.ap(0, ncol), psum[bank][:, :ncol], AF.Sqrt, [pb[bank]] + SM, rstd_t.b(), bias=epsc, scale=1.0 / D)
            recip(rstd_t.ap(0, ncol), rstd_t.ap(0, ncol), rstd_t.b(), rstd_t.b())

        def make_hs(hs, t0, ncol, l, gidx, sidx, c, bank, col_off=0):
            compute_rstd(t0, ncol, bank)
            W = hs.n // 8
            for k in range(8):
                xa, xb = xTc(k, t0, t0 + ncol)
                o = hs.ap(k * W + col_off, k * W + col_off + ncol)
                ob = hs.b(k * W + col_off, k * W + col_off + ncol)
                stt(o, xa, mv[:, l, gidx + k, c:c + 1], rstd_t.ap(0, ncol), ALU.mult, ALU.mult, xb + MB + rstd_t.b(), ob)
                act(o, o, AF.Identity, ob + MB, ob, bias=mv[:, l, sidx + k, c:c + 1])

        def load_wcols(Wd, col0, ncol):
            w = RG["wsm"].next()
            dma(w.v3(8, 128)[:, :, 0:ncol], Wd[:, col0:col0 + ncol].rearrange("(k p) n -> p k n", p=128), [], w.b())
            return w

        def proj_fm(hs, Wd, col0, ncol, c_lo, c_hi, bank):
            w = load_wcols(Wd, col0, ncol)
            Wn = hs.n // 8
            for k in range(8):
                mm(psum[bank][0:ncol, c_lo:c_hi], w.v3(8, 128)[:, k, 0:ncol], hs.ap(k * Wn + c_lo, k * Wn + c_hi), k == 0, k == 7,
                   w.b() + hs.b(k * Wn + c_lo, k * Wn + c_hi), [pb[bank]])

        def conv3(dst, dstb, u, n, cw3, cbias, ub, npart=128):
            ts(dst, u[:, 0:n], cw3[:, 0:1], cbias, ALU.mult, ALU.add, ub + VB, dstb)
            stt(dst, u[:, 1:n + 1], cw3[:, 1:2], dst, ALU.mult, ALU.add, ub + VB + dstb, dstb)
            stt(dst, u[:, 2:n + 2], cw3[:, 2:3], dst, ALU.mult, ALU.add, ub + VB + dstb, dstb)

        def residual_update(l, gate_idx, c, oc, t0, ncol, bank):
            xa, xb = xTc(oc, t0, t0 + ncol)
            stt(xa, psum[bank][:, 0:ncol], mv[:, l, gate_idx + oc, c:c + 1], xa, ALU.mult, ALU.add, [pb[bank]] + MB + xb, xb)

        def edge_halos(l, G, gidx, sidx, c):
            edge = A.alloc(16)
            if not G["shard"]:
                mset(edge.ap(), 0.0, edge.b(), eng="pool")
                return edge
            T = G["T"]
            h2 = A.alloc(16); h3 = A.alloc(16); el = A.alloc(16); e0 = A.alloc(16); e1 = A.alloc(16)
            make_hs(h2, 0, 2, l, gidx, sidx, c, 0)
            make_hs(h3, T - 2, 2, l, gidx, sidx, c, 0)
            cp(el.ap(0, 8).unsqueeze(2), h2.v3(8, 2)[:, :, 0:1], h2.b(), el.b())
            cp(el.ap(8, 16).unsqueeze(2), h3.v3(8, 2)[:, :, 1:2], h3.b(), el.b())
            dma(H_loc, el.ap(), el.b(), [db("hloc")], q=STQ)
            allgather(H_loc, H_all, [db("hloc")], [db("hall")])
            dma(e0.ap(), H_all[0:128, :], [db("hall")], e0.b())
            dma(e1.ap(), H_all[128:256, :], [db("hall")], e1.b())
            mset(edge.ap(0, 8), 0.0, edge.b(), eng="pool")
            ts(e0.ap(8, 16), e0.ap(8, 16), mB, None, ALU.mult, None, e0.b() + SM, e0.b())
            stt(edge.ap(8, 16), e1.ap(8, 16), mA, e0.ap(8, 16), ALU.mult, ALU.add, e1.b() + SM + e0.b(), edge.b())
            return edge

        def load_x(src, T):
            m = A.mark()
            ring = Ring(A, 2, 1024)
            for q in range(T // 128):
                t = ring.next()
                dma(t.ap(), src[q * 128:(q + 1) * 128, :], [], t.b())
                for k in range(8):
                    bank = k % 2
                    P.op("pe", lambda e, bank=bank, t=t, k=k: e.transpose(psum[bank][:, 0:128], t.ap(k * 128, (k + 1) * 128), ident), reads=t.b() + CM, writes=[pb[bank]])
                    xa, xb = xTc(k, q * 128, (q + 1) * 128)
                    act(xa, psum[bank][:, 0:128], AF.Copy, [pb[bank]], xb)
            A.release(m)

        def final_out(dst, T):
            m = A.mark()
            ring = Ring(A, 2, 1024)
            yT = A.alloc(8 * 512)
            for b0 in range(0, T, 512):
                compute_rstd(b0, 512, 0)
                for k in range(8):
                    xa, xb = xTc(k, b0, b0 + 512)
                    stt(yT.ap(k * 512, (k + 1) * 512), xa, sm[:, 8 + k:9 + k], rstd_t.ap(0, 512), ALU.mult, ALU.mult, xb + SM + rstd_t.b(), yT.b(k * 512, (k + 1) * 512))
                for q in range(4):
                    t = ring.next()
                    for k in range(8):
                        bank = 2 + k % 2
                        P.op("pe", lambda e, bank=bank, k=k, q=q: e.transpose(psum[bank][:, 0:128], yT.ap(k * 512 + q * 128, k * 512 + (q + 1) * 128), ident),
                             reads=yT.b(k * 512 + q * 128, k * 512 + (q + 1) * 128) + CM, writes=[pb[bank]])
                        act(t.ap(k * 128, (k + 1) * 128), psum[bank][:, 0:128], AF.Copy, [pb[bank]], t.b(k * 128, (k + 1) * 128))
                    dma(dst[b0 + q * 128:b0 + (q + 1) * 128, :], t.ap(), t.b(), [], final=True, q=STQ)
            A.release(m)

        def ffn_packed(l, G):
            m = A.mark()
            c = G["c"]
            RG["wsm"] = Ring(A, 4, 1024)
            wbig = Ring(A, 2, 22 * 128)
            hs = A.alloc(8 * 512)
            actT = A.alloc(22 * 512)
            ur = Ring(A, 2, 1024)
            tr_ = Ring(A, 3, 1024)
            for u in ur.tiles:
                mset(u.ap(0, 516), 0.0, u.b(), eng="pool")
            Wup = I["ffn_up_w"][l]
            Wdn = I["ffn_down_w"][l]
            make_hs(hs, 0, 512, l, 32, 24, c, 0)
            for j in range(22):
                res = []
                for half in range(2):
                    ch = half * 22 + j
                    bank = 2 * half + (j % 2)
                    proj_fm(hs, Wup, half * FH + j * 128, 128, 0, 512, bank)
                    u = ur.next()
                    act(u.ap(1, 257), psum[bank][:, 0:256], AF.Copy, [pb[bank]], u.b())
                    act(u.ap(259, 515), psum[bank][:, 256:512], AF.Copy, [pb[bank]], u.b())
                    t = tr_.next()
                    cw3 = fcw[:, (l * 44 + ch) * 3:(l * 44 + ch) * 3 + 3]
                    conv3(t.ap(0, 514), t.b(), u.ap(0, 516), 514, cw3, fcb[:, l * 44 + ch:l * 44 + ch + 1], u.b())
                    res.append(t)
                ta, tv = res
                act(ta.ap(0, 514), ta.ap(0, 514), AF.Silu, ta.b(), ta.b())
                tt(actT.v3(2, 256, j * 512), ta.v3(2, 258)[:, :, 0:256], tv.v3(2, 258)[:, :, 0:256], ALU.mult, ta.b() + tv.b(), actT.b(j * 512, (j + 1) * 512))
            for oc in range(8):
                w = wbig.next()
                dma(w.v3(22, 128), Wdn[:, oc * 128:(oc + 1) * 128].rearrange("(k p) n -> p k n", p=128), [], w.b())
                bank = 4 + oc % 2
                for j in range(22):
                    mm(psum[bank][:, 0:512], w.v3(22, 128)[:, j, :], actT.ap(j * 512, (j + 1) * 512), j == 0, j == 21, w.b() + actT.b(j * 512, (j + 1) * 512), [pb[bank]])
                residual_update(l, 40, c, oc, 0, 512, bank)
            A.release(m)

        def ffn_layer(l, G):
            if G["packed"]:
                return ffn_packed(l, G)
            m = A.mark()
            c = G["c"]
            blocks = halo_blocks(G["nseq"], G["L"], 342)
            NB = max(b[1] for b in blocks)
            Wn = NB + 2
            RG["wsm"] = Ring(A, 4, 1024)
            wbig = Ring(A, 2, 22 * 128)
            hs = A.alloc(8 * Wn)
            actT = A.alloc(22 * NB)
            ur = Ring(A, 2, 512)
            tr_ = Ring(A, 3, 512)
            stash = A.alloc(8)
            edge = edge_halos(l, G, 32, 24, c)
            Wup = I["ffn_up_w"][l]
            Wdn = I["ffn_down_w"][l]
            for (t0, n, hl, hr) in blocks:
                c_hi = n + 2 if hr else n + 1
                make_hs(hs, t0, c_hi - 1, l, 32, 24, c, 0, col_off=1)
                if hl:
                    for k in range(8):
                        cp(hs.ap(k * Wn, k * Wn + 1), stash.ap(k, k + 1), stash.b(), hs.b(k * Wn, k * Wn + 1), eng="pool")
                else:
                    cp(hs.v3(8, Wn)[:, :, 0:1], edge.ap(0, 8).unsqueeze(2), edge.b(), hs.b(), eng="pool")
                if not hr:
                    cp(hs.v3(8, Wn)[:, :, n + 1:n + 2], edge.ap(8, 16).unsqueeze(2), edge.b(), hs.b(), eng="pool")
                c_lo, c_hi = 0, n + 2
                for k in range(8):
                    cp(stash.ap(k, k + 1), hs.ap(k * Wn + n, k * Wn + n + 1), hs.b(k * Wn + n, k * Wn + n + 1), stash.b(), eng="pool")
                for j in range(22):
                    res = []
                    for half in range(2):
                        ch = half * 22 + j
                        bank = 2 * half + (j % 2)
                        proj_fm(hs, Wup, half * FH + j * 128, 128, c_lo, c_hi, bank)
                        u = ur.next()
                        act(u.ap(c_lo, c_hi), psum[bank][:, c_lo:c_hi], AF.Copy, [pb[bank]], u.b())
                        t = tr_.next()
                        cw3 = fcw[:, (l * 44 + ch) * 3:(l * 44 + ch) * 3 + 3]
                        conv3(t.ap(0, n), t.b(), u.ap(), n, cw3, fcb[:, l * 44 + ch:l * 44 + ch + 1], u.b())
                        res.append(t)
                    ta, tv = res
                    act(ta.ap(0, n), ta.ap(0, n), AF.Silu, ta.b(), ta.b())
                    tt(actT.ap(j * NB, j * NB + n), ta.ap(0, n), tv.ap(0, n), ALU.mult, ta.b() + tv.b(), actT.b(j * NB, j * NB + n))
                for oc in range(8):
                    w = wbig.next()
                    dma(w.v3(22, 128), Wdn[:, oc * 128:(oc + 1) * 128].rearrange("(k p) n -> p k n", p=128), [], w.b())
                    bank = 4 + oc % 2
                    for j in range(22):
                        mm(psum[bank][:, 0:n], w.v3(22, 128)[:, j, :], actT.ap(j * NB, j * NB + n), j == 0, j == 21, w.b() + actT.b(j * NB, j * NB + n), [pb[bank]])
                    residual_update(l, 40, c, oc, t0, n, bank)
            A.release(m)

        def ssd_layer(l, G):
            slot = l // 2
            c = G["c"]
            T, L, nseq = G["T"], G["L"], G["nseq"]
            Win = I["ssd_in_w"][slot]
            Lf = L
            shard = G["shard"]
            NT = Lf // 128
            m0 = A.mark()
            ssq = A.alloc(nseq * NT * 4 + 8)
            m = A.mark()
            blocks = halo_blocks(nseq, L)
            if G["packed"]:
                blocks = []
                RG["wsm"] = Ring(A, 4, 1024)
                hs = A.alloc(8 * 512)
                ur = Ring(A, 2, 1024)
                tr_ = Ring(A, 3, 1024)
                for u in ur.tiles:
                    mset(u.ap(0, 516), 0.0, u.b(), eng="pool")
                make_hs(hs, 0, 512, l, 8, 0, c, 0)
                for j in range(41):
                    bank = 2 + j % 2
                    ncol = 128 if j < 40 else 64
                    proj_fm(hs, Win, j * 128, ncol, 0, 512, bank)
                    t = tr_.next()
                    if j < 16:
                        act(t.ap(0, 512), psum[bank][:, 0:512], AF.Silu, [pb[bank]], t.b())
                        dma(S_proj[j * 128:(j + 1) * 128, 0:512], t.ap(0, 512), t.b(), [db("proj", j, 0)], q=STQ)
                    elif j < 40:
                        u = ur.next()
                        act(u.ap(1, 257), psum[bank][:, 0:256], AF.Copy, [pb[bank]], u.b())
                        act(u.ap(259, 515), psum[bank][:, 256:512], AF.Copy, [pb[bank]], u.b())
                        ch = slot * 24 + (j - 16)
                        conv3(t.ap(0, 514), t.b(), u.ap(0, 516), 514, scw[:, ch * 3:ch * 3 + 3], scb[:, ch:ch + 1], u.b())
                        act(t.ap(0, 514), t.ap(0, 514), AF.Silu, t.b(), t.b())
                        dma(S_proj[j * 128:(j + 1) * 128, 0:256], t.ap(0, 256), t.b(), [db("proj", j, 0)], q=STQ)
                        dma(S_proj[j * 128:(j + 1) * 128, 256:512], t.ap(258, 514), t.b(), [db("proj", j, 0)], q=STQ)
                    else:
                        tp = t.arena.t[0:64, t.off:t.off + 512]
                        act(tp, psum[bank][0:64, 0:512], AF.Exp, [pb[bank]] + SM, t.b(), bias=sm[0:64, 18 + slot:19 + slot])
                        act(tp, tp, AF.Ln, t.b() + SM, t.b(), bias=sm[0:64, 1:2])
                        t2 = tr_.next()
                        tp2 = t2.arena.t[0:64, t2.off:t2.off + 512]
                        ts(tp2, tp, sm[0:64, 20 + slot:21 + slot], None, ALU.mult, None, t.b() + SM, t2.b())
                        dma(S_proj[5184:5248, 0:512], tp2, t2.b(), [db("proj", 41, 0)], q=STQ)
                        dma(S_proj[5120:5184, 0:512], tp, t.b(), [db("proj", 40, 0)], q=STQ)
            else:
                NB = max(b[1] for b in blocks)
                Wn = NB + 2
                RG["wsm"] = Ring(A, 4, 1024)
                hs = A.alloc(8 * Wn)
                ur = Ring(A, 3, 512)
                tr_ = Ring(A, 3, 512)
                edge = edge_halos(l, G, 8, 0, c)
            for bi, (t0, n, hl, hr) in enumerate(blocks):
                c_lo = 0 if hl else 1
                c_hi = n + 2 if hr else n + 1
                lo_tok = t0 - 1 + c_lo
                make_hs(hs, lo_tok, c_hi - c_lo, l, 8, 0, c, 0, col_off=c_lo)
                if not hl:
                    cp(hs.v3(8, Wn)[:, :, 0:1], edge.ap(0, 8).unsqueeze(2), edge.b(), hs.b(), eng="pool")
                if not hr:
                    cp(hs.v3(8, Wn)[:, :, n + 1:n + 2], edge.ap(8, 16).unsqueeze(2), edge.b(), hs.b(), eng="pool")
                c_lo, c_hi = 0, n + 2
                for j in range(41):
                    bank = 2 + j % 2
                    ncol = 128 if j < 40 else 64
                    proj_fm(hs, Win, j * 128, ncol, c_lo, c_hi, bank)
                    t = tr_.next()
                    if j < 16:
                        act(t.ap(0, n), psum[bank][:, 1:n + 1], AF.Silu, [pb[bank]], t.b())
                    elif j < 40:
                        u = ur.next()
                        act(u.ap(c_lo, c_hi), psum[bank][:, c_lo:c_hi], AF.Copy, [pb[bank]], u.b())
                        ch = slot * 24 + (j - 16)
                        conv3(t.ap(0, n), t.b(), u.ap(), n, scw[:, ch * 3:ch * 3 + 3], scb[:, ch:ch + 1], u.b())
                        act(t.ap(0, n), t.ap(0, n), AF.Silu, t.b(), t.b())
                    else:
                        tp = t.arena.t[0:64, t.off:t.off + n]
                        act(tp, psum[bank][0:64, 1:n + 1], AF.Exp, [pb[bank]] + SM, t.b(), bias=sm[0:64, 18 + slot:19 + slot])
                        act(tp, tp, AF.Ln, t.b() + SM, t.b(), bias=sm[0:64, 1:2])
                        t2 = tr_.next()
                        tp2 = t2.arena.t[0:64, t2.off:t2.off + n]
                        ts(tp2, tp, sm[0:64, 20 + slot:21 + slot], None, ALU.mult, None, t.b() + SM, t2.b())
                        dma(S_proj[5184:5248, t0:t0 + n], tp2, t2.b(), [db("proj", 41, bi)], q=STQ)
                    if j < 40:
                        dma(S_proj[j * 128:(j + 1) * 128, t0:t0 + n], t.ap(0, n), t.b(), [db("proj", j, bi)], q=STQ)
                    else:
                        dma(S_proj[5120:5184, t0:t0 + n], t.arena.t[0:64, t.off:t.off + n], t.b(), [db("proj", 40, bi)], q=STQ)
            A.release(m)
            nblk = max(1, len(blocks))

            def pj(j):
                return [db("proj", j, bi) for bi in range(nblk)]

            def psrc(row0, nrows, tb_, tok0, ntok, j):
                return S_proj[row0:row0 + nrows, tb_ + tok0:tb_ + tok0 + ntok], pj(j)

            m = A.mark()
            BT = A.alloc(Lf); CT = A.alloc(Lf)
            Btok = A.alloc(Lf)
            dtok = A.alloc(NT * 64)
            datok = A.alloc(NT * 64)
            LN = [dict(xtok=A.alloc(Lf), ytok=A.alloc(Lf), h=A.alloc(128), X=2 + 3 * i, Y=3 + 3 * i, Z=4 + 3 * i) for i in range(2)]
            CBm = A.alloc(Lf)
            eaAll = A.alloc(NT * 64)
            wk = Ring(A, 8, 512)
            sm2 = Ring(A, 10, 128)
            ld = Ring(A, 3, 512)
            PC = min(512, Lf)
            mset(ssq.ap(), 0.0, ssq.b(), eng="pool")

            def prep(ln, d, s, tb, hp):
                xtok, ytok, h = ln["xtok"], ln["ytok"], ln["h"]
                HB = h.b()
                for p0 in range(0, Lf, PC):
                    lt = ld.next()
                    sa_, sb_ = psrc(2048 + hp * 128, 128, tb, p0, PC, 16 + hp)
                    dma(lt.ap(0, PC), sa_, sb_, lt.b())
                    for qq in range(PC // 128):
                        q = p0 // 128 + qq
                        P.op("pe", lambda e, lt=lt, qq=qq: e.transpose(psum[1][:, 0:128], lt.ap(qq * 128, (qq + 1) * 128), ident), reads=lt.b() + CM, writes=[pb[1]])
                        cp(xtok.ap(q * 128, (q + 1) * 128), psum[1][:, 0:128], [pb[1]], xtok.b(q * 128, (q + 1) * 128))
                        if d == 0:
                            tt(ytok.v3(2, 64, q * 128), xtok.v3(2, 64, q * 128), sm[:, 64 + slot * 32 + 2 * hp:64 + slot * 32 + 2 * hp + 2].unsqueeze(2).broadcast_to([128, 2, 64]),
                               ALU.mult, xtok.b(q * 128, (q + 1) * 128) + SM, ytok.b(q * 128, (q + 1) * 128), eng="pool")
                if d == 1:
                    for q in range(NT):
                        dma(ytok.ap(q * 128, (q + 1) * 128), S_y[tb + q * 128:tb + (q + 1) * 128, hp * 128:(hp + 1) * 128], [db("y", s, q, hp)], ytok.b(q * 128, (q + 1) * 128))
                if G["sample"]:
                    st_in = wk.next()
                    if d == 0:
                        dma(st_in.ap(0, 128), I["state"][slot, d, hp], [], st_in.b())
                        P.op("pe", lambda e, st_in=st_in: e.transpose(psum[1][:, 0:128], st_in.ap(0, 128), ident), reads=st_in.b() + CM, writes=[pb[1]])
                        cp(h.ap(), psum[1][:, 0:128], [pb[1]], HB)
                    else:
                        st2 = wk.next()
                        dma(st_in.ap(0, 128), S_hstG[hp * 128:(hp + 1) * 128, :], [db("hstG")], st_in.b())
                        dma(st2.ap(0, 128), S_hstG[2048 + hp * 128:2048 + (hp + 1) * 128, :], [db("hstG")], st2.b())
                        ts(st_in.ap(0, 128), st_in.ap(0, 128), mB, None, ALU.mult, None, st_in.b() + SM, st_in.b())
                        stt(h.ap(), st2.ap(0, 128), mA, st_in.ap(0, 128), ALU.mult, ALU.add, st2.b() + SM + st_in.b(), HB)
                else:
                    mset(h.ap(), 0.0, HB, eng="pool")

            def step(ln, d, hp, q):
                xtok, ytok, h = ln["xtok"], ln["ytok"], ln["h"]
                X, Y, Z = ln["X"], ln["Y"], ln["Z"]
                HB = h.b()
                hc0 = d * 32 + 2 * hp
                mR = mT if d == 0 else mL
                mLh = mLs if d == 0 else mTs
                da = datok.v3(NT, 64)[:, q, hc0:hc0 + 2]
                dtq = dtok.v3(NT, 64)[:, q, hc0:hc0 + 2]
                R = wk.next()
                tt(R.v3(2, 128), mR.unsqueeze(1).broadcast_to([128, 2, 128]), da.unsqueeze(2).broadcast_to([128, 2, 128]), ALU.mult, CM + datok.b(), R.b(), eng="pool")
                mm(psum[X][:, 0:256], mLh, R.ap(0, 256), True, True, CM + R.b(), [pb[X]])
                E = wk.next()
                act(E.ap(0, 256), psum[X][:, 0:256], AF.Exp, [pb[X]], E.b())
                ea_acs = eaAll.ap(q * 64 + 2 * hp, q * 64 + 2 * hp + 2)
                ea_dec = eaAll.ap(q * 64 + 32 + 2 * hp, q * 64 + 32 + 2 * hp + 2)
                xd = wk.next()
                tt(xd.v3(2, 64), xtok.v3(2, 64, q * 128), dtq.unsqueeze(2).broadcast_to([128, 2, 64]), ALU.mult, xtok.b(q * 128, (q + 1) * 128) + dtok.b(), xd.b(), eng="pool")
                lastc = 127 if d == 0 else 0
                tt(xd.v3(2, 64, 128), xd.v3(2, 64), E.v3(2, 128)[:, :, lastc:lastc + 1].broadcast_to([128, 2, 64]), ALU.mult, xd.b() + E.b(), xd.b(), eng="pool")
                tt(E.v3(2, 128), E.v3(2, 128), CBm.ap(q * 128, (q + 1) * 128).unsqueeze(1).broadcast_to([128, 2, 128]), ALU.mult, E.b() + CBm.b(q * 128, (q + 1) * 128), E.b())
                for hh in range(2):
                    mm(psum[Z][:, hh * 64:(hh + 1) * 64], E.v3(2, 128)[:, hh, :], xd.ap(hh * 64, (hh + 1) * 64), True, True, E.b() + xd.b(), [pb[Z]])
                mm(psum[Z][:, 128:256], CT.ap(q * 128, (q + 1) * 128), h.ap(), True, True, CT.b(q * 128, (q + 1) * 128) + HB, [pb[Z]])
                mm(psum[Z][:, 256:384], Btok.ap(q * 128, (q + 1) * 128), xd.ap(128, 256), True, True, Btok.b(q * 128, (q + 1) * 128) + xd.b(), [pb[Z]])
                yb = ytok.b(q * 128, (q + 1) * 128)
                tmp = sm2.next()
                tt(tmp.v3(2, 64), psum[Z][:, 128:256].rearrange("p (a b) -> p a b", a=2, b=64), ea_acs.unsqueeze(2).broadcast_to([128, 2, 64]), ALU.mult, [pb[Z]] + eaAll.b(), tmp.b())
                tt(tmp.ap(), tmp.ap(), psum[Z][:, 0:128], ALU.add, tmp.b() + [pb[Z]], tmp.b())
                tt(h.v3(2, 64), h.v3(2, 64), ea_dec.unsqueeze(2).broadcast_to([128, 2, 64]), ALU.mult, HB + eaAll.b(), HB)
                tt(h.ap(), h.ap(), psum[Z][:, 256:384], ALU.add, HB + [pb[Z]], HB)
                tt(ytok.ap(q * 128, (q + 1) * 128), ytok.ap(q * 128, (q + 1) * 128), tmp.ap(), ALU.add, yb + tmp.b(), yb, eng="pool")

            def fin(ln, d, s, tb, hp):
                xtok, ytok, h = ln["xtok"], ln["ytok"], ln["h"]
                HB = h.b()
                g = hp // 4
                if not G["sample"]:
                    P.op("pe", lambda e, h=h: e.transpose(psum[1][:, 0:128], h.ap(), ident), reads=HB + CM, writes=[pb[1]])
                    so = wk.next()
                    cp(so.ap(0, 128), psum[1][:, 0:128], [pb[1]], so.b())
                    dma(O["nstate"][s, slot, d, hp], so.ap(0, 128), so.b(), [], final=True, q=STQ)
                elif d == 0:
                    dma(S_hst[hp * 128:(hp + 1) * 128, :], h.ap(), HB, [db("hst", hp)], q=STQ)
                if d == 1:
                    for p0 in range(0, Lf, PC):
                        lt = ld.next()
                        sa_, sb_ = psrc(hp * 128, 128, tb, p0, PC, hp)
                        dma(lt.ap(0, PC), sa_, sb_, lt.b())
                        for qq in range(PC // 128):
                            q = p0 // 128 + qq
                            P.op("pe", lambda e, lt=lt, qq=qq: e.transpose(psum[0][:, 0:128], lt.ap(qq * 128, (qq + 1) * 128), ident), reads=lt.b() + CM, writes=[pb[0]])
                            yb = ytok.b(q * 128, (q + 1) * 128)
                            tt(ytok.ap(q * 128, (q + 1) * 128), ytok.ap(q * 128, (q + 1) * 128), psum[0][:, 0:128], ALU.mult, yb + [pb[0]], yb)
                            sq_ = sm2.next()
                            sc_ = nseq * NT * 4 + (hp % 2)
                            P.op("act", lambda e, sq_=sq_, q=q, sc_=sc_, ytok=ytok: e.activation(out=sq_.ap(), in_=ytok.ap(q * 128, (q + 1) * 128), func=AF.Square, accum_out=ssq.ap(sc_, sc_ + 1)),
                                 reads=yb, writes=sq_.b() + ssq.b())
                            si = (s * NT + q) * 4 + g
                            tt(ssq.ap(si, si + 1), ssq.ap(si, si + 1), ssq.ap(sc_, sc_ + 1), ALU.add, ssq.b(), ssq.b())
                for q in range(NT):
                    dma(S_y[tb + q * 128:tb + (q + 1) * 128, hp * 128:(hp + 1) * 128], ytok.ap(q * 128, (q + 1) * 128), ytok.b(q * 128, (q + 1) * 128), [db("y", s, q, hp)], q=STQ)

            for d in range(2):
                if d == 1 and shard:
                    allgather(S_hst, S_hstG, [db("hst", hp_) for hp_ in range(16)], [db("hstG")])
                for s in range(nseq):
                    tb = s * Lf
                    if d == 0 or nseq > 1:
                        for srow, dst_ in ((5120, dtok), (5184, datok)):
                            for p0 in range(0, Lf, PC):
                                lt = ld.next()
                                sa_, sb_ = psrc(srow, 64, tb, p0, PC, 40)
                                dma(lt.arena.t[0:64, lt.off:lt.off + PC], sa_, sb_ + pj(41), lt.b())
                                for qq in range(PC // 128):
                                    q = p0 // 128 + qq
                                    P.op("pe", lambda e, lt=lt, qq=qq: e.transpose(psum[0][:, 0:64], lt.arena.t[0:64, lt.off + qq * 128:lt.off + (qq + 1) * 128], ident[0:64, 0:64]),
                                         reads=lt.b() + CM, writes=[pb[0]])
                                    cp(dst_.ap(q * 64, (q + 1) * 64), psum[0][:, 0:64], [pb[0]], dst_.b())
                    mRd = mT if d == 0 else mL
                    for q in range(NT):
                        dq = datok.v3(NT, 64)[:, q, d * 32:(d + 1) * 32]
                        mm(psum[0][:, 0:32], mRd, dq, True, True, CM + datok.b(), [pb[0]])
                        mm(psum[0][:, 32:64], ones, dq, True, True, CM + datok.b(), [pb[0]])
                        act(eaAll.ap(q * 64, (q + 1) * 64), psum[0][:, 0:64], AF.Exp, [pb[0]], eaAll.b())
                    order = list(range(NT)) if d == 0 else list(range(NT - 1, -1, -1))
                    for hp0 in range(0, 16, 2):
                        g = hp0 // 4
                        if hp0 % 4 == 0:
                            sa_, sb_ = psrc(4096 + g * 128, 128, tb, 0, Lf, 32 + g)
                            dma(BT.ap(), sa_, sb_, BT.b())
                            sa_, sb_ = psrc(4608 + g * 128, 128, tb, 0, Lf, 36 + g)
                            dma(CT.ap(), sa_, sb_, CT.b())
                            for q in range(NT):
                                P.op("pe", lambda e, q=q: e.transpose(psum[1][:, 0:128], BT.ap(q * 128, (q + 1) * 128), ident), reads=BT.b(q * 128, (q + 1) * 128) + CM, writes=[pb[1]])
                                cp(Btok.ap(q * 128, (q + 1) * 128), psum[1][:, 0:128], [pb[1]], Btok.b(q * 128, (q + 1) * 128))
                                mm(psum[0][:, 0:128], BT.ap(q * 128, (q + 1) * 128), CT.ap(q * 128, (q + 1) * 128), True, True, BT.b(q * 128, (q + 1) * 128) + CT.b(q * 128, (q + 1) * 128), [pb[0]])
                                tt(CBm.ap(q * 128, (q + 1) * 128), psum[0][:, 0:128], mRd, ALU.mult, [pb[0]] + CM, CBm.b(q * 128, (q + 1) * 128))
                        for i in range(2):
                            prep(LN[i], d, s, tb, hp0 + i)
                        for q in order:
                            for i in range(2):
                                step(LN[i], d, hp0 + i, q)
                        for i in range(2):
                            fin(LN[i], d, s, tb, hp0 + i)
            A.release(m)
            m3 = A.mark()
            NTo = NT
            wbig = Ring(A, 2, 16 * 128)
            yr = Ring(A, 4, 2048)
            yTt = A.alloc(16 * 512)
            rs = A.alloc(nseq * NT * 4)
            act(rs.ap(), ssq.ap(0, nseq * NT * 4), AF.Sqrt, ssq.b() + SM, rs.b(), bias=epsc, scale=1.0 / 512)
            recip(rs.ap(), rs.ap(), rs.b(), rs.b())
            Wo = I["ssd_out_w"][slot]
            NTall = nseq * NT
            TBk = min(4, NTall)
            NW = TBk * 128
            for s in range(1):
                tb = 0
                for q0 in range(0, NTall, TBk):
                    for qq in range(TBk):
                        q = q0 + qq
                        yt = yr.next()
                        dma(yt.ap(), S_y[q * 128:(q + 1) * 128, :], [db("y", q // NT, q % NT, hp_) for hp_ in range(16)], yt.b())
                        for g in range(4):
                            si = q * 4 + g
                            ts(yt.ap(g * 512, (g + 1) * 512), yt.ap(g * 512, (g + 1) * 512), rs.ap(si, si + 1), None, ALU.mult, None, yt.b(g * 512, (g + 1) * 512) + rs.b(), yt.b(g * 512, (g + 1) * 512))
                        for kc in range(16):
                            bank = kc % 2
                            P.op("pe", lambda e, yt=yt, kc=kc, bank=bank: e.transpose(psum[bank][:, 0:128], yt.ap(kc * 128, (kc + 1) * 128), ident), reads=yt.b(kc * 128, (kc + 1) * 128) + CM, writes=[pb[bank]])
                            dst_lo = kc * NW + qq * 128
                            if kc % 2 == 0:
                                ts(yTt.ap(dst_lo, dst_lo + 128), psum[bank][:, 0:128], sng[:, slot * 16 + kc:slot * 16 + kc + 1], None, ALU.mult, None, [pb[bank]] + VB, yTt.b(dst_lo, dst_lo + 128))
                            else:
                                act(yTt.ap(dst_lo, dst_lo + 128), psum[bank][:, 0:128], AF.Copy, [pb[bank]] + VB, yTt.b(dst_lo, dst_lo + 128), scale=sng[:, slot * 16 + kc:slot * 16 + kc + 1])
                    for oc in range(8):
                        w = wbig.next()
                        dma(w.v3(16, 128), Wo[:, oc * 128:(oc + 1) * 128].rearrange("(k p) n -> p k n", p=128), [], w.b())
                        bank = 2 + oc % 2
                        for kc in range(16):
                            mm(psum[bank][:, 0:NW], w.v3(16, 128)[:, kc, :], yTt.ap(kc * NW, (kc + 1) * NW), kc == 0, kc == 15, w.b() + yTt.b(kc * NW, (kc + 1) * NW), [pb[bank]])
                        residual_update(l, 16, c, oc, tb + q0 * 128, NW, bank)
            A.release(m3)
            A.release(m0)

        def attn_layer(l, G):
            slot = l // 2
            c = G["c"]
            T, L, nseq = G["T"], G["L"], G["nseq"]
            sample = G["sample"]
            Wq = I["att_qkv_w"][slot]
            m = A.mark()
            RG["wsm"] = Ring(A, 4, 1024)
            wbig = Ring(A, 2, 8 * 256)
            hs = A.alloc(8 * 512)
            tr_ = Ring(A, 3, 512)
            rc = A.alloc(512); rsn = A.alloc(512)
            for bi, b0 in enumerate(range(0, T, 512)):
                make_hs(hs, b0, 512, l, 8, 0, c, 0)
                if sample:
                    dma(rc.ap(), I["ropeC"][:, b0:b0 + 512], [], rc.b())
                    dma(rsn.ap(), I["ropeS"][:, b0:b0 + 512], [], rsn.b())
                for j in range(16):
                    bank = 2 + j % 2
                    proj_fm(hs, Wq, j * 128, 128, 0, 512, bank)
                    t = tr_.next()
                    act(t.ap(), psum[bank][:, :], AF.Copy, [pb[bank]], t.b())
                    if sample:
                        mm(psum[4 + j % 2][:, :], prot, t.ap(), True, True, CM + t.b(), [pb[4 + j % 2]])
                        t2 = tr_.next()
                        tt(t2.ap(), psum[4 + j % 2][:, :], rsn.ap(), ALU.mult, [pb[4 + j % 2]] + rsn.b(), t2.b())
                        tt(t.ap(), t.ap(), rc.ap(), ALU.mult, t.b() + rc.b(), t.b())
                        tt(t.ap(), t.ap(), t2.ap(), ALU.add, t.b() + t2.b(), t.b())
                    if j < 8:
                        dma(S_q[j * 128:(j + 1) * 128, b0:b0 + 512], t.ap(), t.b(), [db("qk", j, bi)], q=STQ)
                    else:
                        dma(S_kv[(j - 8) * 128:(j - 7) * 128, b0:b0 + 512], t.ap(), t.b(), [db("qk", j, bi)], q=STQ)
                for part in (range(4, 12) if not sample else range(8, 12)):
                    w = wbig.next()
                    dma(w.v3(8, 256), Wq[:, part * 256:(part + 1) * 256].rearrange("(k p) n -> p k n", p=128), [], w.b())
                    for q in range(4):
                        bank = 6 + q % 2
                        for k in range(8):
                            mm(psum[bank][:, 0:256], hs.ap(k * 512 + q * 128, k * 512 + (q + 1) * 128), w.v3(8, 256)[:, k, :], k == 0, k == 7, hs.b(k * 512 + q * 128, k * 512 + (q + 1) * 128) + w.b(), [pb[bank]])
                        t = tr_.next()
                        act(t.ap(0, 256), psum[bank][:, 0:256], AF.Copy, [pb[bank]], t.b())
                        tok = b0 + q * 128
                        if sample:
                            dma(S_kv[D + tok:D + tok + 128, (part - 8) * 256:(part - 7) * 256], t.ap(0, 256), t.b(), [db("v", tok // 128)], q=STQ)
                        else:
                            s_, tq = tok // LP, tok % LP
                            if part < 8:
                                dma(O["nk"][s_, slot, tq:tq + 128, (part - 4) * 256:(part - 3) * 256], t.ap(0, 256), t.b(), [], final=True, q=STQ)
                            else:
                                dma(O["nv"][s_, slot, tq:tq + 128, (part - 8) * 256:(part - 7) * 256], t.ap(0, 256), t.b(), [db("v", tok // 128)], final=True, q=STQ)
            A.release(m)
            nblk = T // 512
            if G["shard"]:
                allgather_rows(S_kv, S_kvG, 2 * D, lambda b0, b1: ([db("qk", 8 + j, bi) for j in range(b0 // 128, b1 // 128) for bi in range(nblk)] if b0 < D
                                                                     else [db("v", q) for q in range((b0 - D) // 128, (b1 - D) // 128)]), "kvG")
            m = A.mark()
            NKT = (G["Lf"] + (256 if sample else 0)) // 128
            LK = NKT * 128
            QB = min(512, L)
            qT = A.alloc(L); kT = A.alloc(LK)
            vv = A.alloc(NKT * 128)
            pTr = Ring(A, 4, 512)
            sacc = [Ring(A, 2, 512), Ring(A, 2, 512)]
            rr = Ring(A, 4, 512)
            ot = Ring(A, 2, 512)
            ck_t = Ring(A, 2, 128)
            neglam = sm[:, 24 + slot:25 + slot]
            subs = sm[:, 26 + slot:27 + slot]
            vva = vv.v3(NKT, 128)
            qbi = 0
            for s in range(nseq):
                tb = s * L
                for hd in range(8):
                    kb = 0
                    dma(qT.ap(), S_q[hd * 128:(hd + 1) * 128, tb:tb + L], [db("qk", hd, bi) for bi in range(nblk)], qT.b())
                    if sample:
                        kb = 256
                        for q in range(2):
                            ct = ck_t.next()
                            dma(ct.ap(0, 128), I["ck"][slot, q * 128:(q + 1) * 128, hd * 128:(hd + 1) * 128], [], ct.b())
                            P.op("pe", lambda e, ct=ct: e.transpose(psum[1][:, 0:128], ct.ap(0, 128), ident), reads=ct.b() + CM, writes=[pb[1]])
                            cp(kT.ap(q * 128, (q + 1) * 128), psum[1][:, 0:128], [pb[1]], kT.b(q * 128, (q + 1) * 128))
                            dma(vva[:, q, :], I["cv"][slot, q * 128:(q + 1) * 128, hd * 128:(hd + 1) * 128], [], vv.b(q * 128, (q + 1) * 128))
                    if sample:
                        for r_ in range(2):
                            k0_ = kb + r_ * LH
                            ga, gb = gathered(S_kvG, 2 * D, "kvG", r_, hd * 128, 128)
                            dma(kT.ap(k0_, k0_ + LH), ga, gb, kT.b(k0_, k0_ + LH))
                            for q in range(LH // 128):
                                kq = k0_ // 128 + q
                                ga, gb = gathered(S_kvG, 2 * D, "kvG", r_, D + q * 128, 128)
                                dma(vva[:, kq, :], ga[:, hd * 128:(hd + 1) * 128], gb, vv.b(kq * 128, (kq + 1) * 128))
                    else:
                        dma(kT.ap(kb, kb + L), S_kv[hd * 128:(hd + 1) * 128, tb:tb + L], [db("qk", 8 + hd, bi) for bi in range(nblk)], kT.b(kb, kb + L))
                        for q in range(L // 128):
                            tok = tb + q * 128
                            src = O["nv"][tok // LP, slot, tok % LP:tok % LP + 128, hd * 128:(hd + 1) * 128]
                            kq = kb // 128 + q
                            dma(vva[:, kq, :], src, [db("v", tok // 128)], vv.b(kq * 128, (kq + 1) * 128))
                    for qb0 in range(0, L, QB):
                        par = qbi % 2
                        qbi += 1
                        sa = [sacc[0].next(), sacc[1].next()]
                        accb = [4 + 2 * par, 5 + 2 * par]

                        def score(kt, mp):
                            sb = 2 * (kt % 2) + mp
                            mm(psum[sb][:, 0:QB], kT.arena.t[mp * 64:(mp + 1) * 64, kT.off + kt * 128:kT.off + (kt + 1) * 128],
                               qT.arena.t[mp * 64:(mp + 1) * 64, qT.off + qb0:qT.off + qb0 + QB],
                               True, True, kT.b(kt * 128, (kt + 1) * 128) + qT.b(qb0, qb0 + QB), [pb[sb]])

                        score(0, 0); score(0, 1)
                        for kt in range(NKT):
                            pts = []
                            for mp in range(2):
                                sb = 2 * (kt % 2) + mp
                                pt = pTr.next()
                                act(pt.ap(0, QB), psum[sb][:, 0:QB], AF.Exp, [pb[sb]], pt.b(), scale=0.125)
                                pts.append(pt)
                            if kt + 1 < NKT:
                                score(kt + 1, 0); score(kt + 1, 1)
                            for mp in range(2):
                                pt = pts[mp]
                                mm(psum[accb[mp]][:, 0:QB], vva[:, kt, :], pt.ap(0, QB), kt == 0, kt == NKT - 1, vv.b(kt * 128, (kt + 1) * 128) + pt.b(), [pb[accb[mp]]])
                                if kt == 0:
                                    cp(sa[mp].ap(0, QB), pt.ap(0, QB), pt.b(), sa[mp].b(), eng="pool")
                                else:
                                    tt(sa[mp].ap(0, QB), sa[mp].ap(0, QB), pt.ap(0, QB), ALU.add, sa[mp].b() + pt.b(), sa[mp].b(), eng="pool")
                        rc_ = []
                        for mp in range(2):
                            mm(psum[mp][:, 0:QB], ones, sa[mp].ap(0, QB), True, True, CM + sa[mp].b(), [pb[mp]])
                            r = rr.next()
                            recip(r.ap(0, QB), psum[mp][:, 0:QB], [pb[mp]], r.b())
                            rc_.append(r)
                        ts(rc_[1].ap(0, QB), rc_[1].ap(0, QB), neglam, None, ALU.mult, None, rc_[1].b() + SM, rc_[1].b())
                        o = ot.next()
                        tt(o.ap(0, QB), psum[accb[0]][:, 0:QB], rc_[0].ap(0, QB), ALU.mult, [pb[accb[0]]] + rc_[0].b(), o.b())
                        tt(rc_[1].ap(0, QB), psum[accb[1]][:, 0:QB], rc_[1].ap(0, QB), ALU.mult, [pb[accb[1]]] + rc_[1].b(), rc_[1].b())
                        tt(o.ap(0, QB), o.ap(0, QB), rc_[1].ap(0, QB), ALU.add, o.b() + rc_[1].b(), o.b())
                        sq_ = rr.next()
                        act(sq_.ap(0, QB), o.ap(0, QB), AF.Square, o.b(), sq_.b())
                        mm(psum[0][:, 0:QB], ones, sq_.ap(0, QB), True, True, CM + sq_.b(), [pb[0]])
                        act(sq_.ap(0, QB), psum[0][:, 0:QB], AF.Sqrt, [pb[0]] + SM, sq_.b(), bias=epsc, scale=1.0 / 128)
                        recip(sq_.ap(0, QB), sq_.ap(0, QB), sq_.b(), sq_.b())
                        stt(o.ap(0, QB), o.ap(0, QB), subs, sq_.ap(0, QB), ALU.mult, ALU.mult, o.b() + SM + sq_.b(), o.b())
                        for b5 in range(qb0, qb0 + QB, 128):
                            pass
                        dma(S_o[hd * 128:(hd + 1) * 128, tb + qb0:tb + qb0 + QB], o.ap(0, QB), o.b(), [db("o", hd, (tb + qb0) // 512, (tb + qb0) % 512)], q=STQ)
            A.release(m)
            m = A.mark()
            RG["wsm"] = Ring(A, 4, 1024)
            oTb = A.alloc(8 * 512)
            Wo = I["att_out_w"][slot]
            for bi, b0 in enumerate(range(0, T, 512)):
                for hd in range(8):
                    dma(oTb.ap(hd * 512, (hd + 1) * 512), S_o[hd * 128:(hd + 1) * 128, b0:b0 + 512], [db("o", hd, bi, off_) for off_ in range(0, 512, min(512, L))], oTb.b(hd * 512, (hd + 1) * 512))
                for oc in range(8):
                    w = load_wcols(Wo, oc * 128, 128)
                    bank = 2 + oc % 2
                    for k in range(8):
                        mm(psum[bank][:, :], w.v3(8, 128)[:, k, :], oTb.ap(k * 512, (k + 1) * 512), k == 0, k == 7, w.b() + oTb.b(k * 512, (k + 1) * 512), [pb[bank]])
                    residual_update(l, 16, c, oc, b0, 512, bank)
            A.release(m)

        groups = []
        if do_sample:
            groups.append(dict(sample=True, shard=True, packed=False, c=0, T=LH, L=LH, Lf=LS, nseq=1, src=I["xs_in"], dst=O["y_s"]))
        if do_prompt:
            groups.append(dict(sample=False, shard=False, packed=True, c=1, T=2 * LP, L=LP, Lf=LP, nseq=2, src=I["xp_in"], dst=O["y_p"]))
        for G in groups:
            load_x(G["src"], G["T"])
            for l in range(depth_run):
                if l % 2 == 0:
                    ssd_layer(l, G)
                else:
                    attn_layer(l, G)
                ffn_layer(l, G)
            final_out(G["dst"], G["T"])
        P.emit(out_events)
    return nc


def _fm(v, nchunk):
    return np.ascontiguousarray(np.asarray(v, np.float32).reshape(nchunk, 128).T)


def _consts():
    a = np.arange(128)
    ident = np.eye(128, dtype=np.float32)
    ones = np.ones((128, 128), np.float32)
    mT = (a[:, None] <= a[None, :]).astype(np.float32)
    mL = (a[:, None] >= a[None, :]).astype(np.float32)
    mTs = (a[:, None] < a[None, :]).astype(np.float32)
    mLs = (a[:, None] > a[None, :]).astype(np.float32)
    i = a % 32
    partner = np.where(i < 16, a + 16, a - 16)
    prot = np.zeros((128, 128), np.float32)
    prot[partner, a] = 1.0
    cmat = np.stack([ident, ones, mT, mL, mTs, mLs, prot], axis=1).reshape(128, 7 * 128)
    t = np.arange(LS)
    row = (t // 64).astype(np.float32)
    col = (t % 64).astype(np.float32)
    inv = (1.0 / (np.float32(10000.0) ** (np.arange(0, 32, 2, dtype=np.float32) / np.float32(32)))).astype(np.float32)
    dd = a % 64
    axis_col = dd >= 32
    f = i % 16
    pos = np.where(axis_col[:, None], col[None, :], row[None, :]).astype(np.float32)
    ang = (pos * inv[f][:, None]).astype(np.float32)
    C = np.cos(ang).astype(np.float32)
    S = np.sin(ang).astype(np.float32)
    S = np.where((i < 16)[:, None], -S, S).astype(np.float32)
    return np.ascontiguousarray(cmat), np.ascontiguousarray(C), np.ascontiguousarray(S)


_CACHE = {}


def kernel(**inp):
    f = lambda k: np.asarray(inp[k], np.float32)
    key = (DEPTH_RUN, DO_SAMPLE, DO_PROMPT)
    if key not in _CACHE:
        _CACHE[key] = build_program(DEPTH_RUN, DO_SAMPLE, DO_PROMPT)
    nc = _CACHE[key]
    cmat, rC, rS = _consts()
    shared = {
        "mod_w": f("mod_w"),
        "mod_bT": np.ascontiguousarray(np.concatenate([_fm(f("mod_b")[l], 48) for l in range(4)], axis=1)),
        "nmg": np.ascontiguousarray(np.concatenate([_fm(f("norm_mix_g")[l], 8) for l in range(4)], axis=1)),
        "nfg": np.ascontiguousarray(np.concatenate([_fm(f("norm_ffn_g")[l], 8) for l in range(4)], axis=1)),
        "fng": _fm(f("final_norm_g"), 8),
        "ssd_in_w": f("ssd_in_w"),
        "ssd_cw": np.ascontiguousarray(f("ssd_conv_w").reshape(2, 3, 24, 128).transpose(3, 0, 2, 1).reshape(128, 2 * 24 * 3)),
        "ssd_cb": np.ascontiguousarray(f("ssd_conv_b").reshape(2, 24, 128).transpose(2, 0, 1).reshape(128, 48)),
        "dtb": np.ascontiguousarray(f("ssd_dt_bias").reshape(2, 64).T),
        "alog": np.ascontiguousarray(f("ssd_a_log").reshape(2, 64).T),
        "dsk": np.ascontiguousarray(np.broadcast_to(f("ssd_d").reshape(1, 64), (128, 64))),
        "sng": np.ascontiguousarray(np.concatenate([_fm(f("ssd_norm_g")[s], 16) for s in range(2)], axis=1)),
        "ssd_out_w": f("ssd_out_w"),
        "att_qkv_w": f("att_qkv_w"),
        "lamb": np.ascontiguousarray(np.broadcast_to(f("att_lambda").reshape(1, 512), (128, 512))),
        "sub": np.ascontiguousarray(f("att_subln_g").T),
        "att_out_w": f("att_out_w"),
        "ffn_up_w": f("ffn_up_w"),
        "fcw": np.ascontiguousarray(f("ffn_conv_w").reshape(4, 3, 44, 128).transpose(3, 0, 2, 1).reshape(128, 4 * 44 * 3)),
        "fcb": np.ascontiguousarray(f("ffn_conv_b").reshape(4, 44, 128).transpose(2, 0, 1).reshape(128, 4 * 44)),
        "ffn_down_w": f("ffn_down_w"),
        "cmat": cmat,
    }
    def dirswap(w):
        w2 = w.copy()
        w2[..., 5120:5152] = w[..., 5152:5184]
        w2[..., 5152:5184] = w[..., 5120:5152]
        return w2
    shared_odd = dict(shared)
    shared_odd["ssd_in_w"] = dirswap(f("ssd_in_w"))
    shared_odd["dtb"] = np.ascontiguousarray(f("ssd_dt_bias")[:, ::-1].reshape(2, 64).T)
    shared_odd["alog"] = np.ascontiguousarray(f("ssd_a_log")[:, ::-1].reshape(2, 64).T)
    shared_odd["ssd_cw"] = np.ascontiguousarray(f("ssd_conv_w")[:, ::-1].reshape(2, 3, 24, 128).transpose(3, 0, 2, 1).reshape(128, 2 * 24 * 3))
    shared_odd["fcw"] = np.ascontiguousarray(f("ffn_conv_w")[:, ::-1].reshape(4, 3, 44, 128).transpose(3, 0, 2, 1).reshape(128, 4 * 44 * 3))
    xs, xp = f("x_sample"), f("x_prompt")
    st, ck, cv, cc, cctx = f("state_ssd"), f("cache_k"), f("cache_v"), f("c"), f("c_ctx")
    in_maps = []
    for core in range(NCORES):
        b, r = core // 2, core % 2
        mp = dict(shared_odd if r else shared)
        fl = (lambda a, ax: np.flip(a, axis=ax)) if r else (lambda a, ax: a)
        mp["xs_in"] = np.ascontiguousarray(fl(xs[b, r * LH:(r + 1) * LH], 0))
        mp["ropeC"] = np.ascontiguousarray(fl(rC[:, r * LH:(r + 1) * LH], 1))
        mp["ropeS"] = np.ascontiguousarray(fl(rS[:, r * LH:(r + 1) * LH], 1))
        mp["rmask"] = np.ascontiguousarray(np.broadcast_to(np.array([[1.0 - r, float(r)]], np.float32), (128, 2)))
        mp["xp_in"] = np.ascontiguousarray(fl(xp[2 * core:2 * core + 2], 1).reshape(2 * LP, D))
        mp["state"] = np.ascontiguousarray(fl(st[b], 1).reshape(2, 2, 16, 128, 128))
        mp["ck"] = np.ascontiguousarray(ck[b].reshape(2, 256, D))
        mp["cv"] = np.ascontiguousarray(cv[b].reshape(2, 256, D))
        cm2 = np.stack([_fm(cc[b], 8), _fm(cctx, 8)], axis=2).reshape(128, 16)
        mp["cmod"] = np.ascontiguousarray(cm2)
        in_maps.append(mp)
    res = run_bass_kernel_spmd(nc, in_maps, core_ids=list(range(NCORES)))
    R = res.results
    def flo(c, a, ax):
        return np.flip(a, axis=ax) if c % 2 else a
    y_prompt = np.concatenate([flo(c, R[c]["y_p"].reshape(2, LP, D), 1) for c in range(NCORES)], axis=0)
    y_sample = np.stack([np.concatenate([R[2 * b]["y_s"], np.flip(R[2 * b + 1]["y_s"], axis=0)], axis=0) for b in range(4)], axis=0)
    nstate = np.concatenate([flo(c, R[c]["nstate"].reshape(2, 2, 2, 32, 64, 128), 2) for c in range(NCORES)], axis=0)
    nk = np.concatenate([flo(c, R[c]["nk"].reshape(2, 2, LP, 8, 2, 64), 2) for c in range(NCORES)], axis=0)
    nv = np.concatenate([flo(c, R[c]["nv"].reshape(2, 2, LP, 8, 128), 2) for c in range(NCORES)], axis=0)
    return (y_prompt.astype(np.float32), y_sample.astype(np.float32), nstate.astype(np.float32),
            nk.astype(np.float32), nv.astype(np.float32))
```

```python
import math
import numpy as np
from contextlib import ExitStack
import concourse.bass as bass
import concourse.mybir as mybir
from concourse.bass_utils import run_bass_kernel_spmd

F32 = mybir.dt.float32
AF = mybir.ActivationFunctionType
ALU = mybir.AluOpType
AX = mybir.AxisListType

DEPTH_RUN = 4
DO_SAMPLE = True
DO_PROMPT = True
NCORES = 8
EPS = 1e-6
D = 1024
LS = 2048
LH = 1024
PAIRS = [[0, 1], [2, 3], [4, 5], [6, 7]]
LP = 256
FH = 2816
NPAGES = 93
STQ = "act"


class Buf:
    __slots__ = ("name", "lw", "rd")

    def __init__(self, name=""):
        self.name = name
        self.lw = None
        self.rd = {}


class Prog:
    ENG = ["pe", "act", "dve", "pool", "sp"]
    NDMA = 16
    DMA_POOLS = {"sp": (0, 16), "act": (0, 16), "pool": (0, 16)}
    SHARED_POOL = True
    SAME_ENGINE_SYNC = True

    def __init__(self, nc):
        self.nc = nc
        self.streams = {e: [] for e in self.ENG}
        self.cnt = {e: 0 for e in self.ENG}
        self.seen = {e: {} for e in self.ENG}
        self.ndma = {q: 0 for q in self.DMA_POOLS}
        self.ncc = 0

    def op(self, eng, fn, reads=(), writes=(), dma=False, cc=False):
        deps = {}

        def add(ev):
            if ev is None:
                return
            k, v = ev
            if v > deps.get(k, 0):
                deps[k] = v

        for b in reads:
            add(b.lw)
        for b in writes:
            add(b.lw)
            for kv in b.rd.items():
                add(kv)
        if dma:
            base, npool = self.DMA_POOLS[eng]
            qk = "sp" if self.SHARED_POOL else eng
            i = self.ndma[qk]
            self.ndma[qk] += 1
            slot = base + i % npool
            val = 16 * (i // npool + 1)
            ev = (("dma", slot), val)
            if val > 16:
                add((("dma", slot), val - 16))
        elif cc:
            self.ncc += 1
            ev = ("cc", self.ncc)
        else:
            self.cnt[eng] += 1
            ev = (eng, self.cnt[eng])
        waits = []
        seen = self.seen[eng]
        for k, v in deps.items():
            if k == eng and (eng == "pe" or not self.SAME_ENGINE_SYNC):
                continue
            if seen.get(k, 0) >= v:
                continue
            seen[k] = v
            waits.append((k, v))
        self.streams[eng].append((fn, waits, ev))
        k, v = ev
        for b in reads:
            if b.rd.get(k, 0) < v:
                b.rd[k] = v
        for b in writes:
            b.lw = ev
            b.rd = {}
        return ev

    def emit(self, final_events=()):
        nc = self.nc
        with ExitStack() as es:
            sems = {}
            for e in self.ENG:
                sems[e] = es.enter_context(nc.semaphore("s_" + e))
            for i in range(self.NDMA):
                sems[("dma", i)] = es.enter_context(nc.semaphore("s_dma%d" % i))
            sems["cc"] = es.enter_context(nc.semaphore("s_cc"))
            block = es.enter_context(nc.Block())
            streams = self.streams

            def run(engname, eng):
                for fn, waits, ev in streams[engname]:
                    for k, v in waits:
                        eng.wait_ge(sems[k], v)
                    inst = fn(eng)
                    k, v = ev
                    inst.then_inc(sems[k], 16 if isinstance(k, tuple) else 1)
                if engname == "sp":
                    fe = {}
                    for k, v in final_events:
                        fe[k] = max(fe.get(k, 0), v)
                    for k, v in fe.items():
                        eng.wait_ge(sems[k], v)

            @block.tensor
            def _(eng):
                run("pe", eng)

            @block.scalar
            def _(eng):
                run("act", eng)

            @block.vector
            def _(eng):
                run("dve", eng)

            @block.gpsimd
            def _(eng):
                run("pool", eng)

            @block.sync
            def _(eng):
                run("sp", eng)


class Tile:
    def __init__(self, arena, off, n):
        self.arena = arena
        self.off = off
        self.n = n

    def ap(self, lo=0, hi=None):
        hi = self.n if hi is None else hi
        return self.arena.t[:, self.off + lo:self.off + hi]

    def v3(self, a, b, lo=0):
        return self.arena.t[:, self.off + lo:self.off + lo + a * b].rearrange("p (a b) -> p a b", a=a, b=b)

    def b(self, lo=0, hi=None):
        hi = self.n if hi is None else hi
        p0 = (self.off + lo) // 512
        p1 = (self.off + hi - 1) // 512
        return self.arena.pages[p0:p1 + 1]


class Arena:
    def __init__(self, nc, es, npages):
        self.t = es.enter_context(nc.sbuf_tensor("arena", [128, npages * 512], F32))
        self.pages = [Buf("pg%d" % i) for i in range(npages)]
        self.top = 0
        self.npages = npages

    def alloc(self, nelem):
        npg = (nelem + 511) // 512
        assert self.top + npg <= self.npages, ("arena overflow", self.top, npg, self.npages)
        t = Tile(self, self.top * 512, nelem)
        self.top += npg
        return t

    def mark(self):
        return self.top

    def release(self, m):
        self.top = m


class Ring:
    def __init__(self, arena, n, nelem):
        self.tiles = [arena.alloc(nelem) for _ in range(n)]
        self.i = 0

    def next(self):
        t = self.tiles[self.i % len(self.tiles)]
        self.i += 1
        return t


def halo_blocks(nseq, L, maxn=510):
    out = []
    if L > maxn:
        nb = -(-L // maxn)
        base = -(-L // nb)
        for s in range(nseq):
            t = 0
            while t < L:
                n = min(base, L - t)
                out.append((s * L + t, n, t > 0, t + n < L))
                t += n
    else:
        for s in range(nseq):
            out.append((s * L, L, False, False))
    return out


def build_program(depth_run, do_sample, do_prompt):
    nc = bass.Bass("TRN2", target_bir_lowering=False)
    es = ExitStack()

    def din(name, shape):
        return nc.dram_tensor(name, list(shape), F32, kind="ExternalInput").ap()

    def dout(name, shape):
        return nc.dram_tensor(name, list(shape), F32, kind="ExternalOutput").ap()

    def dscr(name, shape):
        return nc.dram_tensor(name, list(shape), F32, kind="Internal").ap()

    I = {}
    I["xs_in"] = din("xs_in", [LH, D])
    I["rmask"] = din("rmask", [128, 2])
    I["xp_in"] = din("xp_in", [2 * LP, D])
    I["state"] = din("state", [2, 2, 16, 128, 128])
    I["ck"] = din("ck", [2, 256, D])
    I["cv"] = din("cv", [2, 256, D])
    I["cmod"] = din("cmod", [128, 16])
    I["mod_w"] = din("mod_w", [4, D, 6 * D])
    I["mod_bT"] = din("mod_bT", [128, 4 * 48])
    I["nmg"] = din("nmg", [128, 32])
    I["nfg"] = din("nfg", [128, 32])
    I["fng"] = din("fng", [128, 8])
    I["ssd_in_w"] = din("ssd_in_w", [2, D, 5184])
    I["ssd_cw"] = din("ssd_cw", [128, 2 * 24 * 3])
    I["ssd_cb"] = din("ssd_cb", [128, 2 * 24])
    I["dtb"] = din("dtb", [64, 2])
    I["alog"] = din("alog", [64, 2])
    I["dsk"] = din("dsk", [128, 64])
    I["sng"] = din("sng", [128, 32])
    I["ssd_out_w"] = din("ssd_out_w", [2, 2048, D])
    I["att_qkv_w"] = din("att_qkv_w", [2, D, 3 * D])
    I["lamb"] = din("lamb", [128, 2 * 256])
    I["sub"] = din("sub", [128, 2])
    I["att_out_w"] = din("att_out_w", [2, D, D])
    I["ffn_up_w"] = din("ffn_up_w", [4, D, 2 * FH])
    I["fcw"] = din("fcw", [128, 4 * 44 * 3])
    I["fcb"] = din("fcb", [128, 4 * 44])
    I["ffn_down_w"] = din("ffn_down_w", [4, FH, D])
    I["cmat"] = din("cmat", [128, 7 * 128])
    I["ropeC"] = din("ropeC", [128, LH])
    I["ropeS"] = din("ropeS", [128, LH])
    O = {}
    O["y_s"] = dout("y_s", [LH, D])
    O["y_p"] = dout("y_p", [2 * LP, D])
    O["nstate"] = dout("nstate", [2, 2, 2, 16, 128, 128])
    O["nk"] = dout("nk", [2, 2, LP, D])
    O["nv"] = dout("nv", [2, 2, LP, D])
    S_proj = dscr("s_proj", [5248, LH])
    S_projG = dscr("s_projg", [2 * 5248, LH])
    S_y = dscr("s_y", [LS, 2048])
    S_q = dscr("s_q", [D, LH])
    S_kv = dscr("s_kv", [2 * D, LH])
    S_kvG = dscr("s_kvg", [4 * D, LH])
    S_o = dscr("s_o", [D, LH])
    S_hst = dscr("s_hst", [2048, 128])
    S_hstG = dscr("s_hstg", [4096, 128])
    H_loc = dscr("h_loc", [128, 16])
    H_all = dscr("h_all", [256, 16])
    dbufs = {}

    def db(*key):
        if key not in dbufs:
            dbufs[key] = Buf(str(key))
        return dbufs[key]

    with es:
        P = Prog(nc)
        A = Arena(nc, es, NPAGES)
        psum = [es.enter_context(nc.psum_tensor("ps%d" % i, [128, 512], F32)) for i in range(8)]
        pb = [Buf("psum%d" % i) for i in range(8)]
        out_events = []

        def mm(out, lhsT, rhs, start, stop, rd, wr):
            P.op("pe", lambda e: e.matmul(out, lhsT=lhsT, rhs=rhs, start=start, stop=stop), reads=rd, writes=wr)

        def act(out, in_, func, rd, wr, bias=None, scale=None, eng="act"):
            kw = {}
            if bias is not None:
                kw["bias"] = bias
            if scale is not None:
                kw["scale"] = scale
            P.op("act", lambda e: e.activation(out=out, in_=in_, func=func, **kw), reads=rd, writes=wr)

        def tt(out, in0, in1, op, rd, wr, eng="dve"):
            P.op(eng, lambda e: e.tensor_tensor(out=out, in0=in0, in1=in1, op=op), reads=rd, writes=wr)

        def ts(out, in0, s1, s2, op0, op1, rd, wr, eng="dve"):
            if op1 is None:
                P.op(eng, lambda e: e.tensor_scalar(out=out, in0=in0, scalar1=s1, scalar2=None, op0=op0), reads=rd, writes=wr)
            else:
                P.op(eng, lambda e: e.tensor_scalar(out=out, in0=in0, scalar1=s1, scalar2=s2, op0=op0, op1=op1), reads=rd, writes=wr)

        def stt(out, in0, scalar, in1, op0, op1, rd, wr):
            P.op("dve", lambda e: e.scalar_tensor_tensor(out=out, in0=in0, scalar=scalar, in1=in1, op0=op0, op1=op1), reads=rd, writes=wr)

        def cp(out, in_, rd, wr, eng="dve"):
            P.op(eng, lambda e: e.tensor_copy(out=out, in_=in_), reads=rd, writes=wr)

        def mset(ap, val, wr, eng="pool"):
            P.op(eng, lambda e: e.memset(ap, val), writes=wr)

        def recip(out, in_, rd, wr):
            P.op("dve", lambda e: e.reciprocal(out=out, in_=in_), reads=rd, writes=wr)

        def dma(out, in_, rd, wr, final=False, q="sp"):
            ev = P.op(q, lambda e: e.dma_start(out=out, in_=in_), reads=rd, writes=wr, dma=True)
            if final:
                out_events.append(ev)
            return ev

        def allgather(inp, outp, rd, wr):
            return P.op("pool", lambda e: e.collective_compute("AllGather", ALU.bypass, replica_groups=PAIRS, ins=[inp], outs=[outp]), reads=rd, writes=wr, cc=True)

        CCR = 512

        def allgather_rows(src, dst, nrows, rd_fn, key):
            for i, b0 in enumerate(range(0, nrows, CCR)):
                b1 = min(nrows, b0 + CCR)
                allgather(src[b0:b1, :], dst[2 * b0:2 * b1, :], rd_fn(b0, b1), [db(key, i)])

        def gathered(dst, nrows, key, r_, row0, nr):
            i = row0 // CCR
            b0 = i * CCR
            b1 = min(nrows, b0 + CCR)
            assert row0 + nr <= b1
            base = 2 * b0 + r_ * (b1 - b0) + (row0 - b0)
            return dst[base:base + nr, :], [db(key, i)]

        cm = A.alloc(7 * 128)
        dma(cm.ap(), I["cmat"], [], cm.b())
        cmv = cm.v3(7, 128)
        ident, ones, mT, mL, mTs, mLs, prot = [cmv[:, i, :] for i in range(7)]
        CM = cm.b()
        small = A.alloc(1024)
        SM = small.b()
        sm = small.ap()
        mset(sm[:, 0:1], EPS, SM)
        mset(sm[:, 1:2], 1.0, SM)
        epsc = sm[:, 0:1]
        dma(sm[:, 8:16], I["fng"], [], SM)
        dma(sm[:, 16:18], I["sub"], [], SM)
        dma(sm[0:64, 18:20], I["dtb"], [], SM)
        dma(sm[0:64, 20:22], I["alog"], [], SM)
        act(sm[0:64, 20:22], sm[0:64, 20:22], AF.Exp, SM, SM)
        ts(sm[0:64, 20:22], sm[0:64, 20:22], -1.0, None, ALU.mult, None, SM, SM)
        dma(sm[:, 64:128], I["dsk"], [], SM)
        dma(sm[:, 32:34], I["rmask"], [], SM)
        mA = sm[:, 32:33]
        mB = sm[:, 33:34]
        lam_init = [0.8 - 0.6 * math.exp(-0.3 * 1), 0.8 - 0.6 * math.exp(-0.3 * 3)]
        lw = A.alloc(512)
        dma(lw.ap(), I["lamb"], [], lw.b())
        lwv = lw.v3(2, 256)
        for sl in range(2):
            for j in range(2):
                tt(lwv[:, sl, j * 128:j * 128 + 64], lwv[:, sl, j * 128:j * 128 + 64], lwv[:, sl, j * 128 + 64:j * 128 + 128], ALU.mult, lw.b(), lw.b())
                P.op("dve", lambda e, sl=sl, j=j: e.tensor_reduce(out=sm[:, 28 + 2 * sl + j:29 + 2 * sl + j], in_=lwv[:, sl, j * 128:j * 128 + 64], axis=AX.X, op=ALU.add), reads=lw.b(), writes=SM)
            act(sm[:, 28 + 2 * sl:30 + 2 * sl], sm[:, 28 + 2 * sl:30 + 2 * sl], AF.Exp, SM, SM)
            tt(sm[:, 24 + sl:25 + sl], sm[:, 28 + 2 * sl:29 + 2 * sl], sm[:, 29 + 2 * sl:30 + 2 * sl], ALU.subtract, SM, SM)
            ts(sm[:, 24 + sl:25 + sl], sm[:, 24 + sl:25 + sl], lam_init[sl], -1.0, ALU.add, ALU.mult, SM, SM)
            ts(sm[:, 26 + sl:27 + sl], sm[:, 16 + sl:17 + sl], 1.0 - lam_init[sl], None, ALU.mult, None, SM, SM)
        vec = A.alloc(32 + 32 + 32 + 144 + 48 + 528 + 176)
        VB = vec.b()
        va = vec.ap()
        o_ = 0
        nmg = va[:, 0:32]; nfg = va[:, 32:64]; sng = va[:, 64:96]
        scw = va[:, 96:240]; scb = va[:, 240:288]; fcw = va[:, 288:816]; fcb = va[:, 816:992]
        for ap_, nm in ((nmg, "nmg"), (nfg, "nfg"), (sng, "sng"), (scw, "ssd_cw"), (scb, "ssd_cb"), (fcw, "fcw"), (fcb, "fcb")):
            dma(ap_, I[nm], [], VB)
        modv = A.alloc(4 * 48 * 2)
        MB = modv.b()
        mv = modv.ap().rearrange("p (l f c) -> p l f c", l=4, f=48, c=2)
        mbt = A.alloc(4 * 48)
        dma(mbt.ap(), I["mod_bT"], [], mbt.b())
        mbv = mbt.v3(4, 48)
        cmod = A.alloc(16)
        dma(cmod.ap(), I["cmod"], [], cmod.b())
        act(cmod.ap(), cmod.ap(), AF.Silu, cmod.b(), cmod.b())
        cmv2 = cmod.v3(8, 2)
        xT = A.alloc(8 * LH)
        phase_mark = A.mark()
        RG = {}

        def xTc(k, t0, t1):
            return xT.ap(k * LH + t0, k * LH + t1), xT.b(k * LH + t0, k * LH + t1)

        mM = A.mark()
        RG["wsm"] = Ring(A, 4, 1024)
        for l in range(depth_run):
            for f in range(48):
                w = RG["wsm"].next()
                dma(w.v3(8, 128), I["mod_w"][l, :, f * 128:(f + 1) * 128].rearrange("(k p) n -> p k n", p=128), [], w.b())
                bank = f % 2
                for k in range(8):
                    mm(psum[bank][:, 0:2], w.v3(8, 128)[:, k, :], cmv2[:, k, :], k == 0, k == 7, w.b() + cmod.b(), [pb[bank]])
                ts(mv[:, l, f, :], psum[bank][:, 0:2], mbv[:, l, f:f + 1], None, ALU.add, None, [pb[bank]] + mbt.b(), MB)
        for l in range(depth_run):
            for c in range(2):
                stt(mv[:, l, 8:16, c], mv[:, l, 8:16, c], 1.0, nmg[:, l * 8:(l + 1) * 8], ALU.add, ALU.mult, MB + VB, MB)
                stt(mv[:, l, 32:40, c], mv[:, l, 32:40, c], 1.0, nfg[:, l * 8:(l + 1) * 8], ALU.add, ALU.mult, MB + VB, MB)

        A.release(mM)
        sqr = Ring(A, 2, 512)
        rstd_t = A.alloc(512)
        phase_mark = A.mark()

        def compute_rstd(t0, ncol, bank):
            for k in range(8):
                s = sqr.next()
                xa, xb = xTc(k, t0, t0 + ncol)
                act(s.ap(0, ncol), xa, AF.Square, xb, s.b())
                mm(psum[bank][:, :ncol], ones, s.ap(0, ncol), k == 0, k == 7, s.b() + CM, [pb[bank]])
            act(rstd_t.ap(0, ncol), psum[bank][:, :ncol], AF.Sqrt, [pb[bank]] + SM, rstd_t.b(), bias=epsc, scale=1.0 / D)
            recip(rstd_t.ap(0, ncol), rstd_t.ap(0, ncol), rstd_t.b(), rstd_t.b())

        def make_hs(hs, t0, ncol, l, gidx, sidx, c, bank, col_off=0):
            compute_rstd(t0, ncol, bank)
            W = hs.n // 8
            for k in range(8):
                xa, xb = xTc(k, t0, t0 + ncol)
                o = hs.ap(k * W + col_off, k * W + col_off + ncol)
                ob = hs.b(k * W + col_off, k * W + col_off + ncol)
                stt(o, xa, mv[:, l, gidx + k, c:c + 1], rstd_t.ap(0, ncol), ALU.mult, ALU.mult, xb + MB + rstd_t.b(), ob)
                act(o, o, AF.Identity, ob + MB, ob, bias=mv[:, l, sidx + k, c:c + 1])

        def load_wcols(Wd, col0, ncol):
            w = RG["wsm"].next()
            dma(w.v3(8, 128)[:, :, 0:ncol], Wd[:, col0:col0 + ncol].rearrange("(k p) n -> p k n", p=128), [], w.b())
            return w

        def proj_fm(hs, Wd, col0, ncol, c_lo, c_hi, bank):
            w = load_wcols(Wd, col0, ncol)
            Wn = hs.n // 8
            for k in range(8):
                mm(psum[bank][0:ncol, c_lo:c_hi], w.v3(8, 128)[:, k, 0:ncol], hs.ap(k * Wn + c_lo, k * Wn + c_hi), k == 0, k == 7,
                   w.b() + hs.b(k * Wn + c_lo, k * Wn + c_hi), [pb[bank]])

        def conv3(dst, dstb, u, n, cw3, cbias, ub, npart=128):
            ts(dst, u[:, 0:n], cw3[:, 0:1], cbias, ALU.mult, ALU.add, ub + VB, dstb)
            stt(dst, u[:, 1:n + 1], cw3[:, 1:2], dst, ALU.mult, ALU.add, ub + VB + dstb, dstb)
            stt(dst, u[:, 2:n + 2], cw3[:, 2:3], dst, ALU.mult, ALU.add, ub + VB + dstb, dstb)

        def residual_update(l, gate_idx, c, oc, t0, ncol, bank):
            xa, xb = xTc(oc, t0, t0 + ncol)
            stt(xa, psum[bank][:, 0:ncol], mv[:, l, gate_idx + oc, c:c + 1], xa, ALU.mult, ALU.add, [pb[bank]] + MB + xb, xb)

        def edge_halos(l, G, gidx, sidx, c):
            edge = A.alloc(16)
            if not G["shard"]:
                mset(edge.ap(), 0.0, edge.b(), eng="pool")
                return edge
            T = G["T"]
            h2 = A.alloc(16); h3 = A.alloc(16); el = A.alloc(16); e0 = A.alloc(16); e1 = A.alloc(16)
            make_hs(h2, 0, 2, l, gidx, sidx, c, 0)
            make_hs(h3, T - 2, 2, l, gidx, sidx, c, 0)
            cp(el.ap(0, 8).unsqueeze(2), h2.v3(8, 2)[:, :, 0:1], h2.b(), el.b())
            cp(el.ap(8, 16).unsqueeze(2), h3.v3(8, 2)[:, :, 1:2], h3.b(), el.b())
            dma(H_loc, el.ap(), el.b(), [db("hloc")], q=STQ)
            allgather(H_loc, H_all, [db("hloc")], [db("hall")])
            dma(e0.ap(), H_all[0:128, :], [db("hall")], e0.b())
            dma(e1.ap(), H_all[128:256, :], [db("hall")], e1.b())
            mset(edge.ap(0, 8), 0.0, edge.b(), eng="pool")
            ts(e0.ap(8, 16), e0.ap(8, 16), mB, None, ALU.mult, None, e0.b() + SM, e0.b())
            stt(edge.ap(8, 16), e1.ap(8, 16), mA, e0.ap(8, 16), ALU.mult, ALU.add, e1.b() + SM + e0.b(), edge.b())
            return edge

        def load_x(src, T):
            m = A.mark()
            ring = Ring(A, 2, 1024)
            for q in range(T // 128):
                t = ring.next()
                dma(t.ap(), src[q * 128:(q + 1) * 128, :], [], t.b())
                for k in range(8):
                    bank = k % 2
                    P.op("pe", lambda e, bank=bank, t=t, k=k: e.transpose(psum[bank][:, 0:128], t.ap(k * 128, (k + 1) * 128), ident), reads=t.b() + CM, writes=[pb[bank]])
                    xa, xb = xTc(k, q * 128, (q + 1) * 128)
                    act(xa, psum[bank][:, 0:128], AF.Copy, [pb[bank]], xb)
            A.release(m)

        def final_out(dst, T):
            m = A.mark()
            ring = Ring(A, 2, 1024)
            yT = A.alloc(8 * 512)
            for b0 in range(0, T, 512):
                compute_rstd(b0, 512, 0)
                for k in range(8):
                    xa, xb = xTc(k, b0, b0 + 512)
                    stt(yT.ap(k * 512, (k + 1) * 512), xa, sm[:, 8 + k:9 + k], rstd_t.ap(0, 512), ALU.mult, ALU.mult, xb + SM + rstd_t.b(), yT.b(k * 512, (k + 1) * 512))
                for q in range(4):
                    t = ring.next()
                    for k in range(8):
                        bank = 2 + k % 2
                        P.op("pe", lambda e, bank=bank, k=k, q=q: e.transpose(psum[bank][:, 0:128], yT.ap(k * 512 + q * 128, k * 512 + (q + 1) * 128), ident),
                             reads=yT.b(k * 512 + q * 128, k * 512 + (q + 1) * 128) + CM, writes=[pb[bank]])
                        act(t.ap(k * 128, (k + 1) * 128), psum[bank][:, 0:128], AF.Copy, [pb[bank]], t.b(k * 128, (k + 1) * 128))
                    dma(dst[b0 + q * 128:b0 + (q + 1) * 128, :], t.ap(), t.b(), [], final=True, q=STQ)
            A.release(m)

        def ffn_packed(l, G):
            m = A.mark()
            c = G["c"]
            RG["wsm"] = Ring(A, 4, 1024)
            wbig = Ring(A, 2, 22 * 128)
            hs = A.alloc(8 * 512)
            actT = A.alloc(22 * 512)
            ur = Ring(A, 2, 1024)
            tr_ = Ring(A, 3, 1024)
            for u in ur.tiles:
                mset(u.ap(0, 516), 0.0, u.b(), eng="pool")
            Wup = I["ffn_up_w"][l]
            Wdn = I["ffn_down_w"][l]
            make_hs(hs, 0, 512, l, 32, 24, c, 0)
            for j in range(22):
                res = []
                for half in range(2):
                    ch = half * 22 + j
                    bank = 2 * half + (j % 2)
                    proj_fm(hs, Wup, half * FH + j * 128, 128, 0, 512, bank)
                    u = ur.next()
                    act(u.ap(1, 257), psum[bank][:, 0:256], AF.Copy, [pb[bank]], u.b())
                    act(u.ap(259, 515), psum[bank][:, 256:512], AF.Copy, [pb[bank]], u.b())
                    t = tr_.next()
                    cw3 = fcw[:, (l * 44 + ch) * 3:(l * 44 + ch) * 3 + 3]
                    conv3(t.ap(0, 514), t.b(), u.ap(0, 516), 514, cw3, fcb[:, l * 44 + ch:l * 44 + ch + 1], u.b())
                    res.append(t)
                ta, tv = res
                act(ta.ap(0, 514), ta.ap(0, 514), AF.Silu, ta.b(), ta.b())
                tt(actT.v3(2, 256, j * 512), ta.v3(2, 258)[:, :, 0:256], tv.v3(2, 258)[:, :, 0:256], ALU.mult, ta.b() + tv.b(), actT.b(j * 512, (j + 1) * 512))
            for oc in range(8):
                w = wbig.next()
                dma(w.v3(22, 128), Wdn[:, oc * 128:(oc + 1) * 128].rearrange("(k p) n -> p k n", p=128), [], w.b())
                bank = 4 + oc % 2
                for j in range(22):
                    mm(psum[bank][:, 0:512], w.v3(22, 128)[:, j, :], actT.ap(j * 512, (j + 1) * 512), j == 0, j == 21, w.b() + actT.b(j * 512, (j + 1) * 512), [pb[bank]])
                residual_update(l, 40, c, oc, 0, 512, bank)
            A.release(m)

        def ffn_layer(l, G):
            if G["packed"]:
                return ffn_packed(l, G)
            m = A.mark()
            c = G["c"]
            blocks = halo_blocks(G["nseq"], G["L"], 342)
            NB = max(b[1] for b in blocks)
            Wn = NB + 2
            RG["wsm"] = Ring(A, 4, 1024)
            wbig = Ring(A, 2, 22 * 128)
            hs = A.alloc(8 * Wn)
            actT = A.alloc(22 * NB)
            ur = Ring(A, 2, 512)
            tr_ = Ring(A, 3, 512)
            stash = A.alloc(8)
            edge = edge_halos(l, G, 32, 24, c)
            Wup = I["ffn_up_w"][l]
            Wdn = I["ffn_down_w"][l]
            for (t0, n, hl, hr) in blocks:
                c_hi = n + 2 if hr else n + 1
                make_hs(hs, t0, c_hi - 1, l, 32, 24, c, 0, col_off=1)
                if hl:
                    for k in range(8):
                        cp(hs.ap(k * Wn, k * Wn + 1), stash.ap(k, k + 1), stash.b(), hs.b(k * Wn, k * Wn + 1), eng="pool")
                else:
                    cp(hs.v3(8, Wn)[:, :, 0:1], edge.ap(0, 8).unsqueeze(2), edge.b(), hs.b(), eng="pool")
                if not hr:
                    cp(hs.v3(8, Wn)[:, :, n + 1:n + 2], edge.ap(8, 16).unsqueeze(2), edge.b(), hs.b(), eng="pool")
                c_lo, c_hi = 0, n + 2
                for k in range(8):
                    cp(stash.ap(k, k + 1), hs.ap(k * Wn + n, k * Wn + n + 1), hs.b(k * Wn + n, k * Wn + n + 1), stash.b(), eng="pool")
                for j in range(22):
                    res = []
                    for half in range(2):
                        ch = half * 22 + j
                        bank = 2 * half + (j % 2)
                        proj_fm(hs, Wup, half * FH + j * 128, 128, c_lo, c_hi, bank)
                        u = ur.next()
                        act(u.ap(c_lo, c_hi), psum[bank][:, c_lo:c_hi], AF.Copy, [pb[bank]], u.b())
                        t = tr_.next()
                        cw3 = fcw[:, (l * 44 + ch) * 3:(l * 44 + ch) * 3 + 3]
                        conv3(t.ap(0, n), t.b(), u.ap(), n, cw3, fcb[:, l * 44 + ch:l * 44 + ch + 1], u.b())
                        res.append(t)
                    ta, tv = res
                    act(ta.ap(0, n), ta.ap(0, n), AF.Silu, ta.b(), ta.b())
                    tt(actT.ap(j * NB, j * NB + n), ta.ap(0, n), tv.ap(0, n), ALU.mult, ta.b() + tv.b(), actT.b(j * NB, j * NB + n))
                for oc in range(8):
                    w = wbig.next()
                    dma(w.v3(22, 128), Wdn[:, oc * 128:(oc + 1) * 128].rearrange("(k p) n -> p k n", p=128), [], w.b())
                    bank = 4 + oc % 2
                    for j in range(22):
                        mm(psum[bank][:, 0:n], w.v3(22, 128)[:, j, :], actT.ap(j * NB, j * NB + n), j == 0, j == 21, w.b() + actT.b(j * NB, j * NB + n), [pb[bank]])
                    residual_update(l, 40, c, oc, t0, n, bank)
            A.release(m)

        def ssd_layer(l, G):
            slot = l // 2
            c = G["c"]
            T, L, nseq = G["T"], G["L"], G["nseq"]
            Win = I["ssd_in_w"][slot]
            Lf = L
            shard = G["shard"]
            NT = Lf // 128
            m0 = A.mark()
            ssq = A.alloc(nseq * NT * 4 + 8)
            m = A.mark()
            blocks = halo_blocks(nseq, L)
            if G["packed"]:
                blocks = []
                RG["wsm"] = Ring(A, 4, 1024)
                hs = A.alloc(8 * 512)
                ur = Ring(A, 2, 1024)
                tr_ = Ring(A, 3, 1024)
                for u in ur.tiles:
                    mset(u.ap(0, 516), 0.0, u.b(), eng="pool")
                make_hs(hs, 0, 512, l, 8, 0, c, 0)
                for j in range(41):
                    bank = 2 + j % 2
                    ncol = 128 if j < 40 else 64
                    proj_fm(hs, Win, j * 128, ncol, 0, 512, bank)
                    t = tr_.next()
                    if j < 16:
                        act(t.ap(0, 512), psum[bank][:, 0:512], AF.Silu, [pb[bank]], t.b())
                        dma(S_proj[j * 128:(j + 1) * 128, 0:512], t.ap(0, 512), t.b(), [db("proj", j, 0)], q=STQ)
                    elif j < 40:
                        u = ur.next()
                        act(u.ap(1, 257), psum[bank][:, 0:256], AF.Copy, [pb[bank]], u.b())
                        act(u.ap(259, 515), psum[bank][:, 256:512], AF.Copy, [pb[bank]], u.b())
                        ch = slot * 24 + (j - 16)
                        conv3(t.ap(0, 514), t.b(), u.ap(0, 516), 514, scw[:, ch * 3:ch * 3 + 3], scb[:, ch:ch + 1], u.b())
                        act(t.ap(0, 514), t.ap(0, 514), AF.Silu, t.b(), t.b())
                        dma(S_proj[j * 128:(j + 1) * 128, 0:256], t.ap(0, 256), t.b(), [db("proj", j, 0)], q=STQ)
                        dma(S_proj[j * 128:(j + 1) * 128, 256:512], t.ap(258, 514), t.b(), [db("proj", j, 0)], q=STQ)
                    else:
                        tp = t.arena.t[0:64, t.off:t.off + 512]
                        act(tp, psum[bank][0:64, 0:512], AF.Exp, [pb[bank]] + SM, t.b(), bias=sm[0:64, 18 + slot:19 + slot])
                        act(tp, tp, AF.Ln, t.b() + SM, t.b(), bias=sm[0:64, 1:2])
                        t2 = tr_.next()
                        tp2 = t2.arena.t[0:64, t2.off:t2.off + 512]
                        ts(tp2, tp, sm[0:64, 20 + slot:21 + slot], None, ALU.mult, None, t.b() + SM, t2.b())
                        dma(S_proj[5184:5248, 0:512], tp2, t2.b(), [db("proj", 41, 0)], q=STQ)
                        dma(S_proj[5120:5184, 0:512], tp, t.b(), [db("proj", 40, 0)], q=STQ)
            else:
                NB = max(b[1] for b in blocks)
                Wn = NB + 2
                RG["wsm"] = Ring(A, 4, 1024)
                hs = A.alloc(8 * Wn)
                ur = Ring(A, 3, 512)
                tr_ = Ring(A, 3, 512)
                edge = edge_halos(l, G, 8, 0, c)
            for bi, (t0, n, hl, hr) in enumerate(blocks):
                c_lo = 0 if hl else 1
                c_hi = n + 2 if hr else n + 1
                lo_tok = t0 - 1 + c_lo
                make_hs(hs, lo_tok, c_hi - c_lo, l, 8, 0, c, 0, col_off=c_lo)
                if not hl:
                    cp(hs.v3(8, Wn)[:, :, 0:1], edge.ap(0, 8).unsqueeze(2), edge.b(), hs.b(), eng="pool")
                if not hr:
                    cp(hs.v3(8, Wn)[:, :, n + 1:n + 2], edge.ap(8, 16).unsqueeze(2), edge.b(), hs.b(), eng="pool")
                c_lo, c_hi = 0, n + 2
                for j in range(41):
                    bank = 2 + j % 2
                    ncol = 128 if j < 40 else 64
                    proj_fm(hs, Win, j * 128, ncol, c_lo, c_hi, bank)
                    t = tr_.next()
                    if j < 16:
                        act(t.ap(0, n), psum[bank][:, 1:n + 1], AF.Silu, [pb[bank]], t.b())
                    elif j < 40:
                        u = ur.next()
                        act(u.ap(c_lo, c_hi), psum[bank][:, c_lo:c_hi], AF.Copy, [pb[bank]], u.b())
                        ch = slot * 24 + (j - 16)
                        conv3(t.ap(0, n), t.b(), u.ap(), n, scw[:, ch * 3:ch * 3 + 3], scb[:, ch:ch + 1], u.b())
                        act(t.ap(0, n), t.ap(0, n), AF.Silu, t.b(), t.b())
                    else:
                        tp = t.arena.t[0:64, t.off:t.off + n]
                        act(tp, psum[bank][0:64, 1:n + 1], AF.Exp, [pb[bank]] + SM, t.b(), bias=sm[0:64, 18 + slot:19 + slot])
                        act(tp, tp, AF.Ln, t.b() + SM, t.b(), bias=sm[0:64, 1:2])
                        t2 = tr_.next()
                        tp2 = t2.arena.t[0:64, t2.off:t2.off + n]
                        ts(tp2, tp, sm[0:64, 20 + slot:21 + slot], None, ALU.mult, None, t.b() + SM, t2.b())
                        dma(S_proj[5184:5248, t0:t0 + n], tp2, t2.b(), [db("proj", 41, bi)], q=STQ)
                    if j < 40:
                        dma(S_proj[j * 128:(j + 1) * 128, t0:t0 + n], t.ap(0, n), t.b(), [db("proj", j, bi)], q=STQ)
                    else:
                        dma(S_proj[5120:5184, t0:t0 + n], t.arena.t[0:64, t.off:t.off + n], t.b(), [db("proj", 40, bi)], q=STQ)
            A.release(m)
            nblk = max(1, len(blocks))

            def pj(j):
                return [db("proj", j, bi) for bi in range(nblk)]

            def psrc(row0, nrows, tb_, tok0, ntok, j):
                return S_proj[row0:row0 + nrows, tb_ + tok0:tb_ + tok0 + ntok], pj(j)

            m = A.mark()
            BT = A.alloc(Lf); CT = A.alloc(Lf)
            Btok = A.alloc(Lf)
            dtok = A.alloc(NT * 64)
            datok = A.alloc(NT * 64)
            LN = [dict(xtok=A.alloc(Lf), ytok=A.alloc(Lf), h=A.alloc(128), X=2 + 3 * i, Y=3 + 3 * i, Z=4 + 3 * i) for i in range(2)]
            CBm = A.alloc(Lf)
            eaAll = A.alloc(NT * 64)
            wk = Ring(A, 14, 512)
            sm2 = Ring(A, 10, 128)
            ld = Ring(A, 3, 512)
            PC = min(512, Lf)
            mset(ssq.ap(), 0.0, ssq.b(), eng="pool")

            def prep(ln, d, s, tb, hp):
                xtok, ytok, h = ln["xtok"], ln["ytok"], ln["h"]
                HB = h.b()
                for p0 in range(0, Lf, PC):
                    lt = ld.next()
                    sa_, sb_ = psrc(2048 + hp * 128, 128, tb, p0, PC, 16 + hp)
                    dma(lt.ap(0, PC), sa_, sb_, lt.b())
                    for qq in range(PC // 128):
                        q = p0 // 128 + qq
                        P.op("pe", lambda e, lt=lt, qq=qq: e.transpose(psum[1][:, 0:128], lt.ap(qq * 128, (qq + 1) * 128), ident), reads=lt.b() + CM, writes=[pb[1]])
                        cp(xtok.ap(q * 128, (q + 1) * 128), psum[1][:, 0:128], [pb[1]], xtok.b(q * 128, (q + 1) * 128))
                        if d == 0:
                            tt(ytok.v3(2, 64, q * 128), xtok.v3(2, 64, q * 128), sm[:, 64 + slot * 32 + 2 * hp:64 + slot * 32 + 2 * hp + 2].unsqueeze(2).broadcast_to([128, 2, 64]),
                               ALU.mult, xtok.b(q * 128, (q + 1) * 128) + SM, ytok.b(q * 128, (q + 1) * 128), eng="pool")
                if d == 1:
                    for q in range(NT):
                        dma(ytok.ap(q * 128, (q + 1) * 128), S_y[tb + q * 128:tb + (q + 1) * 128, hp * 128:(hp + 1) * 128], [db("y", s, q, hp)], ytok.b(q * 128, (q + 1) * 128))
                if G["sample"]:
                    st_in = wk.next()
                    if d == 0:
                        dma(st_in.ap(0, 128), I["state"][slot, d, hp], [], st_in.b())
                        P.op("pe", lambda e, st_in=st_in: e.transpose(psum[1][:, 0:128], st_in.ap(0, 128), ident), reads=st_in.b() + CM, writes=[pb[1]])
                        cp(h.ap(), psum[1][:, 0:128], [pb[1]], HB)
                    else:
                        st2 = wk.next()
                        dma(st_in.ap(0, 128), S_hstG[hp * 128:(hp + 1) * 128, :], [db("hstG")], st_in.b())
                        dma(st2.ap(0, 128), S_hstG[2048 + hp * 128:2048 + (hp + 1) * 128, :], [db("hstG")], st2.b())
                        ts(st_in.ap(0, 128), st_in.ap(0, 128), mB, None, ALU.mult, None, st_in.b() + SM, st_in.b())
                        stt(h.ap(), st2.ap(0, 128), mA, st_in.ap(0, 128), ALU.mult, ALU.add, st2.b() + SM + st_in.b(), HB)
                else:
                    mset(h.ap(), 0.0, HB, eng="pool")

            def step1(ln, d, hp, q):
                xtok, ytok, h = ln["xtok"], ln["ytok"], ln["h"]
                X, Y, Z = ln["X"], ln["Y"], ln["Z"]
                HB = h.b()
                hc0 = d * 32 + 2 * hp
                mR = mT if d == 0 else mL
                mLh = mLs if d == 0 else mTs
                da = datok.v3(NT, 64)[:, q, hc0:hc0 + 2]
                dtq = dtok.v3(NT, 64)[:, q, hc0:hc0 + 2]
                R = wk.next()
                tt(R.v3(2, 128), mR.unsqueeze(1).broadcast_to([128, 2, 128]), da.unsqueeze(2).broadcast_to([128, 2, 128]), ALU.mult, CM + datok.b(), R.b(), eng="pool")
                mm(psum[X][:, 0:256], mLh, R.ap(0, 256), True, True, CM + R.b(), [pb[X]])
                E = wk.next()
                act(E.ap(0, 256), psum[X][:, 0:256], AF.Exp, [pb[X]], E.b())
                ea_acs = eaAll.ap(q * 64 + 2 * hp, q * 64 + 2 * hp + 2)
                ea_dec = eaAll.ap(q * 64 + 32 + 2 * hp, q * 64 + 32 + 2 * hp + 2)
                xd = wk.next()
                tt(xd.v3(2, 64), xtok.v3(2, 64, q * 128), dtq.unsqueeze(2).broadcast_to([128, 2, 64]), ALU.mult, xtok.b(q * 128, (q + 1) * 128) + dtok.b(), xd.b(), eng="pool")
                lastc = 127 if d == 0 else 0
                tt(xd.v3(2, 64, 128), xd.v3(2, 64), E.v3(2, 128)[:, :, lastc:lastc + 1].broadcast_to([128, 2, 64]), ALU.mult, xd.b() + E.b(), xd.b(), eng="pool")
                tt(E.v3(2, 128), E.v3(2, 128), CBm.ap(q * 128, (q + 1) * 128).unsqueeze(1).broadcast_to([128, 2, 128]), ALU.mult, E.b() + CBm.b(q * 128, (q + 1) * 128), E.b())
                return E, xd

            def step2(ln, d, hp, q, E, xd):
                xtok, ytok, h = ln["xtok"], ln["ytok"], ln["h"]
                X, Y, Z = ln["X"], ln["Y"], ln["Z"]
                HB = h.b()
                ea_acs = eaAll.ap(q * 64 + 2 * hp, q * 64 + 2 * hp + 2)
                ea_dec = eaAll.ap(q * 64 + 32 + 2 * hp, q * 64 + 32 + 2 * hp + 2)
                for hh in range(2):
                    mm(psum[Z][:, hh * 64:(hh + 1) * 64], E.v3(2, 128)[:, hh, :], xd.ap(hh * 64, (hh + 1) * 64), True, True, E.b() + xd.b(), [pb[Z]])
                mm(psum[Z][:, 128:256], CT.ap(q * 128, (q + 1) * 128), h.ap(), True, True, CT.b(q * 128, (q + 1) * 128) + HB, [pb[Z]])
                mm(psum[Z][:, 256:384], Btok.ap(q * 128, (q + 1) * 128), xd.ap(128, 256), True, True, Btok.b(q * 128, (q + 1) * 128) + xd.b(), [pb[Z]])
                yb = ytok.b(q * 128, (q + 1) * 128)
                tmp = sm2.next()
                tt(tmp.v3(2, 64), psum[Z][:, 128:256].rearrange("p (a b) -> p a b", a=2, b=64), ea_acs.unsqueeze(2).broadcast_to([128, 2, 64]), ALU.mult, [pb[Z]] + eaAll.b(), tmp.b())
                tt(tmp.ap(), tmp.ap(), psum[Z][:, 0:128], ALU.add, tmp.b() + [pb[Z]], tmp.b())
                tt(h.v3(2, 64), h.v3(2, 64), ea_dec.unsqueeze(2).broadcast_to([128, 2, 64]), ALU.mult, HB + eaAll.b(), HB)
                tt(h.ap(), h.ap(), psum[Z][:, 256:384], ALU.add, HB + [pb[Z]], HB)
                tt(ytok.ap(q * 128, (q + 1) * 128), ytok.ap(q * 128, (q + 1) * 128), tmp.ap(), ALU.add, yb + tmp.b(), yb, eng="pool")

            def fin(ln, d, s, tb, hp):
                xtok, ytok, h = ln["xtok"], ln["ytok"], ln["h"]
                HB = h.b()
                g = hp // 4
                if not G["sample"]:
                    P.op("pe", lambda e, h=h: e.transpose(psum[1][:, 0:128], h.ap(), ident), reads=HB + CM, writes=[pb[1]])
                    so = wk.next()
                    cp(so.ap(0, 128), psum[1][:, 0:128], [pb[1]], so.b())
                    dma(O["nstate"][s, slot, d, hp], so.ap(0, 128), so.b(), [], final=True, q=STQ)
                elif d == 0:
                    dma(S_hst[hp * 128:(hp + 1) * 128, :], h.ap(), HB, [db("hst", hp)], q=STQ)
                if d == 1:
                    for p0 in range(0, Lf, PC):
                        lt = ld.next()
                        sa_, sb_ = psrc(hp * 128, 128, tb, p0, PC, hp)
                        dma(lt.ap(0, PC), sa_, sb_, lt.b())
                        for qq in range(PC // 128):
                            q = p0 // 128 + qq
                            P.op("pe", lambda e, lt=lt, qq=qq: e.transpose(psum[0][:, 0:128], lt.ap(qq * 128, (qq + 1) * 128), ident), reads=lt.b() + CM, writes=[pb[0]])
                            yb = ytok.b(q * 128, (q + 1) * 128)
                            tt(ytok.ap(q * 128, (q + 1) * 128), ytok.ap(q * 128, (q + 1) * 128), psum[0][:, 0:128], ALU.mult, yb + [pb[0]], yb)
                            sq_ = sm2.next()
                            sc_ = nseq * NT * 4 + (hp % 2)
                            P.op("act", lambda e, sq_=sq_, q=q, sc_=sc_, ytok=ytok: e.activation(out=sq_.ap(), in_=ytok.ap(q * 128, (q + 1) * 128), func=AF.Square, accum_out=ssq.ap(sc_, sc_ + 1)),
                                 reads=yb, writes=sq_.b() + ssq.b())
                            si = (s * NT + q) * 4 + g
                            tt(ssq.ap(si, si + 1), ssq.ap(si, si + 1), ssq.ap(sc_, sc_ + 1), ALU.add, ssq.b(), ssq.b())
                for q in range(NT):
                    dma(S_y[tb + q * 128:tb + (q + 1) * 128, hp * 128:(hp + 1) * 128], ytok.ap(q * 128, (q + 1) * 128), ytok.b(q * 128, (q + 1) * 128), [db("y", s, q, hp)], q=STQ)

            for d in range(2):
                if d == 1 and shard:
                    allgather(S_hst, S_hstG, [db("hst", hp_) for hp_ in range(16)], [db("hstG")])
                for s in range(nseq):
                    tb = s * Lf
                    if d == 0 or nseq > 1:
                        for srow, dst_ in ((5120, dtok), (5184, datok)):
                            for p0 in range(0, Lf, PC):
                                lt = ld.next()
                                sa_, sb_ = psrc(srow, 64, tb, p0, PC, 40)
                                dma(lt.arena.t[0:64, lt.off:lt.off + PC], sa_, sb_ + pj(41), lt.b())
                                for qq in range(PC // 128):
                                    q = p0 // 128 + qq
                                    P.op("pe", lambda e, lt=lt, qq=qq: e.transpose(psum[0][:, 0:64], lt.arena.t[0:64, lt.off + qq * 128:lt.off + (qq + 1) * 128], ident[0:64, 0:64]),
                                         reads=lt.b() + CM, writes=[pb[0]])
                                    cp(dst_.ap(q * 64, (q + 1) * 64), psum[0][:, 0:64], [pb[0]], dst_.b())
                    mRd = mT if d == 0 else mL
                    for q in range(NT):
                        dq = datok.v3(NT, 64)[:, q, d * 32:(d + 1) * 32]
                        mm(psum[0][:, 0:32], mRd, dq, True, True, CM + datok.b(), [pb[0]])
                        mm(psum[0][:, 32:64], ones, dq, True, True, CM + datok.b(), [pb[0]])
                        act(eaAll.ap(q * 64, (q + 1) * 64), psum[0][:, 0:64], AF.Exp, [pb[0]], eaAll.b())
                    order = list(range(NT)) if d == 0 else list(range(NT - 1, -1, -1))
                    for hp0 in range(0, 16, 2):
                        g = hp0 // 4
                        if hp0 % 4 == 0:
                            sa_, sb_ = psrc(4096 + g * 128, 128, tb, 0, Lf, 32 + g)
                            dma(BT.ap(), sa_, sb_, BT.b())
                            sa_, sb_ = psrc(4608 + g * 128, 128, tb, 0, Lf, 36 + g)
                            dma(CT.ap(), sa_, sb_, CT.b())
                            for q in range(NT):
                                P.op("pe", lambda e, q=q: e.transpose(psum[1][:, 0:128], BT.ap(q * 128, (q + 1) * 128), ident), reads=BT.b(q * 128, (q + 1) * 128) + CM, writes=[pb[1]])
                                cp(Btok.ap(q * 128, (q + 1) * 128), psum[1][:, 0:128], [pb[1]], Btok.b(q * 128, (q + 1) * 128))
                                mm(psum[0][:, 0:128], BT.ap(q * 128, (q + 1) * 128), CT.ap(q * 128, (q + 1) * 128), True, True, BT.b(q * 128, (q + 1) * 128) + CT.b(q * 128, (q + 1) * 128), [pb[0]])
                                tt(CBm.ap(q * 128, (q + 1) * 128), psum[0][:, 0:128], mRd, ALU.mult, [pb[0]] + CM, CBm.b(q * 128, (q + 1) * 128))
                        for i in range(2):
                            prep(LN[i], d, s, tb, hp0 + i)
                        pend = [step1(LN[i], d, hp0 + i, order[0]) for i in range(2)]
                        for qi, q in enumerate(order):
                            nxt = None
                            if qi + 1 < len(order):
                                nxt = [step1(LN[i], d, hp0 + i, order[qi + 1]) for i in range(2)]
                            for i in range(2):
                                step2(LN[i], d, hp0 + i, q, *pend[i])
                            pend = nxt
                        for i in range(2):
                            fin(LN[i], d, s, tb, hp0 + i)
            A.release(m)
            m3 = A.mark()
            NTo = NT
            wbig = Ring(A, 2, 16 * 128)
            yr = Ring(A, 4, 2048)
            yTt = A.alloc(16 * 512)
            rs = A.alloc(nseq * NT * 4)
            act(rs.ap(), ssq.ap(0, nseq * NT * 4), AF.Sqrt, ssq.b() + SM, rs.b(), bias=epsc, scale=1.0 / 512)
            recip(rs.ap(), rs.ap(), rs.b(), rs.b())
            Wo = I["ssd_out_w"][slot]
            NTall = nseq * NT
            TBk = min(4, NTall)
            NW = TBk * 128
            for s in range(1):
                tb = 0
                for q0 in range(0, NTall, TBk):
                    for qq in range(TBk):
                        q = q0 + qq
                        yt = yr.next()
                        dma(yt.ap(), S_y[q * 128:(q + 1) * 128, :], [db("y", q // NT, q % NT, hp_) for hp_ in range(16)], yt.b())
                        for g in range(4):
                            si = q * 4 + g
                            ts(yt.ap(g * 512, (g + 1) * 512), yt.ap(g * 512, (g + 1) * 512), rs.ap(si, si + 1), None, ALU.mult, None, yt.b(g * 512, (g + 1) * 512) + rs.b(), yt.b(g * 512, (g + 1) * 512))
                        for kc in range(16):
                            bank = kc % 2
                            P.op("pe", lambda e, yt=yt, kc=kc, bank=bank: e.transpose(psum[bank][:, 0:128], yt.ap(kc * 128, (kc + 1) * 128), ident), reads=yt.b(kc * 128, (kc + 1) * 128) + CM, writes=[pb[bank]])
                            dst_lo = kc * NW + qq * 128
                            if kc % 2 == 0:
                                ts(yTt.ap(dst_lo, dst_lo + 128), psum[bank][:, 0:128], sng[:, slot * 16 + kc:slot * 16 + kc + 1], None, ALU.mult, None, [pb[bank]] + VB, yTt.b(dst_lo, dst_lo + 128))
                            else:
                                act(yTt.ap(dst_lo, dst_lo + 128), psum[bank][:, 0:128], AF.Copy, [pb[bank]] + VB, yTt.b(dst_lo, dst_lo + 128), scale=sng[:, slot * 16 + kc:slot * 16 + kc + 1])
                    for oc in range(8):
                        w = wbig.next()
                        dma(w.v3(16, 128), Wo[:, oc * 128:(oc + 1) * 128].rearrange("(k p) n -> p k n", p=128), [], w.b())
                        bank = 2 + oc % 2
                        for kc in range(16):
                            mm(psum[bank][:, 0:NW], w.v3(16, 128)[:, kc, :], yTt.ap(kc * NW, (kc + 1) * NW), kc == 0, kc == 15, w.b() + yTt.b(kc * NW, (kc + 1) * NW), [pb[bank]])
                        residual_update(l, 16, c, oc, tb + q0 * 128, NW, bank)
            A.release(m3)
            A.release(m0)

        def attn_layer(l, G):
            slot = l // 2
            c = G["c"]
            T, L, nseq = G["T"], G["L"], G["nseq"]
            sample = G["sample"]
            Wq = I["att_qkv_w"][slot]
            m = A.mark()
            RG["wsm"] = Ring(A, 4, 1024)
            wbig = Ring(A, 2, 8 * 256)
            hs = A.alloc(8 * 512)
            tr_ = Ring(A, 3, 512)
            rc = A.alloc(512); rsn = A.alloc(512)
            for bi, b0 in enumerate(range(0, T, 512)):
                make_hs(hs, b0, 512, l, 8, 0, c, 0)
                if sample:
                    dma(rc.ap(), I["ropeC"][:, b0:b0 + 512], [], rc.b())
                    dma(rsn.ap(), I["ropeS"][:, b0:b0 + 512], [], rsn.b())
                for j in range(16):
                    bank = 2 + j % 2
                    proj_fm(hs, Wq, j * 128, 128, 0, 512, bank)
                    t = tr_.next()
                    act(t.ap(), psum[bank][:, :], AF.Copy, [pb[bank]], t.b())
                    if sample:
                        mm(psum[4 + j % 2][:, :], prot, t.ap(), True, True, CM + t.b(), [pb[4 + j % 2]])
                        t2 = tr_.next()
                        tt(t2.ap(), psum[4 + j % 2][:, :], rsn.ap(), ALU.mult, [pb[4 + j % 2]] + rsn.b(), t2.b())
                        tt(t.ap(), t.ap(), rc.ap(), ALU.mult, t.b() + rc.b(), t.b())
                        tt(t.ap(), t.ap(), t2.ap(), ALU.add, t.b() + t2.b(), t.b())
                    if j < 8:
                        dma(S_q[j * 128:(j + 1) * 128, b0:b0 + 512], t.ap(), t.b(), [db("qk", j, bi)], q=STQ)
                    else:
                        dma(S_kv[(j - 8) * 128:(j - 7) * 128, b0:b0 + 512], t.ap(), t.b(), [db("qk", j, bi)], q=STQ)
                for part in (range(4, 12) if not sample else range(8, 12)):
                    w = wbig.next()
                    dma(w.v3(8, 256), Wq[:, part * 256:(part + 1) * 256].rearrange("(k p) n -> p k n", p=128), [], w.b())
                    for q in range(4):
                        bank = 6 + q % 2
                        for k in range(8):
                            mm(psum[bank][:, 0:256], hs.ap(k * 512 + q * 128, k * 512 + (q + 1) * 128), w.v3(8, 256)[:, k, :], k == 0, k == 7, hs.b(k * 512 + q * 128, k * 512 + (q + 1) * 128) + w.b(), [pb[bank]])
                        t = tr_.next()
                        act(t.ap(0, 256), psum[bank][:, 0:256], AF.Copy, [pb[bank]], t.b())
                        tok = b0 + q * 128
                        if sample:
                            dma(S_kv[D + tok:D + tok + 128, (part - 8) * 256:(part - 7) * 256], t.ap(0, 256), t.b(), [db("v", tok // 128)], q=STQ)
                        else:
                            s_, tq = tok // LP, tok % LP
                            if part < 8:
                                dma(O["nk"][s_, slot, tq:tq + 128, (part - 4) * 256:(part - 3) * 256], t.ap(0, 256), t.b(), [], final=True, q=STQ)
                            else:
                                dma(O["nv"][s_, slot, tq:tq + 128, (part - 8) * 256:(part - 7) * 256], t.ap(0, 256), t.b(), [db("v", tok // 128)], final=True, q=STQ)
            A.release(m)
            nblk = T // 512
            if G["shard"]:
                allgather_rows(S_kv, S_kvG, 2 * D, lambda b0, b1: ([db("qk", 8 + j, bi) for j in range(b0 // 128, b1 // 128) for bi in range(nblk)] if b0 < D
                                                                     else [db("v", q) for q in range((b0 - D) // 128, (b1 - D) // 128)]), "kvG")
            m = A.mark()
            NKT = (G["Lf"] + (256 if sample else 0)) // 128
            LK = NKT * 128
            QB = min(512, L)
            qT = A.alloc(L); kT = A.alloc(LK)
            vv = A.alloc(NKT * 128)
            pTr = Ring(A, 4, 512)
            sacc = [Ring(A, 2, 512), Ring(A, 2, 512)]
            rr = Ring(A, 4, 512)
            ot = Ring(A, 2, 512)
            ck_t = Ring(A, 2, 128)
            neglam = sm[:, 24 + slot:25 + slot]
            subs = sm[:, 26 + slot:27 + slot]
            vva = vv.v3(NKT, 128)
            qbi = 0
            for s in range(nseq):
                tb = s * L
                for hd in range(8):
                    kb = 0
                    dma(qT.ap(), S_q[hd * 128:(hd + 1) * 128, tb:tb + L], [db("qk", hd, bi) for bi in range(nblk)], qT.b())
                    if sample:
                        kb = 256
                        for q in range(2):
                            ct = ck_t.next()
                            dma(ct.ap(0, 128), I["ck"][slot, q * 128:(q + 1) * 128, hd * 128:(hd + 1) * 128], [], ct.b())
                            P.op("pe", lambda e, ct=ct: e.transpose(psum[1][:, 0:128], ct.ap(0, 128), ident), reads=ct.b() + CM, writes=[pb[1]])
                            cp(kT.ap(q * 128, (q + 1) * 128), psum[1][:, 0:128], [pb[1]], kT.b(q * 128, (q + 1) * 128))
                            dma(vva[:, q, :], I["cv"][slot, q * 128:(q + 1) * 128, hd * 128:(hd + 1) * 128], [], vv.b(q * 128, (q + 1) * 128))
                    if sample:
                        for r_ in range(2):
                            k0_ = kb + r_ * LH
                            ga, gb = gathered(S_kvG, 2 * D, "kvG", r_, hd * 128, 128)
                            dma(kT.ap(k0_, k0_ + LH), ga, gb, kT.b(k0_, k0_ + LH))
                            for q in range(LH // 128):
                                kq = k0_ // 128 + q
                                ga, gb = gathered(S_kvG, 2 * D, "kvG", r_, D + q * 128, 128)
                                dma(vva[:, kq, :], ga[:, hd * 128:(hd + 1) * 128], gb, vv.b(kq * 128, (kq + 1) * 128))
                    else:
                        dma(kT.ap(kb, kb + L), S_kv[hd * 128:(hd + 1) * 128, tb:tb + L], [db("qk", 8 + hd, bi) for bi in range(nblk)], kT.b(kb, kb + L))
                        for q in range(L // 128):
                            tok = tb + q * 128
                            src = O["nv"][tok // LP, slot, tok % LP:tok % LP + 128, hd * 128:(hd + 1) * 128]
                            kq = kb // 128 + q
                            dma(vva[:, kq, :], src, [db("v", tok // 128)], vv.b(kq * 128, (kq + 1) * 128))
                    for qb0 in range(0, L, QB):
                        par = qbi % 2
                        qbi += 1
                        sa = [sacc[0].next(), sacc[1].next()]
                        accb = [4 + 2 * par, 5 + 2 * par]

                        def score(kt, mp):
                            sb = 2 * (kt % 2) + mp
                            mm(psum[sb][:, 0:QB], kT.arena.t[mp * 64:(mp + 1) * 64, kT.off + kt * 128:kT.off + (kt + 1) * 128],
                               qT.arena.t[mp * 64:(mp + 1) * 64, qT.off + qb0:qT.off + qb0 + QB],
                               True, True, kT.b(kt * 128, (kt + 1) * 128) + qT.b(qb0, qb0 + QB), [pb[sb]])

                        score(0, 0); score(0, 1)
                        for kt in range(NKT):
                            pts = []
                            for mp in range(2):
                                sb = 2 * (kt % 2) + mp
                                pt = pTr.next()
                                act(pt.ap(0, QB), psum[sb][:, 0:QB], AF.Exp, [pb[sb]], pt.b(), scale=0.125)
                                pts.append(pt)
                            if kt + 1 < NKT:
                                score(kt + 1, 0); score(kt + 1, 1)
                            for mp in range(2):
                                pt = pts[mp]
                                mm(psum[accb[mp]][:, 0:QB], vva[:, kt, :], pt.ap(0, QB), kt == 0, kt == NKT - 1, vv.b(kt * 128, (kt + 1) * 128) + pt.b(), [pb[accb[mp]]])
                                if kt == 0:
                                    cp(sa[mp].ap(0, QB), pt.ap(0, QB), pt.b(), sa[mp].b(), eng="pool")
                                else:
                                    tt(sa[mp].ap(0, QB), sa[mp].ap(0, QB), pt.ap(0, QB), ALU.add, sa[mp].b() + pt.b(), sa[mp].b(), eng="pool")
                        rc_ = []
                        for mp in range(2):
                            mm(psum[mp][:, 0:QB], ones, sa[mp].ap(0, QB), True, True, CM + sa[mp].b(), [pb[mp]])
                            r = rr.next()
                            recip(r.ap(0, QB), psum[mp][:, 0:QB], [pb[mp]], r.b())
                            rc_.append(r)
                        ts(rc_[1].ap(0, QB), rc_[1].ap(0, QB), neglam, None, ALU.mult, None, rc_[1].b() + SM, rc_[1].b())
                        o = ot.next()
                        tt(o.ap(0, QB), psum[accb[0]][:, 0:QB], rc_[0].ap(0, QB), ALU.mult, [pb[accb[0]]] + rc_[0].b(), o.b())
                        tt(rc_[1].ap(0, QB), psum[accb[1]][:, 0:QB], rc_[1].ap(0, QB), ALU.mult, [pb[accb[1]]] + rc_[1].b(), rc_[1].b())
                        tt(o.ap(0, QB), o.ap(0, QB), rc_[1].ap(0, QB), ALU.add, o.b() + rc_[1].b(), o.b())
                        sq_ = rr.next()
                        act(sq_.ap(0, QB), o.ap(0, QB), AF.Square, o.b(), sq_.b())
                        mm(psum[0][:, 0:QB], ones, sq_.ap(0, QB), True, True, CM + sq_.b(), [pb[0]])
                        act(sq_.ap(0, QB), psum[0][:, 0:QB], AF.Sqrt, [pb[0]] + SM, sq_.b(), bias=epsc, scale=1.0 / 128)
                        recip(sq_.ap(0, QB), sq_.ap(0, QB), sq_.b(), sq_.b())
                        stt(o.ap(0, QB), o.ap(0, QB), subs, sq_.ap(0, QB), ALU.mult, ALU.mult, o.b() + SM + sq_.b(), o.b())
                        for b5 in range(qb0, qb0 + QB, 128):
                            pass
                        dma(S_o[hd * 128:(hd + 1) * 128, tb + qb0:tb + qb0 + QB], o.ap(0, QB), o.b(), [db("o", hd, (tb + qb0) // 512, (tb + qb0) % 512)], q=STQ)
            A.release(m)
            m = A.mark()
            RG["wsm"] = Ring(A, 4, 1024)
            oTb = A.alloc(8 * 512)
            Wo = I["att_out_w"][slot]
            for bi, b0 in enumerate(range(0, T, 512)):
                for hd in range(8):
                    dma(oTb.ap(hd * 512, (hd + 1) * 512), S_o[hd * 128:(hd + 1) * 128, b0:b0 + 512], [db("o", hd, bi, off_) for off_ in range(0, 512, min(512, L))], oTb.b(hd * 512, (hd + 1) * 512))
                for oc in range(8):
                    w = load_wcols(Wo, oc * 128, 128)
                    bank = 2 + oc % 2
                    for k in range(8):
                        mm(psum[bank][:, :], w.v3(8, 128)[:, k, :], oTb.ap(k * 512, (k + 1) * 512), k == 0, k == 7, w.b() + oTb.b(k * 512, (k + 1) * 512), [pb[bank]])
                    residual_update(l, 16, c, oc, b0, 512, bank)
            A.release(m)

        groups = []
        if do_sample:
            groups.append(dict(sample=True, shard=True, packed=False, c=0, T=LH, L=LH, Lf=LS, nseq=1, src=I["xs_in"], dst=O["y_s"]))
        if do_prompt:
            groups.append(dict(sample=False, shard=False, packed=True, c=1, T=2 * LP, L=LP, Lf=LP, nseq=2, src=I["xp_in"], dst=O["y_p"]))
        for G in groups:
            load_x(G["src"], G["T"])
            for l in range(depth_run):
                if l % 2 == 0:
                    ssd_layer(l, G)
                else:
                    attn_layer(l, G)
                ffn_layer(l, G)
            final_out(G["dst"], G["T"])
        P.emit(out_events)
    return nc


def _fm(v, nchunk):
    return np.ascontiguousarray(np.asarray(v, np.float32).reshape(nchunk, 128).T)


def _consts():
    a = np.arange(128)
    ident = np.eye(128, dtype=np.float32)
    ones = np.ones((128, 128), np.float32)
    mT = (a[:, None] <= a[None, :]).astype(np.float32)
    mL = (a[:, None] >= a[None, :]).astype(np.float32)
    mTs = (a[:, None] < a[None, :]).astype(np.float32)
    mLs = (a[:, None] > a[None, :]).astype(np.float32)
    i = a % 32
    partner = np.where(i < 16, a + 16, a - 16)
    prot = np.zeros((128, 128), np.float32)
    prot[partner, a] = 1.0
    cmat = np.stack([ident, ones, mT, mL, mTs, mLs, prot], axis=1).reshape(128, 7 * 128)
    t = np.arange(LS)
    row = (t // 64).astype(np.float32)
    col = (t % 64).astype(np.float32)
    inv = (1.0 / (np.float32(10000.0) ** (np.arange(0, 32, 2, dtype=np.float32) / np.float32(32)))).astype(np.float32)
    dd = a % 64
    axis_col = dd >= 32
    f = i % 16
    pos = np.where(axis_col[:, None], col[None, :], row[None, :]).astype(np.float32)
    ang = (pos * inv[f][:, None]).astype(np.float32)
    C = np.cos(ang).astype(np.float32)
    S = np.sin(ang).astype(np.float32)
    S = np.where((i < 16)[:, None], -S, S).astype(np.float32)
    return np.ascontiguousarray(cmat), np.ascontiguousarray(C), np.ascontiguousarray(S)


_CACHE = {}


def kernel(**inp):
    f = lambda k: np.asarray(inp[k], np.float32)
    key = (DEPTH_RUN, DO_SAMPLE, DO_PROMPT)
    if key not in _CACHE:
        _CACHE[key] = build_program(DEPTH_RUN, DO_SAMPLE, DO_PROMPT)
    nc = _CACHE[key]
    cmat, rC, rS = _consts()
    shared = {
        "mod_w": f("mod_w"),
        "mod_bT": np.ascontiguousarray(np.concatenate([_fm(f("mod_b")[l], 48) for l in range(4)], axis=1)),
        "nmg": np.ascontiguousarray(np.concatenate([_fm(f("norm_mix_g")[l], 8) for l in range(4)], axis=1)),
        "nfg": np.ascontiguousarray(np.concatenate([_fm(f("norm_ffn_g")[l], 8) for l in range(4)], axis=1)),
        "fng": _fm(f("final_norm_g"), 8),
        "ssd_in_w": f("ssd_in_w"),
        "ssd_cw": np.ascontiguousarray(f("ssd_conv_w").reshape(2, 3, 24, 128).transpose(3, 0, 2, 1).reshape(128, 2 * 24 * 3)),
        "ssd_cb": np.ascontiguousarray(f("ssd_conv_b").reshape(2, 24, 128).transpose(2, 0, 1).reshape(128, 48)),
        "dtb": np.ascontiguousarray(f("ssd_dt_bias").reshape(2, 64).T),
        "alog": np.ascontiguousarray(f("ssd_a_log").reshape(2, 64).T),
        "dsk": np.ascontiguousarray(np.broadcast_to(f("ssd_d").reshape(1, 64), (128, 64))),
        "sng": np.ascontiguousarray(np.concatenate([_fm(f("ssd_norm_g")[s], 16) for s in range(2)], axis=1)),
        "ssd_out_w": f("ssd_out_w"),
        "att_qkv_w": f("att_qkv_w"),
        "lamb": np.ascontiguousarray(np.broadcast_to(f("att_lambda").reshape(1, 512), (128, 512))),
        "sub": np.ascontiguousarray(f("att_subln_g").T),
        "att_out_w": f("att_out_w"),
        "ffn_up_w": f("ffn_up_w"),
        "fcw": np.ascontiguousarray(f("ffn_conv_w").reshape(4, 3, 44, 128).transpose(3, 0, 2, 1).reshape(128, 4 * 44 * 3)),
        "fcb": np.ascontiguousarray(f("ffn_conv_b").reshape(4, 44, 128).transpose(2, 0, 1).reshape(128, 4 * 44)),
        "ffn_down_w": f("ffn_down_w"),
        "cmat": cmat,
    }
    def dirswap(w):
        w2 = w.copy()
        w2[..., 5120:5152] = w[..., 5152:5184]
        w2[..., 5152:5184] = w[..., 5120:5152]
        return w2
    shared_odd = dict(shared)
    shared_odd["ssd_in_w"] = dirswap(f("ssd_in_w"))
    shared_odd["dtb"] = np.ascontiguousarray(f("ssd_dt_bias")[:, ::-1].reshape(2, 64).T)
    shared_odd["alog"] = np.ascontiguousarray(f("ssd_a_log")[:, ::-1].reshape(2, 64).T)
    shared_odd["ssd_cw"] = np.ascontiguousarray(f("ssd_conv_w")[:, ::-1].reshape(2, 3, 24, 128).transpose(3, 0, 2, 1).reshape(128, 2 * 24 * 3))
    shared_odd["fcw"] = np.ascontiguousarray(f("ffn_conv_w")[:, ::-1].reshape(4, 3, 44, 128).transpose(3, 0, 2, 1).reshape(128, 4 * 44 * 3))
    xs, xp = f("x_sample"), f("x_prompt")
    st, ck, cv, cc, cctx = f("state_ssd"), f("cache_k"), f("cache_v"), f("c"), f("c_ctx")
    in_maps = []
    for core in range(NCORES):
        b, r = core // 2, core % 2
        mp = dict(shared_odd if r else shared)
        fl = (lambda a, ax: np.flip(a, axis=ax)) if r else (lambda a, ax: a)
        mp["xs_in"] = np.ascontiguousarray(fl(xs[b, r * LH:(r + 1) * LH], 0))
        mp["ropeC"] = np.ascontiguousarray(fl(rC[:, r * LH:(r + 1) * LH], 1))
        mp["ropeS"] = np.ascontiguousarray(fl(rS[:, r * LH:(r + 1) * LH], 1))
        mp["rmask"] = np.ascontiguousarray(np.broadcast_to(np.array([[1.0 - r, float(r)]], np.float32), (128, 2)))
        mp["xp_in"] = np.ascontiguousarray(fl(xp[2 * core:2 * core + 2], 1).reshape(2 * LP, D))
        mp["state"] = np.ascontiguousarray(fl(st[b], 1).reshape(2, 2, 16, 128, 128))
        mp["ck"] = np.ascontiguousarray(ck[b].reshape(2, 256, D))
        mp["cv"] = np.ascontiguousarray(cv[b].reshape(2, 256, D))
        cm2 = np.stack([_fm(cc[b], 8), _fm(cctx, 8)], axis=2).reshape(128, 16)
        mp["cmod"] = np.ascontiguousarray(cm2)
        in_maps.append(mp)
    res = run_bass_kernel_spmd(nc, in_maps, core_ids=list(range(NCORES)))
    R = res.results
    def flo(c, a, ax):
        return np.flip(a, axis=ax) if c % 2 else a
    y_prompt = np.concatenate([flo(c, R[c]["y_p"].reshape(2, LP, D), 1) for c in range(NCORES)], axis=0)
    y_sample = np.stack([np.concatenate([R[2 * b]["y_s"], np.flip(R[2 * b + 1]["y_s"], axis=0)], axis=0) for b in range(4)], axis=0)
    nstate = np.concatenate([flo(c, R[c]["nstate"].reshape(2, 2, 2, 32, 64, 128), 2) for c in range(NCORES)], axis=0)
    nk = np.concatenate([flo(c, R[c]["nk"].reshape(2, 2, LP, 8, 2, 64), 2) for c in range(NCORES)], axis=0)
    nv = np.concatenate([flo(c, R[c]["nv"].reshape(2, 2, LP, 8, 128), 2) for c in range(NCORES)], axis=0)
    return (y_prompt.astype(np.float32), y_sample.astype(np.float32), nstate.astype(np.float32),
            nk.astype(np.float32), nv.astype(np.float32))
```
